# Optimizing a Trainium2 kernel written in Bass

```python
import jax, jax.numpy as jnp
from jax import lax
import numpy as np

D_MODEL = 1024
BATCH = 2
SEQ = 8192
DEPTH = 2

D_MIX = D_MODEL
NORM_EPS = 1e-6

GLA_HEADS = 4
GLA_DK = 64
GLA_DV = 128
GLA_KW = GLA_HEADS * GLA_DK
GLA_VW = GLA_HEADS * GLA_DV
GLA_GATE_RANK = 16
GLA_TAU = 16.0
GLA_CHUNK = 64

RWKV_HEAD = 64
RWKV_W = D_MIX - GLA_VW
RWKV_HEADS = RWKV_W // RWKV_HEAD
RWKV_DECAY_RANK = 64
RWKV_A_RANK = 64
RWKV_GN_EPS = 64e-5

GLA_SPLITS = (GLA_KW, GLA_KW, GLA_VW, GLA_GATE_RANK, GLA_VW)
RWKV_SHIFT_SPLITS = (RWKV_W, RWKV_W, RWKV_W, RWKV_DECAY_RANK, RWKV_A_RANK)
GLA_IN = sum(GLA_SPLITS)
RWKV_SHIFT_W = sum(RWKV_SHIFT_SPLITS)
RWKV_IN = RWKV_SHIFT_W + RWKV_W
IN_WIDTH = GLA_IN + RWKV_IN

kernel_name = "hymba_gla_rwkv7_hybrid"


def _split(t, sizes):
    idx = np.cumsum(np.array(sizes))[:-1].tolist()
    return jnp.split(t, idx, axis=-1)


def _rmsnorm(x, w):
    xf = x.astype(jnp.float32)
    y = xf * lax.rsqrt(jnp.mean(xf * xf, axis=-1, keepdims=True) + NORM_EPS)
    return (y * w.astype(jnp.float32)).astype(x.dtype)


def _gla_chunked(q, k, v, log_a):
    B, T, H, DK = q.shape
    DV = v.shape[-1]
    C = GLA_CHUNK
    N = T // C

    def to_chunks(t):
        return t.reshape(B, N, C, H, t.shape[-1]).transpose(1, 0, 3, 2, 4)

    qc, kc, vc, gc = to_chunks(q), to_chunks(k), to_chunks(v), to_chunks(log_a)
    b = jnp.cumsum(gc, axis=3)
    b_last = b[:, :, :, -1:, :]
    q_in = qc * jnp.exp(b)
    k_in = kc * jnp.exp(-b)
    k_st = kc * jnp.exp(b_last - b)
    causal = jnp.tril(jnp.ones((C, C), dtype=bool))
    scores = jnp.einsum('nbhid,nbhjd->nbhij', q_in, k_in)
    scores = jnp.where(causal, scores, 0.0)
    o_intra = jnp.einsum('nbhij,nbhjv->nbhiv', scores, vc)
    d_state = jnp.einsum('nbhjd,nbhjv->nbhdv', k_st, vc)
    chunk_decay = jnp.exp(b_last[:, :, :, 0, :])

    def step(S, inp):
        dS, dec = inp
        return S * dec[..., None] + dS, S

    S0 = jnp.zeros((B, H, DK, DV), jnp.float32)
    _, S_prev = lax.scan(step, S0, (d_state, chunk_decay))
    o_inter = jnp.einsum('nbhid,nbhdv->nbhiv', q_in, S_prev)
    o = o_intra + o_inter
    return o.transpose(1, 0, 3, 2, 4).reshape(B, T, H, DV)


def _rwkv7_scan(r, decay, k, v, kk, a):
    B, T, H, N = r.shape

    def step(S, inp):
        r_t, w_t, k_t, v_t, kk_t, a_t = inp
        sa = jnp.einsum('bhvk,bhk->bhv', S, -kk_t)
        S = (S * w_t[:, :, None, :]
             + sa[..., None] * (kk_t * a_t)[:, :, None, :]
             + v_t[..., None] * k_t[:, :, None, :])
        y = jnp.einsum('bhvk,bhk->bhv', S, r_t)
        return S, y

    xs = tuple(t.transpose(1, 0, 2, 3) for t in (r, decay, k, v, kk, a))
    S0 = jnp.zeros((B, H, N, N), jnp.float32)
    _, y = lax.scan(step, S0, xs)
    return y.transpose(1, 0, 2, 3)


def _hybrid_layer(x, norm_w, w_in, gla_gate_up, gla_gate_bias, gla_norm_w,
                  rwkv_mu, rwkv_w0, rwkv_w_up, rwkv_a0, rwkv_a_up, rwkv_k_k,
                  rwkv_k_a, rwkv_r_k, rwkv_ln_w, rwkv_ln_b, w_out):
    B, T, _ = x.shape
    f32 = jnp.float32
    h = _rmsnorm(x, norm_w)
    u = (h @ w_in).astype(f32)
    gla_u, rwkv_u = u[..., :GLA_IN], u[..., GLA_IN:]

    gq, gk, gv, glr, gg = _split(gla_u, GLA_SPLITS)
    log_a = jax.nn.log_sigmoid(glr @ gla_gate_up.astype(f32) + gla_gate_bias.astype(f32)) / GLA_TAU
    q = gq.reshape(B, T, GLA_HEADS, GLA_DK) * (GLA_DK ** -0.5)
    k = gk.reshape(B, T, GLA_HEADS, GLA_DK)
    v = gv.reshape(B, T, GLA_HEADS, GLA_DV)
    o = _gla_chunked(q, k, v, log_a.reshape(B, T, GLA_HEADS, GLA_DK))
    o = o * lax.rsqrt(jnp.mean(o * o, axis=-1, keepdims=True) + NORM_EPS) * gla_norm_w.astype(f32)
    gla_out = o.reshape(B, T, GLA_VW) * jax.nn.silu(gg)

    shifted_cols = rwkv_u[..., :RWKV_SHIFT_W]
    rg = rwkv_u[..., RWKV_SHIFT_W:]
    prev = jnp.pad(shifted_cols, ((0, 0), (1, 0), (0, 0)))[:, :-1]
    mixed = shifted_cols + (prev - shifted_cols) * rwkv_mu.astype(f32)
    r, rk, rv, wl, al = _split(mixed, RWKV_SHIFT_SPLITS)
    w = -jax.nn.softplus(-(rwkv_w0.astype(f32) + jnp.tanh(wl) @ rwkv_w_up.astype(f32))) - 0.5
    decay = jnp.exp(-jnp.exp(w))
    a = jax.nn.sigmoid(rwkv_a0.astype(f32) + al @ rwkv_a_up.astype(f32))
    kk = (rk * rwkv_k_k.astype(f32)).reshape(B, T, RWKV_HEADS, RWKV_HEAD)
    kk = kk / jnp.maximum(jnp.linalg.norm(kk, axis=-1, keepdims=True), 1e-12)
    rk = rk * (1.0 + (a - 1.0) * rwkv_k_a.astype(f32))
    hs = (B, T, RWKV_HEADS, RWKV_HEAD)
    r_h, k_h, v_h = r.reshape(hs), rk.reshape(hs), rv.reshape(hs)
    y = _rwkv7_scan(r_h, decay.reshape(hs), k_h, v_h, kk, a.reshape(hs))
    mu = jnp.mean(y, axis=-1, keepdims=True)
    var = jnp.mean(jnp.square(y - mu), axis=-1, keepdims=True)
    y = ((y - mu) * lax.rsqrt(var + RWKV_GN_EPS)).reshape(B, T, RWKV_W)
    y = y * rwkv_ln_w.astype(f32) + rwkv_ln_b.astype(f32)
    bonus = jnp.sum(r_h * k_h * rwkv_r_k.astype(f32), axis=-1, keepdims=True) * v_h
    rwkv_out = (y + bonus.reshape(B, T, RWKV_W)) * jax.nn.silu(rg)

    merged = jnp.concatenate([gla_out, rwkv_out], axis=-1).astype(x.dtype)
    return x + merged @ w_out


def setup_inputs(seed: int = 0) -> dict:
    key = jax.random.key(seed)
    ks = jax.random.split(key, 20)
    n = jax.random.normal
    L = DEPTH
    return {
        "x": n(ks[0], (BATCH, SEQ, D_MODEL), jnp.float32),
        "norm_w": 1.0 + 0.05 * n(ks[1], (L, D_MODEL), jnp.float32),
        "w_in": n(ks[2], (L, D_MODEL, IN_WIDTH), jnp.float32) * D_MODEL ** -0.5,
        "gla_gate_up": n(ks[3], (L, GLA_GATE_RANK, GLA_KW), jnp.float32) * GLA_GATE_RANK ** -0.5,
        "gla_gate_bias": 0.1 * n(ks[4], (L, GLA_KW), jnp.float32),
        "gla_norm_w": 1.0 + 0.05 * n(ks[5], (L, GLA_DV), jnp.float32),
        "rwkv_mu": jax.random.uniform(ks[6], (L, RWKV_SHIFT_W), jnp.float32),
        "rwkv_w0": jax.random.uniform(ks[7], (L, RWKV_W), jnp.float32, -6.0, -1.0),
        "rwkv_w_up": 0.1 * n(ks[8], (L, RWKV_DECAY_RANK, RWKV_W), jnp.float32),
        "rwkv_a0": 0.1 * n(ks[9], (L, RWKV_W), jnp.float32),
        "rwkv_a_up": 0.1 * n(ks[10], (L, RWKV_A_RANK, RWKV_W), jnp.float32),
        "rwkv_k_k": 0.85 + 0.05 * n(ks[11], (L, RWKV_W), jnp.float32),
        "rwkv_k_a": 1.0 + 0.05 * n(ks[12], (L, RWKV_W), jnp.float32),
        "rwkv_r_k": 0.1 * n(ks[13], (L, RWKV_HEADS, RWKV_HEAD), jnp.float32),
        "rwkv_ln_w": 1.0 + 0.05 * n(ks[14], (L, RWKV_W), jnp.float32),
        "rwkv_ln_b": 0.02 * n(ks[15], (L, RWKV_W), jnp.float32),
        "w_out": n(ks[16], (L, D_MIX, D_MODEL), jnp.float32) * D_MIX ** -0.5,
        "final_norm_w": 1.0 + 0.05 * n(ks[17], (D_MODEL,), jnp.float32),
    }


def reference(x, norm_w, w_in, gla_gate_up, gla_gate_bias, gla_norm_w, rwkv_mu,
              rwkv_w0, rwkv_w_up, rwkv_a0, rwkv_a_up, rwkv_k_k, rwkv_k_a,
              rwkv_r_k, rwkv_ln_w, rwkv_ln_b, w_out, final_norm_w):
    for l in range(DEPTH):
        x = _hybrid_layer(x, norm_w[l], w_in[l], gla_gate_up[l], gla_gate_bias[l],
                          gla_norm_w[l], rwkv_mu[l], rwkv_w0[l], rwkv_w_up[l],
                          rwkv_a0[l], rwkv_a_up[l], rwkv_k_k[l], rwkv_k_a[l],
                          rwkv_r_k[l], rwkv_ln_w[l], rwkv_ln_b[l], w_out[l])
    return _rmsnorm(x, final_norm_w)
```

```python
import contextlib
import numpy as np
import concourse.bass as bass
import concourse.mybir as mybir
from concourse.bass_utils import run_bass_kernel_spmd

F32 = mybir.dt.float32
BF16 = mybir.dt.bfloat16
AF = mybir.ActivationFunctionType
ALU = mybir.AluOpType
AX = mybir.AxisListType

D = 1024
NCORES = 8
TOK = 2048
GLA_IN = 1552
SHIFT_W = 1664
IN_W = 3728
KDEC = float(np.exp(-0.5))


class Sched:
    ENG = ("sync", "scalar", "vector", "gpsimd", "tensor")
    NDMA = 24

    def __init__(self, nc):
        self.nc = nc
        self.dma_streams = {"sync": ["dq_sync%d" % i for i in range(self.NDMA)], "gpsimd": ["dq_pool%d" % i for i in range(8)]}
        self.streams = list(self.ENG) + self.dma_streams["sync"] + self.dma_streams["gpsimd"]
        self.rrs = {"sync": 0, "gpsimd": 0}
        self.ops = {e: [] for e in self.ENG}
        self.tick = {s: 0 for s in self.streams}
        self.seen = {e: {s: 0 for s in self.streams} for e in self.ENG}
        self.lastw = {}
        self.readers = {}
        self.n = 0
        self.rr = 0
        self.pe_cur = None
        self._drain = -1

    def _need(self, eng, deps, pe_fifo=False):
        req = {}
        for (s, t) in deps:
            if pe_fifo and s == "tensor" and eng == "tensor" and t != self._drain:
                continue
            if t > self.seen[eng][s]:
                req[s] = max(req.get(s, 0), t)
        for s, t in req.items():
            self.seen[eng][s] = t
            self.ops[eng].append(("wait", s, t))

    def op(self, eng, fn, reads=(), writes=(), dma=False, pe_fifo=False, pe_class=None):
        deps = []
        if eng == "tensor":
            if pe_class is not None and pe_class != "O" and pe_class == self.pe_cur:
                pe_fifo = True
            else:
                pe_fifo = True
                self._drain = self.tick["tensor"]
                if self.tick["tensor"] > 0:
                    deps.append(("tensor", self.tick["tensor"]))
            self.pe_cur = pe_class
        for k in reads:
            if k in self.lastw:
                deps.append(self.lastw[k])
        for k in writes:
            if k in self.lastw:
                deps.append(self.lastw[k])
            deps.extend(self.readers.get(k, ()))
        if dma:
            pool_ = self.dma_streams[eng]
            stream = pool_[self.rrs[eng]]
            self.rrs[eng] = (self.rrs[eng] + 1) % len(pool_)
            if self.tick[stream] > 0:
                deps.append((stream, self.tick[stream]))
            inc = 16
        else:
            stream, inc = eng, 1
        self._need(eng, deps, pe_fifo)
        self.tick[stream] += inc
        t = self.tick[stream]
        self.ops[eng].append(("op", fn, stream, inc))
        self.n += 1
        for k in reads:
            self.readers.setdefault(k, []).append((stream, t))
        for k in writes:
            self.lastw[k] = (stream, t)
            self.readers[k] = []

    def finish(self):
        for eng in self.ENG:
            self._need(eng, [(s, self.tick[s]) for s in self.streams if self.tick[s] > 0])

    def emit(self):
        nc = self.nc
        waited = {e: set() for e in self.ENG}
        for e in self.ENG:
            for it in self.ops[e]:
                if it[0] == "wait" and it[1] in waited:
                    waited[it[1]].add(it[2])
        rank = {e: {t: i + 1 for i, t in enumerate(sorted(waited[e]))} for e in self.ENG}
        with contextlib.ExitStack() as st:
            sems = {s: st.enter_context(nc.semaphore("s_" + s)) for s in self.streams}
            block = st.enter_context(nc.Block())

            def runner(e):
                def f(engine):
                    cnt = 0
                    for it in self.ops[e]:
                        if it[0] == "wait":
                            v = rank[it[1]][it[2]] if it[1] in rank else it[2]
                            engine.wait_ge(sems[it[1]], v)
                        else:
                            ins = it[1](engine)
                            if it[2] in rank:
                                cnt += 1
                                if cnt in rank[it[2]]:
                                    ins.then_inc(sems[it[2]], 1)
                            else:
                                ins.then_inc(sems[it[2]], it[3])
                return f
            block.sync(runner("sync"))
            block.scalar(runner("scalar"))
            block.vector(runner("vector"))
            block.gpsimd(runner("gpsimd"))
            block.tensor(runner("tensor"))


class _Stop(Exception):
    pass


def build_pass(NT=16, dbg_names=(), stop_after=None, fused=False):
    nc = bass.Bass("TRN2", target_bir_lowering=False)
    T = NT * 128

    def din(name, shape, dt=F32):
        return nc.dram_tensor(name, list(shape), dt, kind="ExternalInput").ap()

    def dout(name, shape, dt=F32):
        return nc.dram_tensor(name, list(shape), dt, kind="ExternalOutput").ap()

    LD = [2] if fused else []
    x_sh = din("x_sh", [T + 1, D])
    w_in = din("w_in", LD + [D, IN_W])
    w_out = din("w_out", LD + [D, D])
    pcol = din("pcol", LD + [128, 32])
    rksel_d = din("rksel", LD + [128, 8])
    mu_d = din("mu", LD + [SHIFT_W])
    lnw_d = din("lnw", LD + [512])
    lnb_d = din("lnb", LD + [512])
    gnw_d = din("gnw", LD + [128])
    fnw_d = din("fnw", [D])
    lrw_d = din("lrw", LD + [128, 512])
    gup_d = din("gup", LD + [16, 256])
    cst_d = din("cst", [128, 1024])
    y_out = dout("y_out", [T, D])
    if fused:
        x1s = nc.dram_tensor("x1s", [T, D], F32).ap()
    else:
        sumM_d = din("sumM", [8, 128, 256])
        sumPhi_d = din("sumPhi", [8, 128, 256])
        sumS_d = din("sumS", [8, 128, 256])
        sumD_d = din("sumD", [8, 128, 2])
        use_d = din("use", [128, 8])
        x_out = dout("x_out", [T, D])
        oA = dout("oA", [128, 512])
        oS = dout("oS", [128, 256])
        oD = dout("oD", [128, 2])
    dbg_out = {}

    with contextlib.ExitStack() as st:
        def sb(name, shape, dt=F32):
            return st.enter_context(nc.sbuf_tensor(name, list(shape), dt))

        S = Sched(nc)
        w1b = sb("w1b", [128, 8, IN_W], BF16)
        w2b = sb("w2b", [128, 8, SHIFT_W], BF16)
        wob = sb("wob", [128, 8, D], BF16)
        cst = sb("cst_sb", [128, 1024])
        identf = cst[:, 0:128]
        identblk = cst[:, 128:192]
        maskNA = cst[:, 192:320]
        maskSL = cst[:, 320:384]
        maskUU = cst[:, 256:320]
        blockones = cst[:, 384:512]
        scanmask = cst[:, 512:1024]
        identb = sb("identb", [128, 128], BF16)
        pc = sb("pcol_sb", [128, 32])
        negb = sb("negb", [128, 2])
        rksel = sb("rksel_sb", [128, 8])
        lnw_bc = sb("lnw_bc", [128, 512])
        lnb_bc = sb("lnb_bc", [128, 512])
        gnw_bc = sb("gnw_bc", [128, 128])
        fnw_bc = sb("fnw_bc", [128, D])
        lrw = sb("lrw_sb", [128, 512])
        gup = sb("gup_sb", [16, 256])
        use = sb("use_sb", [128, 8])
        NF = 13
        Fpool = sb("Fpool", [128, NF * 512])

        def F(i):
            return Fpool[:, i * 512:(i + 1) * 512]

        def fk(lo, hi):
            return ["F%d" % i for i in range(lo // 512, (hi - 1) // 512 + 1)]

        xt = sb("xt", [128, D])
        xt2 = sb("xt2", [128, D])
        xt3 = sb("xt3", [128, D])
        xts = [xt, xt2, xt3]
        hb = sb("hb", [128, D], BF16)
        hT = sb("hT", [128, 8, 129], BF16)
        st1 = sb("st1", [128, 16])
        arb = sb("arb", [128, 4, 2, 2, 64], BF16)
        btb = sb("btb", [128, 4, 128], BF16)
        ktb = sb("ktb", [128, 4, 128], BF16)
        Bhb = sb("Bhb", [128, 4, 128], BF16)
        Khb = sb("Khb", [128, 4, 128], BF16)
        atb = sb("atb", [128, 4, 128], BF16)
        atok = sb("atok", [128, 4, 128], BF16)
        Bhtok = sb("Bhtok", [128, 4, 128], BF16)
        Khtok = sb("Khtok", [128, 4, 128], BF16)
        Vtok = sb("Vtok", [128, 4, 128], BF16)
        rvb = sb("rvb", [128, 4, 128], BF16)
        NAb = sb("NAb", [128, 8, 128], BF16)
        KAb = sb("KAb", [128, 8, 128], BF16)
        Lb = [sb("Lb%d" % i, [128, 8, 64], BF16) for i in range(2)]
        Nb = [sb("Nb%d" % i, [128, 8, 64], BF16) for i in range(2)]
        Xb = sb("Xb", [128, 8, 128], BF16)
        Pb = sb("Pb", [128, 4, 2, 64], BF16)
        GTb = sb("GTb", [128, 4, 128], BF16)
        A32 = sb("A32", [128, 4, 128])
        Ab = sb("Ab", [128, 4, 128], BF16)
        gamc = sb("gamc", [128, 4, 2])
        dG = sb("dG", [128, 4, 2, 64])
        qT = sb("qT", [128, 2, 128])
        kT = sb("kT", [128, 2, 128])
        glrT = sb("glrT", [16, 128])
        qib = sb("qib", [128, 2, 128], BF16)
        kib = sb("kib", [128, 2, 128], BF16)
        ksb = sb("ksb", [128, 2, 128], BF16)
        kstok = sb("kstok", [128, 2, 128], BF16)
        gvb = sb("gvb", [128, 512], BF16)
        scb = sb("scb", [128, 4, 64], BF16)
        S32 = sb("S32", [128, 2, 128])
        Sb = sb("Sb", [128, 2, 128], BF16)
        Dc = sb("Dc", [128, 2])
        gsil = sb("gsil", [128, 512], BF16)
        rsil = sb("rsil", [128, 512], BF16)
        merged = sb("merged", [128, D], BF16)
        mT = sb("mT", [128, 8, 128], BF16)
        on = sb("on", [128, D])
        gn = sb("gn", [128, 40])
        omka = sb("omka", [128, 4])
        wlalb = sb("wlalb", [128, 128], BF16)
        lrwb = sb("lrwb", [128, 512], BF16)
        bonesb = sb("bonesb", [128, 128], BF16)
        rkselb = sb("rkselb", [128, 8], BF16)
        PS = [st.enter_context(nc.psum_tensor("ps%d" % i, [128, 512], F32)) for i in range(8)]

        def pk(i):
            return ["ps%dc0" % i, "ps%dc1" % i] if i in (0, 3) else ["ps%d" % i]

        def PSb(i):
            return PS[i][:].bitcast(BF16)

        def ck(stage):
            if stop_after is not None and stage == stop_after:
                raise _Stop()

        def dma(out, in_, reads=(), writes=(), eng="sync"):
            S.op(eng, lambda e: e.dma_start(out=out, in_=in_), reads, writes, dma=True)

        def vec(fn, reads, writes):
            S.op("vector", fn, reads, writes)

        def act(fn, reads, writes):
            S.op("scalar", fn, reads, writes)

        def pool(fn, reads, writes):
            S.op("gpsimd", fn, reads, writes)

        def pe(fn, reads, writes, cls):
            S.op("tensor", fn, reads, writes, pe_class=cls)

        def mm(out, lhsT, rhs, reads, writes, start=True, stop=True, tp=None, fifo=False):
            K_, M_ = lhsT.shape[0], int(np.prod(lhsT.shape[1:]))
            if tp is None:
                cls = "F" if K_ == 128 else "O"
            elif M_ == 64 and K_ == 64:
                cls = "D" if tp[0] == tp[1] else "X"
            else:
                cls = "O"
            pe(lambda e: e.matmul(out, lhsT=lhsT, rhs=rhs, start=start, stop=stop, tile_position=tp), reads, writes, cls)

        def tr(out, in_, ident, reads, writes, fifo=False):
            pe(lambda e: e.transpose(out=out, in_=in_, identity=ident), reads, writes, "F")

        def dbg(name, ap, keys, shape, dt=F32):
            if name in dbg_names:
                d = dout("dbg_" + name, shape, dt)
                dma(d, ap, reads=keys)

        def rstd_from(dst, src, scale, eps, keys_src, keys_dst):
            act(lambda e: e.activation(out=dst, in_=src, func=AF.Ln, bias=eps, scale=scale), keys_src, keys_dst)
            act(lambda e: e.activation(out=dst, in_=dst, func=AF.Exp, scale=-0.5), keys_dst, keys_dst)

        try:
            layers = [0, 1] if fused else [0]
            for l in layers:
                Lx = (lambda ap, l=l: ap[l]) if fused else (lambda ap: ap)
                if l == 0:
                    dma(cst[:], cst_d, writes=["cst"])
                    dma(fnw_bc[:], fnw_d.partition_broadcast(128), writes=["fnw"], eng="gpsimd")
                    if not fused:
                        dma(use[:], use_d, writes=["use"])
                dma(pc[:], Lx(pcol), writes=["pc"])
                dma(rksel[:], Lx(rksel_d), writes=["rksel"])
                vec(lambda e: e.tensor_copy(out=rkselb[:], in_=rksel[:]), ["rksel"], ["rkselb"])
                dma(lrw[:], Lx(lrw_d), writes=["lrw"])
                vec(lambda e: e.tensor_copy(out=lrwb[:], in_=lrw[:]), ["lrw"], ["lrwb"])
                dma(gup[:], Lx(gup_d), writes=["gup"])
                dma(lnw_bc[:], Lx(lnw_d).partition_broadcast(128), writes=["lnw"], eng="gpsimd")
                dma(lnb_bc[:], Lx(lnb_d).partition_broadcast(128), writes=["lnb"], eng="gpsimd")
                dma(gnw_bc[:], Lx(gnw_d).partition_broadcast(128), writes=["gnw"], eng="gpsimd")
                if l == 0:
                    vec(lambda e: e.tensor_copy(out=identb[:], in_=identf), ["cst"], ["identb"])
                    vec(lambda e: e.tensor_copy(out=bonesb[:], in_=blockones), ["cst"], ["bonesb"])
                vec(lambda e: e.tensor_scalar(out=negb[:], in0=pc[:, 16:18], scalar1=-1.0, scalar2=None, op0=ALU.mult), ["pc"], ["negb"])
                vec(lambda e: e.tensor_scalar(out=omka[:], in0=pc[:, 12:16], scalar1=-1.0, scalar2=1.0, op0=ALU.mult, op1=ALU.add), ["pc"], ["omka"])
                W0 = lambda g: pc[:, g:g + 1]
                A0 = lambda g: pc[:, 4 + g:5 + g]
                KK = lambda g: pc[:, 8 + g:9 + g]
                KA = lambda g: pc[:, 12 + g:13 + g]
                NW = lambda kc: pc[:, 20 + kc:21 + kc]

                ck("setup")
                mu_bc = Fpool[:, 0:1664]
                omu_bc = Fpool[:, 1664:3328]
                stg = [Fpool[:, 3328:4992], Fpool[:, 4992:6656]]
                kmu, komu = fk(0, 1664), fk(1664, 3328)
                kst = [fk(3328, 4992), fk(4992, 6656)]
                dma(mu_bc, Lx(mu_d).partition_broadcast(128), writes=kmu, eng="gpsimd")
                vec(lambda e: e.tensor_scalar(out=omu_bc, in0=mu_bc, scalar1=-1.0, scalar2=1.0, op0=ALU.mult, op1=ALU.add), kmu, komu)
                si = 0
                for kc in range(8):
                    rows = slice(kc * 128, (kc + 1) * 128)
                    s_, ks_ = stg[si % 2], kst[si % 2]; si += 1
                    dma(s_[:, 0:GLA_IN], Lx(w_in)[rows, 0:GLA_IN], writes=ks_)
                    act(lambda e, s_=s_, kc=kc: e.activation(out=w1b[:, kc, 0:GLA_IN], in_=s_[:, 0:GLA_IN], func=AF.Copy, scale=NW(kc)),
                        ks_ + ["pc"], ["w1b"])
                    s_, ks_ = stg[si % 2], kst[si % 2]; si += 1
                    dma(s_[:, 0:SHIFT_W], Lx(w_in)[rows, GLA_IN:GLA_IN + SHIFT_W], writes=ks_)
                    pool(lambda e, s_=s_, kc=kc: e.tensor_scalar(out=s_[:, 0:SHIFT_W], in0=s_[:, 0:SHIFT_W], scalar1=NW(kc), scalar2=None, op0=ALU.mult),
                         ks_ + ["pc"], ks_)
                    vec(lambda e, s_=s_, kc=kc: e.tensor_tensor(out=w2b[:, kc, :], in0=s_[:, 0:SHIFT_W], in1=mu_bc, op=ALU.mult), ks_ + kmu, ["w2b"])
                    pool(lambda e, s_=s_, kc=kc: e.tensor_tensor(out=w1b[:, kc, GLA_IN:GLA_IN + SHIFT_W], in0=s_[:, 0:SHIFT_W], in1=omu_bc, op=ALU.mult),
                         ks_ + komu, ["w1b"])
                    s_, ks_ = stg[si % 2], kst[si % 2]; si += 1
                    dma(s_[:, 0:512], Lx(w_in)[rows, GLA_IN + SHIFT_W:IN_W], writes=ks_)
                    act(lambda e, s_=s_, kc=kc: e.activation(out=w1b[:, kc, GLA_IN + SHIFT_W:IN_W], in_=s_[:, 0:512], func=AF.Copy, scale=NW(kc)),
                        ks_ + ["pc"], ["w1b"])
                    s_, ks_ = stg[si % 2], kst[si % 2]; si += 1
                    dma(s_[:, 0:D], Lx(w_out)[rows, :], writes=ks_)
                    vec(lambda e, s_=s_, kc=kc: e.tensor_copy(out=wob[:, kc, :], in_=s_[:, 0:D]), ks_, ["wob"])

                ck("wprep")
                Mst = F(0)[:, 0:256]
                Sst = F(0)[:, 256:512]
                kF0 = ["F0"]
                vec(lambda e: e.memset(Mst, 0.0), [], kF0)
                vec(lambda e: e.memset(Sst, 0.0), [], kF0)
                for j in (range(8) if not fused else []):
                    bM, bPhi, bS = F(1)[:, 0:256], F(1)[:, 256:512], F(2)[:, 0:256]
                    bD, dM, dS_ = F(2)[:, 256:258], F(3)[:, 0:256], F(3)[:, 256:512]
                    dma(bM, sumM_d[j], writes=["F1"])
                    dma(bPhi, sumPhi_d[j], writes=["F1"])
                    dma(bS, sumS_d[j], writes=["F2"])
                    dma(bD, sumD_d[j], writes=["F2"])
                    for g in range(4):
                        for hp in range(2):
                            p0 = hp * 64
                            mm(PS[0][p0:p0 + 64, g * 64:(g + 1) * 64], bPhi[p0:p0 + 64, g * 64:(g + 1) * 64],
                               Mst[p0:p0 + 64, g * 64:(g + 1) * 64], ["F1"] + kF0, pk(0), tp=(p0, p0))
                    vec(lambda e: e.tensor_tensor(out=dM, in0=PS[0][:, 0:256], in1=bM, op=ALU.add), pk(0) + ["F1"], ["F3"])
                    vec(lambda e: e.tensor_tensor(out=dM, in0=dM, in1=Mst, op=ALU.subtract), ["F3"] + kF0, ["F3"])
                    vec(lambda e, j=j: e.scalar_tensor_tensor(out=Mst, in0=dM, scalar=use[:, j:j + 1], in1=Mst, op0=ALU.mult, op1=ALU.add),
                        ["F3", "use"] + kF0, kF0)
                    for grp in range(2):
                        vec(lambda e, grp=grp: e.scalar_tensor_tensor(out=dS_[:, grp * 128:(grp + 1) * 128], in0=Sst[:, grp * 128:(grp + 1) * 128],
                                                                      scalar=bD[:, grp:grp + 1], in1=bS[:, grp * 128:(grp + 1) * 128],
                                                                      op0=ALU.mult, op1=ALU.add), ["F2"] + kF0, ["F3"])
                    vec(lambda e: e.tensor_tensor(out=dS_, in0=dS_, in1=Sst, op=ALU.subtract), ["F3"] + kF0, ["F3"])
                    vec(lambda e, j=j: e.scalar_tensor_tensor(out=Sst, in0=dS_, scalar=use[:, j:j + 1], in1=Sst, op0=ALU.mult, op1=ALU.add),
                        ["F3", "use"] + kF0, kF0)
                for g in range(4):
                    vec(lambda e, g=g: e.tensor_copy(out=A32[:, g, 0:64], in_=Mst[:, g * 64:(g + 1) * 64]), kF0, ["A32"])
                    vec(lambda e, g=g: e.tensor_copy(out=A32[:, g, 64:128], in_=identblk), ["cst"], ["A32"])
                vec(lambda e: e.tensor_copy(out=Ab[:], in_=A32[:]), ["A32"], ["Ab"])
                vec(lambda e: e.tensor_copy(out=S32[:].rearrange("p a b -> p (a b)"), in_=Sst), kF0, ["S32"])
                vec(lambda e: e.tensor_copy(out=Sb[:], in_=S32[:]), ["S32"], ["Sb"])
                vec(lambda e: e.memset(Dc[:], 1.0), [], ["Dc"])

                ck("fold")
                def load_norm(buf, key, src_rows, nrows, rkeys=()):
                    dma(buf[0:nrows, :], src_rows, reads=list(rkeys), writes=[key])
                    act(lambda e: e.activation(out=hb[:], in_=buf[:], func=AF.Square, scale=1.0 / 32.0, accum_out=st1[:, 0:1]),
                        [key], ["hb", "st1"])
                    rstd_from(st1[:, 1:2], st1[:, 0:1], 1.0, 1e-6, ["st1"], ["st1b"])
                    vec(lambda e: e.tensor_scalar(out=hb[:], in0=buf[:], scalar1=st1[:, 1:2], scalar2=None, op0=ALU.mult), [key, "st1b"], ["hb"])

                def transposes():
                    for k in range(8):
                        tr(PSb(0)[:, k * 128:(k + 1) * 128], hb[:, k * 128:(k + 1) * 128], identb[:], ["hb", "identb"], pk(0))

                def transposes_evac():
                    transposes()
                    pool(lambda e: e.tensor_copy(out=hT[:, :, 0:1], in_=hT[:, :, 128:129]), ["hT"], ["hT"])
                    vec(lambda e: e.tensor_copy(out=hT[:, :, 1:129], in_=PSb(0).rearrange("p (k t) -> p k t", t=128)), pk(0), ["hT"])

                def tile_src(tt):
                    if l == 0:
                        return x_sh[1 + tt * 128:1 + (tt + 1) * 128, :], ()
                    return x1s[tt * 128:(tt + 1) * 128, :], ["x1s%d" % tt]

                if l == 0:
                    vec(lambda e: e.memset(xts[0][:], 0.0), [], ["xt0"])
                    load_norm(xts[0], "xt0", x_sh[0:1, :], 1)
                    transposes()
                    vec(lambda e: e.tensor_copy(out=hT[:, :, 128:129], in_=PSb(0).rearrange("p (k t) -> p k t", t=128)[:, :, 0:1]), pk(0), ["hT"])
                else:
                    vec(lambda e: e.memset(hT[:, :, 128:129], 0.0), [], ["hT"])

                ck("halo")
                GHC = [(g, hp, cc) for same in (True, False) for g in range(4) for hp in range(2) for cc in range(2) if (hp == cc) == same]
                HC = [(h, cc) for same in (True, False) for h in range(4) for cc in range(2) if ((h % 2) == cc) == same]
                src0, rk0 = tile_src(0)
                load_norm(xts[0], "xt0", src0, 128, rk0)
                transposes_evac()
                def back_a(t, xt, kx):
                    ys, sq = F(6), F(7)
                    act(lambda e: e.copy(out=sq, in_=PS[2][:]), pk(2), ["F7"])
                    vec(lambda e: e.tensor_tensor(out=ys, in0=PS[3][:], in1=sq, op=ALU.add), pk(3) + ["F7"], ["F6"])
                    ys3 = ys.rearrange("p (h v) -> p h v", v=64)
                    sq3 = sq.rearrange("p (h v) -> p h v", v=64)
                    pool(lambda e: e.tensor_tensor(out=sq, in0=ys, in1=ys, op=ALU.mult), ["F6"], ["F7"])
                    s1, s2, mean, msq, var, rstdg = gn[:, 0:8], gn[:, 8:16], gn[:, 16:24], gn[:, 24:32], gn[:, 32:40], gn[:, 8:16]
                    vec(lambda e: e.reduce_sum(out=s1, in_=ys3, axis=AX.X), ["F6"], ["gn_s1"])
                    vec(lambda e: e.reduce_sum(out=s2, in_=sq3, axis=AX.X), ["F7"], ["gn_s2"])
                    vec(lambda e: e.tensor_scalar(out=mean, in0=s1, scalar1=1.0 / 64, scalar2=None, op0=ALU.mult), ["gn_s1"], ["gn_mean"])
                    vec(lambda e: e.tensor_tensor(out=msq, in0=mean, in1=mean, op=ALU.mult), ["gn_mean"], ["gn_msq"])
                    vec(lambda e: e.scalar_tensor_tensor(out=var, in0=s2, scalar=1.0 / 64, in1=msq, op0=ALU.mult, op1=ALU.subtract), ["gn_s2", "gn_msq"], ["gn_var"])
                    rstd_from(rstdg, var, 1.0, 64e-5, ["gn_var"], ["gn_s2"])
                    bc8 = lambda ap: ap.unsqueeze(2).to_broadcast([128, 8, 64])
                    vec(lambda e: e.tensor_tensor(out=ys3, in0=ys3, in1=bc8(mean), op=ALU.subtract), ["F6", "gn_mean"], ["F6"])
                    vec(lambda e: e.tensor_tensor(out=ys3, in0=ys3, in1=bc8(rstdg), op=ALU.mult), ["F6", "gn_s2"], ["F6"])
                    pool(lambda e: e.tensor_tensor(out=ys, in0=ys, in1=lnw_bc[:], op=ALU.mult), ["F6", "lnw"], ["F6"])
                    pool(lambda e: e.tensor_tensor(out=ys, in0=ys, in1=lnb_bc[:], op=ALU.add), ["F6", "lnb"], ["F6"])
                    vec(lambda e: e.tensor_tensor(out=sq3, in0=Vtok[:].rearrange("p g (h v) -> p (g h) v", v=64), in1=bc8(st1[:, 8:16]), op=ALU.mult),
                        ["Vtok", "bcoef"], ["F7"])
                    vec(lambda e: e.tensor_tensor(out=ys, in0=ys, in1=sq, op=ALU.add), ["F6", "F7"], ["F6"])
                    vec(lambda e: e.tensor_tensor(out=merged[:, 512:1024], in0=ys, in1=rsil[:], op=ALU.mult), ["F6", "rsil"], ["merged_r"])


                def back_b(t, xt, kx):
                    ck("gla")
                    for k in range(8):
                        tr(PSb(2)[:, k * 128:(k + 1) * 128], merged[:, k * 128:(k + 1) * 128], identb[:],
                           ["merged_g" if k < 4 else "merged_r", "identb"], pk(2))
                    vec(lambda e: e.tensor_copy(out=mT[:].rearrange("p a b -> p (a b)"), in_=PSb(2)), pk(2), ["mT"])
                    for half in range(2):
                        bank = 4 + half
                        for k in range(8):
                            mm(PS[bank][:], mT[:, k, :], wob[:, k, half * 512:(half + 1) * 512], ["mT", "wob"], pk(bank), start=(k == 0), stop=(k == 7), fifo=True)
                        vec(lambda e, half=half, bank=bank, xt=xt: e.tensor_tensor(out=xt[:, half * 512:(half + 1) * 512], in0=PS[bank][:],
                                                                            in1=xt[:, half * 512:(half + 1) * 512], op=ALU.add), pk(bank) + [kx], [kx])
                    if not fused:
                        dma(x_out[t * 128:(t + 1) * 128, :], xt[:], reads=[kx])
                    elif l == 0:
                        dma(x1s[t * 128:(t + 1) * 128, :], xt[:], reads=[kx], writes=["x1s%d" % t])
                    if (not fused) or l == layers[-1]:
                        act(lambda e, xt=xt: e.activation(out=on[:], in_=xt[:], func=AF.Square, scale=1.0 / 32.0, accum_out=st1[:, 6:7]), [kx], ["on", "fst"])
                        rstd_from(st1[:, 7:8], st1[:, 6:7], 1.0, 1e-6, ["fst"], ["fst2"])
                        vec(lambda e, xt=xt: e.scalar_tensor_tensor(out=on[:], in0=xt[:], scalar=st1[:, 7:8], in1=fnw_bc[:], op0=ALU.mult, op1=ALU.mult),
                            [kx, "fst2", "fnw"], ["on"])
                        dma(y_out[t * 128:(t + 1) * 128, :], on[:], reads=["on"])

                pending = None
                for t in range(NT):
                    last = (t == NT - 1)
                    xt = xts[t % 3]
                    kx = "xt%d" % (t % 3)
                    if pending is not None:
                        back_a(*pending)
                    if not last:
                        srcn, rkn = tile_src(t + 1)
                        load_norm(xts[(t + 1) % 3], "xt%d" % ((t + 1) % 3), srcn, 128, rkn)
                    cur = lambda k: hT[:, k, 1:129]
                    prv = lambda k: hT[:, k, 0:128]

                    def proj_fm_shift(dst_ps, col0, ncols, pskey):
                        for k in range(8):
                            mm(dst_ps, w1b[:, k, GLA_IN + col0:GLA_IN + col0 + ncols], cur(k), ["w1b", "hT"], pskey, start=(k == 0), stop=False, fifo=True)
                            mm(dst_ps, w2b[:, k, col0:col0 + ncols], prv(k), ["w2b", "hT"], pskey, start=False, stop=(k == 7), fifo=True)

                    def proj_fm(dst_ps, col0, ncols, pskey):
                        for k in range(8):
                            mm(dst_ps, w1b[:, k, col0:col0 + ncols], cur(k), ["w1b", "hT"], pskey, start=(k == 0), stop=(k == 7), fifo=True)

                    def proj_tm(dst_ps, col0, pskey):
                        for k in range(8):
                            mm(dst_ps, cur(k), w1b[:, k, col0:col0 + 512], ["w1b", "hT"], pskey, start=(k == 0), stop=(k == 7), fifo=True)

                    rT, rkT, rvT, sw, aT, cs_, Ea, Eb, kk_, kT2, bT, tmp = [F(i) for i in range(12)]
                    for g in range(4):
                        proj_fm_shift(PS[1][:, g * 128:(g + 1) * 128], 0 * 512 + g * 128, 128, pk(1))
                    act(lambda e: e.copy(out=rT, in_=PS[1][:]), pk(1), ["F0"])
                    for g in range(4):
                        proj_fm_shift(PS[2][:, g * 128:(g + 1) * 128], 1 * 512 + g * 128, 128, pk(2))
                    act(lambda e: e.copy(out=rkT, in_=PS[2][:]), pk(2), ["F1"])
                    for g in range(4):
                        gs = slice(g * 128, (g + 1) * 128)
                        vec(lambda e, g=g, gs=gs: e.tensor_scalar(out=kk_[:, gs], in0=rkT[:, gs], scalar1=KK(g), scalar2=None, op0=ALU.mult), ["F1", "pc"], ["F8"])
                    pool(lambda e: e.tensor_tensor(out=rvb[:].rearrange("p a b -> p (a b)"), in0=kk_, in1=kk_, op=ALU.mult), ["F8"], ["rvb"])
                    for g in range(4):
                        proj_fm_shift(PS[3][:, g * 128:(g + 1) * 128], 2 * 512 + g * 128, 128, pk(3))
                    act(lambda e: e.copy(out=rvT, in_=PS[3][:]), pk(3), ["F2"])
                    proj_fm_shift(PS[4][:, 0:128], 1536, 128, pk(4))
                    proj_fm(PS[4][0:16, 128:256], 1024, 16, pk(4))
                    proj_fm(PS[4][:, 256:384], 0, 128, pk(4))
                    proj_fm(PS[4][:, 384:512], 128, 128, pk(4))
                    act(lambda e: e.activation(out=wlalb[0:64, :], in_=PS[4][0:64, 0:128], func=AF.Tanh), pk(4), ["wlalb"])
                    act(lambda e: e.copy(out=wlalb[64:128, :], in_=PS[4][64:128, 0:128]), pk(4), ["wlalb"])
                    act(lambda e: e.copy(out=glrT[:], in_=PS[4][0:16, 128:256]), pk(4), ["glrT"])
                    act(lambda e: e.copy(out=qT[:].rearrange("p a b -> p (a b)"), in_=PS[4][:, 256:512]), pk(4), ["qT"])
                    proj_fm(PS[5][:, 0:128], 256, 128, pk(5))
                    proj_fm(PS[5][:, 128:256], 384, 128, pk(5))
                    act(lambda e: e.copy(out=kT[:].rearrange("p a b -> p (a b)"), in_=PS[5][:, 0:256]), pk(5), ["kT"])
                    for g in range(4):
                        mm(PS[1][:, g * 128:(g + 1) * 128], lrwb[0:64, g * 128:(g + 1) * 128], wlalb[0:64, :], ["lrwb", "wlalb"], pk(1), tp=(0, 0))
                    for g in range(4):
                        mm(PS[2][:, g * 128:(g + 1) * 128], lrwb[64:128, g * 128:(g + 1) * 128], wlalb[64:128, :], ["lrwb", "wlalb"], pk(2), tp=(64, 0))
                    for g in range(4):
                        gs = slice(g * 128, (g + 1) * 128)
                        act(lambda e, g=g, gs=gs: e.activation(out=sw[:, gs], in_=PS[1][:, gs], func=AF.Sigmoid, bias=W0(g)), pk(1) + ["pc"], ["F3"])
                    vec(lambda e: e.tensor_tensor_scan(out=cs_, data0=scanmask, data1=sw, initial=0.0, op0=ALU.mult, op1=ALU.add), ["cst", "F3"], ["F5"])
                    pool(lambda e: e.tensor_tensor(out=F(12), in0=cs_, in1=sw, op=ALU.subtract), ["F5", "F3"], ["F12"])
                    vec(lambda e: e.tensor_tensor(out=tmp.rearrange("p (a b) -> p a b", b=64),
                                                  in0=cs_.rearrange("p (a b) -> p a b", b=64)[:, :, 63:64].to_broadcast([128, 8, 64]),
                                                  in1=cs_.rearrange("p (a b) -> p a b", b=64), op=ALU.subtract), ["F5"], ["F11"])
                    for g in range(4):
                        gs = slice(g * 128, (g + 1) * 128)
                        act(lambda e, g=g, gs=gs: e.activation(out=aT[:, gs], in_=PS[2][:, gs], func=AF.Sigmoid, bias=A0(g)), pk(2) + ["pc"], ["F4"])
                    proj_tm(PS[6][:], 512, pk(6))
                    vec(lambda e: e.tensor_copy(out=gvb[:], in_=PS[6][:]), pk(6), ["gvb"])
                    for g in range(4):
                        gs = slice(g * 128, (g + 1) * 128)
                        mm(PS[3][:, gs], bonesb[:], rvb[:, g, :], ["bonesb", "rvb"], pk(3))
                    rstd_from(bT, PS[3][:], 1.0, 1e-20, pk(3), ["F10"])
                    vec(lambda e: e.tensor_tensor(out=kk_, in0=kk_, in1=bT, op=ALU.mult), ["F8", "F10"], ["F8"])
                    for g in range(4):
                        gs = slice(g * 128, (g + 1) * 128)
                        vec(lambda e, g=g, gs=gs: e.tensor_scalar(out=kT2[:, gs], in0=aT[:, gs], scalar1=KA(g), scalar2=omka[:, g:g + 1], op0=ALU.mult, op1=ALU.add),
                            ["F4", "pc", "omka"], ["F9"])
                    vec(lambda e: e.tensor_tensor(out=kT2, in0=kT2, in1=rkT, op=ALU.mult), ["F9", "F1"], ["F9"])
                    pool(lambda e: e.tensor_tensor(out=bT, in0=kk_, in1=aT, op=ALU.mult), ["F8", "F4"], ["F10"])
                    cs3 = cs_.rearrange("p (a b) -> p a b", b=64)
                    arb_a = arb[:, :, :, 0, :].rearrange("p g c i -> p (g c) i")
                    arb_r = arb[:, :, :, 1, :].rearrange("p g c i -> p (g c) i")
                    v3 = lambda ap: ap.rearrange("p (a b) -> p a b", b=64)
                    b3 = lambda ap: ap[:].rearrange("p g (c i) -> p (g c) i", i=64)
                    Dpv = F(12)
                    act(lambda e: e.activation(out=Dpv, in_=Dpv, func=AF.Exp, scale=-KDEC), ["F12"], ["F12"])
                    act(lambda e: e.activation(out=tmp, in_=tmp, func=AF.Exp, scale=-KDEC), ["F11"], ["F11"])
                    act(lambda e: e.activation(out=Ea, in_=cs_, func=AF.Exp, scale=-KDEC), ["F5"], ["F6"])
                    act(lambda e: e.activation(out=Eb, in_=cs_, func=AF.Exp, scale=KDEC), ["F5"], ["F7"])
                    vec(lambda e: e.scalar_tensor_tensor(out=atb[:].rearrange("p a b -> p (a b)"), in0=kk_, scalar=-1.0, in1=Dpv, op0=ALU.mult, op1=ALU.mult),
                        ["F8", "F12"], ["atb"])
                    pool(lambda e: e.tensor_tensor(out=Khb[:].rearrange("p a b -> p (a b)"), in0=kT2, in1=tmp, op=ALU.mult), ["F9", "F11"], ["Khb"])
                    vec(lambda e: e.tensor_tensor(out=Bhb[:].rearrange("p a b -> p (a b)"), in0=bT, in1=tmp, op=ALU.mult), ["F10", "F11"], ["Bhb"])
                    pool(lambda e: e.tensor_copy(out=arb_a, in_=b3(atb)), ["atb"], ["arb_a"])
                    vec(lambda e: e.tensor_tensor(out=arb_r, in0=v3(rT), in1=v3(Ea), op=ALU.mult), ["F0", "F6"], ["arb_r"])
                    pool(lambda e: e.tensor_tensor(out=ktb[:].rearrange("p a b -> p (a b)"), in0=kT2, in1=Eb, op=ALU.mult), ["F9", "F7"], ["ktb"])
                    vec(lambda e: e.tensor_tensor(out=btb[:].rearrange("p a b -> p (a b)"), in0=bT, in1=Eb, op=ALU.mult), ["F10", "F7"], ["btb"])
                    pool(lambda e: e.tensor_copy(out=gamc[:].rearrange("p g c -> p (g c)"), in_=v3(Ea)[:, :, 63]), ["F6"], ["gamc"])

                    proj_tm(PS[7][:], 1040, pk(7))
                    act(lambda e: e.activation(out=gsil[:], in_=PS[7][:], func=AF.Silu), pk(7), ["gsil"])
                    proj_tm(PS[0][:], GLA_IN + SHIFT_W, pk(0))
                    act(lambda e: e.activation(out=rsil[:], in_=PS[0][:], func=AF.Silu), pk(0), ["rsil"])
                    if not last:
                        transposes_evac()
                    if pending is not None:
                        back_b(*pending)
                        pending = None

                    ck("proj")
                    v2 = lambda ap: ap.rearrange("p (a b) -> p a b", b=128)
                    spg, csg = v2(F(1)[:, 0:256]), v2(F(1)[:, 256:512])
                    Eg = [v2(F(4)[:, 0:256]), v2(F(4)[:, 256:512]), v2(F(3)[:, 0:256])]
                    f2 = lambda t_: t_.rearrange("p a b -> p (a b)")
                    for grp in range(2):
                        mm(PS[5][:, grp * 128:(grp + 1) * 128], gup[:, grp * 128:(grp + 1) * 128], glrT[:], ["gup", "glrT"], pk(5))
                    for grp in range(2):
                        act(lambda e, grp=grp: e.activation(out=spg[:, grp, :], in_=PS[5][:, grp * 128:(grp + 1) * 128], func=AF.Exp, scale=-1.0, bias=negb[:, grp:grp + 1]),
                            pk(5) + ["negb"], ["F1"])
                    act(lambda e: e.activation(out=f2(spg), in_=f2(spg), func=AF.Ln, bias=1.0), ["F1"], ["F1"])
                    vec(lambda e: e.tensor_tensor_scan(out=f2(csg), data0=scanmask[:, 0:256], data1=f2(spg), initial=0.0, op0=ALU.mult, op1=ALU.add), ["cst", "F1"], ["F1"])
                    act(lambda e: e.activation(out=f2(Eg[0]), in_=f2(csg), func=AF.Exp, scale=-1.0 / 16), ["F1"], ["F4"])
                    act(lambda e: e.activation(out=f2(Eg[1]), in_=f2(csg), func=AF.Exp, scale=1.0 / 16), ["F1"], ["F4"])
                    csg3 = f2(csg).rearrange("p (a b) -> p a b", b=64)
                    pool(lambda e: e.tensor_tensor(out=f2(Eg[2]).rearrange("p (a b) -> p a b", b=64), in0=csg3[:, :, 63:64].to_broadcast([128, 4, 64]), in1=csg3, op=ALU.subtract),
                         ["F1"], ["F3"])
                    act(lambda e: e.activation(out=f2(Eg[2]), in_=f2(Eg[2]), func=AF.Exp, scale=-1.0 / 16), ["F3"], ["F3"])
                    vec(lambda e: e.scalar_tensor_tensor(out=f2(qib[:]), in0=f2(qT[:]), scalar=0.125, in1=f2(Eg[0]), op0=ALU.mult, op1=ALU.mult), ["qT", "F4"], ["qib"])
                    vec(lambda e: e.tensor_tensor(out=f2(kib[:]), in0=f2(kT[:]), in1=f2(Eg[1]), op=ALU.mult), ["kT", "F4"], ["kib"])
                    pool(lambda e: e.tensor_tensor(out=f2(ksb[:]), in0=f2(kT[:]), in1=f2(Eg[2]), op=ALU.mult), ["kT", "F3"], ["ksb"])
                    ck("elem")
                    for (src, dst, key, dkey, bank) in [(atb, atok, "atb", "atok", 1), (Bhb, Bhtok, "Bhb", "Bhtok", 2), (Khb, Khtok, "Khb", "Khtok", 4)]:
                        for g in range(4):
                            tr(PSb(bank)[:, g * 128:(g + 1) * 128], src[:, g, :], identb[:], [key, "identb"], pk(bank))
                        act(lambda e, dst=dst, bank=bank: e.copy(out=dst[:].rearrange("p a b -> p (a b)"), in_=PSb(bank)[:, 0:512]),
                            pk(bank), [dkey])
                    pool(lambda e: e.tensor_copy(out=rvb[:].rearrange("p a b -> p (a b)"), in_=rvT), ["F2"], ["rvb"])
                    for g in range(4):
                        tr(PSb(5)[:, g * 128:(g + 1) * 128], rvb[:, g, :], identb[:], ["rvb", "identb"], pk(5))
                    act(lambda e: e.copy(out=Vtok[:].rearrange("p a b -> p (a b)"), in_=PSb(5)[:, 0:512]), pk(5), ["Vtok"])

                    pool(lambda e: e.tensor_tensor(out=rvb[:].rearrange("p a b -> p (a b)"), in0=rT, in1=kT2, op=ALU.mult), ["F0", "F9"], ["rvb"])
                    for g in range(4):
                        mm(PS[5][:, g * 2:(g + 1) * 2], rvb[:, g, :], rkselb[:, g * 2:(g + 1) * 2], ["rvb", "rkselb"], pk(5))
                    act(lambda e: e.copy(out=st1[:, 8:16], in_=PS[5][:, 0:8]), pk(5), ["bcoef"])
                    ck("trans")
                    for (g, hp, cc) in GHC:
                        gh = g * 2 + hp
                        p0, c0 = hp * 64, cc * 64
                        tk = slice(cc * 64, (cc + 1) * 64)
                        rhs_ar = arb[p0:p0 + 64, g, cc, :, :].rearrange("p a i -> p (a i)")
                        bN, bK, cg = (6 if gh < 4 else 7), (0 if gh < 4 else 1), (gh % 4) * 128
                        mm(PS[bN][c0:c0 + 64, cg:cg + 128], btb[p0:p0 + 64, g, tk], rhs_ar, ["btb", "arb_a", "arb_r"], pk(bN), tp=(p0, c0))
                        mm(PS[bK][c0:c0 + 64, cg:cg + 128], ktb[p0:p0 + 64, g, tk], rhs_ar, ["ktb", "arb_a", "arb_r"], pk(bK), tp=(p0, c0))
                        mm(PS[2][c0:c0 + 64, gh * 64:gh * 64 + 64], atb[p0:p0 + 64, g, tk], btb[p0:p0 + 64, g, tk], ["atb", "btb"], pk(2), tp=(p0, c0))
                    mNA4 = maskNA.unsqueeze(1).to_broadcast([128, 4, 128])
                    mSL8 = maskSL.unsqueeze(1).to_broadcast([128, 8, 64])
                    p4 = lambda ap: ap.rearrange("p (a b) -> p a b", b=128)
                    vec(lambda e: e.tensor_tensor(out=NAb[:, 0:4, :], in0=p4(PS[6][:]), in1=mNA4, op=ALU.mult), pk(6) + ["cst"], ["NAb"])
                    vec(lambda e: e.tensor_tensor(out=NAb[:, 4:8, :], in0=p4(PS[7][:]), in1=mNA4, op=ALU.mult), pk(7) + ["cst"], ["NAb"])
                    vec(lambda e: e.tensor_tensor(out=KAb[:, 0:4, :], in0=p4(PS[0][:]), in1=mNA4, op=ALU.mult), pk(0) + ["cst"], ["KAb"])
                    vec(lambda e: e.tensor_tensor(out=KAb[:, 4:8, :], in0=p4(PS[1][:]), in1=mNA4, op=ALU.mult), pk(1) + ["cst"], ["KAb"])
                    vec(lambda e: e.tensor_tensor(out=Lb[0][:], in0=PS[2][:].rearrange("p (a b) -> p a b", b=64), in1=mSL8, op=ALU.mult), pk(2) + ["cst"], ["Lb0h0", "Lb0h1"])

                    for grp in range(2):
                        tr(PSb(6)[:, grp * 128:(grp + 1) * 128], ksb[:, grp, :], identb[:], ["ksb", "identb"], pk(6))
                    act(lambda e: e.copy(out=f2(kstok[:]), in_=PSb(6)[:, 0:256]), pk(6), ["kstok"])
                    for (h, cc) in HC:
                        grp, hp = h // 2, h % 2
                        p0 = hp * 64
                        c0 = cc * 64
                        tk = slice(c0, c0 + 64)
                        mm(PS[7][c0:c0 + 64, h * 64:(h + 1) * 64], kib[p0:p0 + 64, grp, tk], qib[p0:p0 + 64, grp, tk], ["kib", "qib"], pk(7), tp=(p0, c0))
                    vec(lambda e: e.tensor_tensor(out=scb[:], in0=PS[7][:, 0:256].rearrange("p (a b) -> p a b", b=64),
                                                  in1=maskUU.unsqueeze(1).to_broadcast([128, 4, 64]), op=ALU.mult), pk(7) + ["cst"], ["scb"])
                    ck("score")
                    for g in range(4):
                        for hp in range(2):
                            gh = g * 2 + hp
                            for cc in range(2):
                                c0 = cc * 64
                                mm(PS[3][c0:c0 + 64, gh * 64:gh * 64 + 64], KAb[c0:c0 + 64, gh, 0:64], Vtok[c0:c0 + 64, g, hp * 64:hp * 64 + 64],
                                   ["KAb", "Vtok"], pk(3), tp=(c0, c0))
                    pool(lambda e: e.tensor_copy(out=Xb[:, :, 0:64], in_=atok[:].rearrange("p g (h k) -> p (g h) k", k=64)), ["atok"], ["Xb0", "Xb1"])
                    act(lambda e: e.copy(out=Xb[:, :, 64:128], in_=PS[3][:].rearrange("p (a b) -> p a b", b=64)), pk(3), ["Xb0", "Xb1"])

                    ck("x0")
                    for lvl in range(6):
                        a_, b_ = lvl % 2, (lvl + 1) % 2
                        for hf in range(2):
                            ghs = range(hf * 4, hf * 4 + 4)
                            kX = "Xb%d" % hf
                            kN = "NAb" if lvl == 0 else "Nb%dh%d" % (a_, hf)
                            kL = "Lb%dh%d" % (a_, hf)
                            bank = 4 + hf
                            for gh in ghs:
                                for cc in range(2):
                                    c0 = cc * 64
                                    lhs = NAb[c0:c0 + 64, gh, 0:64] if lvl == 0 else Nb[a_][c0:c0 + 64, gh, :]
                                    mm(PS[bank][c0:c0 + 64, (gh % 4) * 128:(gh % 4) * 128 + 128], lhs, Xb[c0:c0 + 64, gh, :], [kN, kX], pk(bank), tp=(c0, c0))
                            if lvl < 5:
                                for gh in ghs:
                                    for cc in range(2):
                                        c0 = cc * 64
                                        lhsN = NAb[c0:c0 + 64, gh, 0:64] if lvl == 0 else Nb[a_][c0:c0 + 64, gh, :]
                                        Lp = Lb[a_][c0:c0 + 64, gh, :]
                                        gq = (gh % 4) * 64
                                        mm(PS[6 + hf][c0:c0 + 64, gq:gq + 64], Lp, lhsN, [kL, kN], pk(6 + hf), tp=(c0, c0))
                                        if lvl < 4:
                                            mm(PS[hf][c0:c0 + 64, gq:gq + 64], lhsN, Lp, [kL, kN], pk(hf), tp=(c0, c0))
                            hs = slice(hf * 4, hf * 4 + 4)
                            cs256 = slice(hf * 256, hf * 256 + 256)
                            vec(lambda e, hs=hs, bank=bank: e.tensor_tensor(out=Xb[:, hs, :], in0=p4(PS[bank][:]), in1=Xb[:, hs, :], op=ALU.add), pk(bank) + [kX], [kX])
                            if lvl < 5:
                                act(lambda e, b_=b_, hs=hs, hf=hf: e.copy(out=Nb[b_][:, hs, :].rearrange("p a b -> p (a b)"), in_=PS[6 + hf][:, 0:256]),
                                    pk(6 + hf), ["Nb%dh%d" % (b_, hf)])
                                if lvl < 4:
                                    act(lambda e, b_=b_, hs=hs, hf=hf: e.copy(out=Lb[b_][:, hs, :].rearrange("p a b -> p (a b)"), in_=PS[hf][:, 0:256]),
                                        pk(hf), ["Lb%dh%d" % (b_, hf)])

                    ck("neumann")
                    QTs = sw
                    for (g, hp, cc) in GHC:
                        gh = g * 2 + hp
                        if True:
                            if True:
                                p0, c0 = hp * 64, cc * 64
                                Wt = Xb[c0:c0 + 64, gh, 0:64]
                                Ut = Xb[c0:c0 + 64, gh, 64:128]
                                col = (g * 2 + cc) * 64
                                mm(PS[0][p0:p0 + 64, col:col + 64], Wt, Bhtok[c0:c0 + 64, g, p0:p0 + 64], ["Xb0", "Xb1", "Bhtok"], pk(0), tp=(c0, p0))
                                mm(PS[1][p0:p0 + 64, col:col + 64], Bhtok[c0:c0 + 64, g, p0:p0 + 64], Ut, ["Xb0", "Xb1", "Bhtok"], pk(1), start=True, stop=False, tp=(c0, p0))
                                mm(PS[1][p0:p0 + 64, col:col + 64], Khtok[c0:c0 + 64, g, p0:p0 + 64], Vtok[c0:c0 + 64, g, p0:p0 + 64], ["Khtok", "Vtok"], pk(1),
                                   start=False, stop=True, tp=(c0, p0))
                                mm(PS[2][p0:p0 + 64, col:col + 64], Wt, NAb[c0:c0 + 64, gh, 64:128], ["Xb0", "Xb1", "NAb"], pk(2), tp=(c0, p0))
                    pool(lambda e: e.tensor_tensor(out=dG[:].rearrange("p g c k -> p (g c) k"),
                                                   in0=identblk.unsqueeze(1).to_broadcast([128, 8, 64]),
                                                   in1=gamc[:].rearrange("p g c -> p (g c)").unsqueeze(2).to_broadcast([128, 8, 64]), op=ALU.mult),
                         ["cst", "gamc"], ["dG"])
                    vec(lambda e: e.tensor_tensor(out=Pb[:].rearrange("p g c k -> p (g c k)"), in0=PS[0][:], in1=dG[:].rearrange("p g c k -> p (g c k)"), op=ALU.add),
                        pk(0) + ["dG"], ["Pb"])
                    act(lambda e: e.copy(out=QTs, in_=PS[1][:]), pk(1), ["F3"])
                    vec(lambda e: e.tensor_tensor(out=GTb[:].rearrange("p g (c i) -> p (g c) i", i=64), in0=PS[2][:].rearrange("p (a b) -> p a b", b=64),
                                                  in1=arb_r, op=ALU.add), pk(2) + ["arb_r"], ["GTb"])

                    ck("pqg")
                    for cc in range(2):
                        c0 = cc * 64
                        kY = "ps3c%d" % cc
                        for g in range(4):
                            for hp in range(2):
                                gh = g * 2 + hp
                                p0 = hp * 64
                                yo = PS[3][c0:c0 + 64, gh * 64:gh * 64 + 64]
                                mm(yo, NAb[c0:c0 + 64, gh, 64:128], Xb[c0:c0 + 64, gh, 64:128], ["NAb", "Xb0", "Xb1"], [kY], start=True, stop=False, tp=(c0, c0))
                                mm(yo, KAb[c0:c0 + 64, gh, 64:128], Vtok[c0:c0 + 64, g, p0:p0 + 64], ["KAb", "Vtok"], [kY], start=False, stop=True, tp=(c0, c0))
                    for cc in range(2):
                        c0 = cc * 64
                        for hp in (cc, 1 - cc):
                            for g in range(4):
                                gh = g * 2 + hp
                                p0 = hp * 64
                                mm(PS[2][c0:c0 + 64, gh * 64:gh * 64 + 64], GTb[p0:p0 + 64, g, c0:c0 + 64], Ab[p0:p0 + 64, g, 0:64], ["GTb", "Ab"], pk(2), tp=(p0, c0))
                        for g in range(4):
                            for hp in range(2):
                                p0 = hp * 64
                                mm(PS[4][p0:p0 + 64, g * 128:(g + 1) * 128], Pb[p0:p0 + 64, g, cc, :], Ab[p0:p0 + 64, g, :], ["Pb", "Ab"], pk(4), tp=(p0, p0))
                        q4 = QTs.rearrange("p (g c v) -> p g c v", c=2, v=64)
                        vec(lambda e, cc=cc, q4=q4: e.tensor_tensor(out=A32[:, :, 0:64], in0=p4(PS[4][:])[:, :, 0:64], in1=q4[:, :, cc, :], op=ALU.add),
                            pk(4) + ["F3"], ["A32"])
                        vec(lambda e: e.tensor_copy(out=A32[:, :, 64:128], in_=p4(PS[4][:])[:, :, 64:128]), pk(4), ["A32"])
                        pool(lambda e: e.tensor_copy(out=Ab[:], in_=A32[:]), ["A32"], ["Ab"])

                    ck("scan")
                    ck("repi")
                    for cc in range(2):
                        c0 = cc * 64
                        tk = slice(c0, c0 + 64)
                        kO = "ps0c%d" % cc
                        for h in range(4):
                            oo = PS[0][c0:c0 + 64, h * 128:(h + 1) * 128]
                            mm(oo, scb[c0:c0 + 64, h, :], gvb[c0:c0 + 64, h * 128:(h + 1) * 128], ["scb", "gvb"], [kO], tp=(c0, c0))
                        for h in (cc, cc + 2, 1 - cc, 3 - cc):
                            grp, hp = h // 2, h % 2
                            p0 = hp * 64
                            mm(PS[4][c0:c0 + 64, h * 128:(h + 1) * 128], qib[p0:p0 + 64, grp, tk], Sb[p0:p0 + 64, grp, :], ["qib", "Sb"], pk(4), tp=(p0, c0))
                        for h in (cc, cc + 2, 1 - cc, 3 - cc):
                            grp, hp = h // 2, h % 2
                            p0 = hp * 64
                            mm(PS[1][p0:p0 + 64, grp * 128:(grp + 1) * 128], kstok[c0:c0 + 64, grp, p0:p0 + 64], gvb[c0:c0 + 64, h * 128:(h + 1) * 128],
                               ["kstok", "gvb"], pk(1), tp=(c0, p0))
                        for grp in range(2):
                            dcol = Eg[0][:, grp, c0 + 63:c0 + 64]
                            vec(lambda e, grp=grp, dcol=dcol: e.scalar_tensor_tensor(out=S32[:, grp, :], in0=S32[:, grp, :], scalar=dcol,
                                                                                     in1=PS[1][:, grp * 128:(grp + 1) * 128], op0=ALU.mult, op1=ALU.add),
                                ["S32", "F4", "ps1"], ["S32"])
                        pool(lambda e: e.tensor_copy(out=Sb[:], in_=S32[:]), ["S32"], ["Sb"])
                        vec(lambda e, c0=c0: e.tensor_tensor(out=Dc[:], in0=Dc[:], in1=Eg[0][:, :, c0 + 63], op=ALU.mult), ["Dc", "F4"], ["Dc"])
                    os_, osq = kk_, kT2
                    act(lambda e: e.copy(out=osq, in_=PS[4][:]), pk(4), ["F9"])
                    vec(lambda e: e.tensor_tensor(out=os_, in0=PS[0][:], in1=osq, op=ALU.add), pk(0) + ["F9"], ["F8"])
                    pool(lambda e: e.tensor_tensor(out=osq, in0=os_, in1=os_, op=ALU.mult), ["F8"], ["F9"])
                    vec(lambda e: e.reduce_sum(out=st1[:, 2:6], in_=osq.rearrange("p (h v) -> p h v", v=128), axis=AX.X), ["F9"], ["gst"])
                    rstd_from(st1[:, 2:6], st1[:, 2:6], 1.0 / 128, 1e-6, ["gst"], ["gst"])
                    os3 = os_.rearrange("p (h v) -> p h v", v=128)
                    vec(lambda e: e.tensor_tensor(out=os3, in0=os3, in1=st1[:, 2:6].unsqueeze(2).to_broadcast([128, 4, 128]), op=ALU.mult), ["F8", "gst"], ["F8"])
                    pool(lambda e: e.tensor_tensor(out=os3, in0=os3, in1=gnw_bc[:].unsqueeze(1).to_broadcast([128, 4, 128]), op=ALU.mult), ["F8", "gnw"], ["F8"])
                    vec(lambda e: e.tensor_tensor(out=merged[:, 0:512], in0=os_, in1=gsil[:], op=ALU.mult), ["F8", "gsil"], ["merged_g"])

                    pending = (t, xt, kx)
                if pending is not None:
                    back_a(*pending)
                    back_b(*pending)
                    pending = None

        except _Stop:
            pass
        if not fused:
            dma(oA, A32[:].rearrange("p a b -> p (a b)"), reads=["A32"])
            dma(oS, S32[:].rearrange("p a b -> p (a b)"), reads=["S32"])
            dma(oD, Dc[:], reads=["Dc"])
        S.finish()
        S.emit()
    return nc, S


def _consts():
    c = np.zeros((128, 1024), np.float32)
    c[:, 0:128] = np.eye(128, dtype=np.float32)
    e64 = np.eye(64, dtype=np.float32)
    c[:, 128:192] = np.concatenate([e64, e64], 0)
    su = np.triu(np.ones((64, 64), np.float32), 1)
    uu = np.triu(np.ones((64, 64), np.float32), 0)
    c[:, 192:256] = np.concatenate([su, su], 0)
    c[:, 256:320] = np.concatenate([uu, uu], 0)
    c[:, 320:384] = np.concatenate([su.T, su.T], 0)
    bo = np.zeros((128, 128), np.float32)
    bo[0:64, 0:64] = 1.0
    bo[64:128, 64:128] = 1.0
    c[:, 384:512] = bo
    sm = np.ones((128, 512), np.float32)
    sm[:, ::64] = 0.0
    c[:, 512:1024] = sm
    return c


def _fm_cols(v512):
    return np.ascontiguousarray(v512.reshape(4, 128).T)


def _layer_params(inp, l):
    pcol = np.zeros((128, 32), np.float32)
    pcol[:, 0:4] = _fm_cols(inp["rwkv_w0"][l])
    pcol[:, 4:8] = _fm_cols(inp["rwkv_a0"][l])
    pcol[:, 8:12] = _fm_cols(inp["rwkv_k_k"][l])
    pcol[:, 12:16] = _fm_cols(inp["rwkv_k_a"][l])
    pcol[:, 16:18] = inp["gla_gate_bias"][l].reshape(2, 128).T
    pcol[:, 20:28] = inp["norm_w"][l].reshape(8, 128).T
    rk = inp["rwkv_r_k"][l].reshape(512)
    rksel = np.zeros((128, 4, 2), np.float32)
    for g in range(4):
        for hp in range(2):
            rksel[hp * 64:(hp + 1) * 64, g, hp] = rk[g * 128 + hp * 64:g * 128 + (hp + 1) * 64]
    lrw = np.concatenate([inp["rwkv_w_up"][l], inp["rwkv_a_up"][l]], 0)
    return {
        "w_in": np.ascontiguousarray(inp["w_in"][l]), "w_out": np.ascontiguousarray(inp["w_out"][l]),
        "pcol": pcol, "rksel": rksel.reshape(128, 8), "mu": np.ascontiguousarray(inp["rwkv_mu"][l]),
        "lnw": np.ascontiguousarray(inp["rwkv_ln_w"][l]), "lnb": np.ascontiguousarray(inp["rwkv_ln_b"][l]),
        "gnw": np.ascontiguousarray(inp["gla_norm_w"][l]), "fnw": np.ascontiguousarray(inp["final_norm_w"]),
        "lrw": np.ascontiguousarray(lrw), "gup": np.ascontiguousarray(inp["gla_gate_up"][l]), "cst": _consts(),
    }


def _zero_sums():
    return {"sumM": np.zeros((8, 128, 256), np.float32), "sumPhi": np.zeros((8, 128, 256), np.float32),
            "sumS": np.zeros((8, 128, 256), np.float32), "sumD": np.zeros((8, 128, 2), np.float32)}


def _sums_from_results(results):
    s = _zero_sums()
    for j, r in enumerate(results):
        A = r["oA"].reshape(128, 4, 128)
        s["sumM"][j] = A[:, :, 0:64].reshape(128, 256)
        Pc = A[:, :, 64:128].reshape(2, 64, 4, 64)
        s["sumPhi"][j] = Pc.transpose(0, 3, 2, 1).reshape(128, 256)
        s["sumS"][j] = r["oS"]
        s["sumD"][j] = r["oD"]
    return s


def _use_masks():
    u = np.zeros((NCORES, 128, 8), np.float32)
    for c in range(NCORES):
        b, q = divmod(c, 4)
        for j in range(NCORES):
            bj, qj = divmod(j, 4)
            if bj == b and qj < q:
                u[c, :, j] = 1.0
    return u


_NC_CACHE = {}


def _get_nc(NT, fused=False):
    if (NT, fused) not in _NC_CACHE:
        _NC_CACHE[(NT, fused)] = build_pass(NT, fused=fused)[0]
    return _NC_CACHE[(NT, fused)]


def _run_pass(xfull, lp, sums, use):
    in_maps = []
    for c in range(NCORES):
        b, q = divmod(c, 4)
        xs = np.zeros((TOK + 1, D), np.float32)
        xs[1:] = xfull[b, q * TOK:(q + 1) * TOK]
        if q > 0:
            xs[0] = xfull[b, q * TOK - 1]
        m = dict(lp)
        m.update(sums)
        m["x_sh"] = xs
        m["use"] = use[c]
        in_maps.append(m)
    res = run_bass_kernel_spmd(_get_nc(TOK // 128), in_maps, core_ids=list(range(NCORES)))
    return res.results


def _gather(results, key):
    out = np.zeros((2, 4 * TOK, D), np.float32)
    for c, r in enumerate(results):
        b, q = divmod(c, 4)
        out[b, q * TOK:(q + 1) * TOK] = r[key]
    return out


def kernel_unfused(**inputs):
    inp = {k: np.asarray(v, dtype=np.float32) for k, v in inputs.items()}
    use = _use_masks()
    x = inp["x"]
    res = None
    for l in range(2):
        lp = _layer_params(inp, l)
        r0 = _run_pass(x, lp, _zero_sums(), use)
        res = _run_pass(x, lp, _sums_from_results(r0), use)
        x = _gather(res, "x_out")
    return _gather(res, "y_out")


def _fused_params(inp):
    lps = [_layer_params(inp, l) for l in range(2)]
    m = {}
    for k in lps[0]:
        if k in ("fnw", "cst"):
            m[k] = lps[0][k]
        else:
            m[k] = np.ascontiguousarray(np.stack([lps[0][k], lps[1][k]], 0))
    return m


def kernel(**inputs):
    inp = {k: np.asarray(v, dtype=np.float32) for k, v in inputs.items()}
    T = inp["x"].shape[1]
    fp = _fused_params(inp)
    in_maps = []
    for c in range(NCORES):
        b, q = divmod(c, 4)
        xs = np.zeros((T + 1, D), np.float32)
        xs[1:] = inp["x"][b]
        m = dict(fp)
        m["x_sh"] = xs
        in_maps.append(m)
    res = run_bass_kernel_spmd(_get_nc(T // 128, True), in_maps, core_ids=list(range(NCORES)))
    out = np.zeros((2, T, D), np.float32)
    Q = T // 4
    for c, r in enumerate(res.results):
        b, q = divmod(c, 4)
        out[b, q * Q:(q + 1) * Q] = r["y_out"][q * Q:(q + 1) * Q]
    return out
```

```python
import contextlib
import numpy as np
import concourse.bass as bass
import concourse.mybir as mybir
from concourse.bass_utils import run_bass_kernel_spmd

F32 = mybir.dt.float32
BF16 = mybir.dt.bfloat16
AF = mybir.ActivationFunctionType
ALU = mybir.AluOpType
AX = mybir.AxisListType

D = 1024
NCORES = 8
TOK = 2048
GLA_IN = 1552
SHIFT_W = 1664
IN_W = 3728
KDEC = float(np.exp(-0.5))


class Sched:
    ENG = ("sync", "scalar", "vector", "gpsimd", "tensor")
    NDMA = 24

    def __init__(self, nc):
        self.nc = nc
        self.dma_streams = {"sync": ["dq_sync%d" % i for i in range(self.NDMA)], "gpsimd": ["dq_pool%d" % i for i in range(8)]}
        self.streams = list(self.ENG) + self.dma_streams["sync"] + self.dma_streams["gpsimd"]
        self.rrs = {"sync": 0, "gpsimd": 0}
        self.ops = {e: [] for e in self.ENG}
        self.tick = {s: 0 for s in self.streams}
        self.seen = {e: {s: 0 for s in self.streams} for e in self.ENG}
        self.lastw = {}
        self.readers = {}
        self.n = 0
        self.rr = 0
        self.pe_cur = None
        self._drain = -1

    def _need(self, eng, deps, pe_fifo=False):
        req = {}
        for (s, t) in deps:
            if pe_fifo and s == "tensor" and eng == "tensor" and t != self._drain:
                continue
            if t > self.seen[eng][s]:
                req[s] = max(req.get(s, 0), t)
        for s, t in req.items():
            self.seen[eng][s] = t
            self.ops[eng].append(("wait", s, t))

    def op(self, eng, fn, reads=(), writes=(), dma=False, pe_fifo=False, pe_class=None):
        deps = []
        if eng == "tensor":
            if pe_class is not None and pe_class != "O" and pe_class == self.pe_cur:
                pe_fifo = True
            else:
                pe_fifo = True
                self._drain = self.tick["tensor"]
                if self.tick["tensor"] > 0:
                    deps.append(("tensor", self.tick["tensor"]))
            self.pe_cur = pe_class
        for k in reads:
            if k in self.lastw:
                deps.append(self.lastw[k])
        for k in writes:
            if k in self.lastw:
                deps.append(self.lastw[k])
            deps.extend(self.readers.get(k, ()))
        if dma:
            pool_ = self.dma_streams[eng]
            stream = pool_[self.rrs[eng]]
            self.rrs[eng] = (self.rrs[eng] + 1) % len(pool_)
            if self.tick[stream] > 0:
                deps.append((stream, self.tick[stream]))
            inc = 16
        else:
            stream, inc = eng, 1
        self._need(eng, deps, pe_fifo)
        self.tick[stream] += inc
        t = self.tick[stream]
        self.ops[eng].append(("op", fn, stream, inc))
        self.n += 1
        for k in reads:
            self.readers.setdefault(k, []).append((stream, t))
        for k in writes:
            self.lastw[k] = (stream, t)
            self.readers[k] = []

    def finish(self):
        for eng in self.ENG:
            self._need(eng, [(s, self.tick[s]) for s in self.streams if self.tick[s] > 0])

    def emit(self):
        nc = self.nc
        waited = {e: set() for e in self.ENG}
        for e in self.ENG:
            for it in self.ops[e]:
                if it[0] == "wait" and it[1] in waited:
                    waited[it[1]].add(it[2])
        rank = {e: {t: i + 1 for i, t in enumerate(sorted(waited[e]))} for e in self.ENG}
        with contextlib.ExitStack() as st:
            sems = {s: st.enter_context(nc.semaphore("s_" + s)) for s in self.streams}
            block = st.enter_context(nc.Block())

            def runner(e):
                def f(engine):
                    cnt = 0
                    for it in self.ops[e]:
                        if it[0] == "wait":
                            v = rank[it[1]][it[2]] if it[1] in rank else it[2]
                            engine.wait_ge(sems[it[1]], v)
                        else:
                            ins = it[1](engine)
                            if it[2] in rank:
                                cnt += 1
                                if cnt in rank[it[2]]:
                                    ins.then_inc(sems[it[2]], 1)
                            else:
                                ins.then_inc(sems[it[2]], it[3])
                return f
            block.sync(runner("sync"))
            block.scalar(runner("scalar"))
            block.vector(runner("vector"))
            block.gpsimd(runner("gpsimd"))
            block.tensor(runner("tensor"))


class _Stop(Exception):
    pass


def build_pass(NT=16, dbg_names=(), stop_after=None, fused=False):
    nc = bass.Bass("TRN2", target_bir_lowering=False)
    T = NT * 128

    def din(name, shape, dt=F32):
        return nc.dram_tensor(name, list(shape), dt, kind="ExternalInput").ap()

    def dout(name, shape, dt=F32):
        return nc.dram_tensor(name, list(shape), dt, kind="ExternalOutput").ap()

    LD = [2] if fused else []
    x_sh = din("x_sh", [T + 1, D])
    w_in = din("w_in", LD + [D, IN_W])
    w_out = din("w_out", LD + [D, D])
    pcol = din("pcol", LD + [128, 32])
    rksel_d = din("rksel", LD + [128, 8])
    mu_d = din("mu", LD + [SHIFT_W])
    lnw_d = din("lnw", LD + [512])
    lnb_d = din("lnb", LD + [512])
    gnw_d = din("gnw", LD + [128])
    fnw_d = din("fnw", [D])
    lrw_d = din("lrw", LD + [128, 512])
    gup_d = din("gup", LD + [16, 256])
    cst_d = din("cst", [128, 1024])
    y_out = dout("y_out", [T, D])
    if fused:
        x1s = nc.dram_tensor("x1s", [T, D], F32).ap()
    else:
        sumM_d = din("sumM", [8, 128, 256])
        sumPhi_d = din("sumPhi", [8, 128, 256])
        sumS_d = din("sumS", [8, 128, 256])
        sumD_d = din("sumD", [8, 128, 2])
        use_d = din("use", [128, 8])
        x_out = dout("x_out", [T, D])
        oA = dout("oA", [128, 512])
        oS = dout("oS", [128, 256])
        oD = dout("oD", [128, 2])
    dbg_out = {}

    with contextlib.ExitStack() as st:
        def sb(name, shape, dt=F32):
            return st.enter_context(nc.sbuf_tensor(name, list(shape), dt))

        S = Sched(nc)
        w1b = sb("w1b", [128, 8, IN_W], BF16)
        w2b = sb("w2b", [128, 8, SHIFT_W], BF16)
        wob = sb("wob", [128, 8, D], BF16)
        cst = sb("cst_sb", [128, 1024])
        identf = cst[:, 0:128]
        identblk = cst[:, 128:192]
        maskNA = cst[:, 192:320]
        maskSL = cst[:, 320:384]
        maskUU = cst[:, 256:320]
        blockones = cst[:, 384:512]
        scanmask = cst[:, 512:1024]
        identb = sb("identb", [128, 128], BF16)
        pc = sb("pcol_sb", [128, 32])
        negb = sb("negb", [128, 2])
        rksel = sb("rksel_sb", [128, 8])
        lnw_bc = sb("lnw_bc", [128, 512])
        lnb_bc = sb("lnb_bc", [128, 512])
        gnw_bc = sb("gnw_bc", [128, 128])
        fnw_bc = sb("fnw_bc", [128, D])
        lrw = sb("lrw_sb", [128, 512])
        gup = sb("gup_sb", [16, 256])
        use = sb("use_sb", [128, 8])
        NF = 13
        Fpool = sb("Fpool", [128, NF * 512])

        def F(i):
            return Fpool[:, i * 512:(i + 1) * 512]

        def fk(lo, hi):
            return ["F%d" % i for i in range(lo // 512, (hi - 1) // 512 + 1)]

        xt = sb("xt", [128, D])
        xt2 = sb("xt2", [128, D])
        xt3 = sb("xt3", [128, D])
        xts = [xt, xt2, xt3]
        hb = sb("hb", [128, D], BF16)
        hT = sb("hT", [128, 8, 129], BF16)
        st1 = sb("st1", [128, 16])
        arb = sb("arb", [128, 4, 2, 2, 64], BF16)
        btb = sb("btb", [128, 4, 128], BF16)
        ktb = sb("ktb", [128, 4, 128], BF16)
        Bhb = sb("Bhb", [128, 4, 128], BF16)
        Khb = sb("Khb", [128, 4, 128], BF16)
        atb = sb("atb", [128, 4, 128], BF16)
        atok = sb("atok", [128, 4, 128], BF16)
        Bhtok = sb("Bhtok", [128, 4, 128], BF16)
        Khtok = sb("Khtok", [128, 4, 128], BF16)
        Vtok = sb("Vtok", [128, 4, 128], BF16)
        rvb = sb("rvb", [128, 4, 128], BF16)
        NAb = sb("NAb", [128, 8, 128], BF16)
        KAb = sb("KAb", [128, 8, 128], BF16)
        Lb = [sb("Lb%d" % i, [128, 8, 64], BF16) for i in range(2)]
        Nb = [sb("Nb%d" % i, [128, 8, 64], BF16) for i in range(2)]
        Xb = sb("Xb", [128, 8, 128], BF16)
        Pb = sb("Pb", [128, 4, 2, 64], BF16)
        GTb = sb("GTb", [128, 4, 128], BF16)
        A32 = sb("A32", [128, 4, 128])
        Ab = sb("Ab", [128, 4, 128], BF16)
        gamc = sb("gamc", [128, 4, 2])
        dG = sb("dG", [128, 4, 2, 64])
        qT = sb("qT", [128, 2, 128])
        kT = sb("kT", [128, 2, 128])
        glrT = sb("glrT", [16, 128])
        qib = sb("qib", [128, 2, 128], BF16)
        kib = sb("kib", [128, 2, 128], BF16)
        ksb = sb("ksb", [128, 2, 128], BF16)
        kstok = sb("kstok", [128, 2, 128], BF16)
        gvb = sb("gvb", [128, 512], BF16)
        scb = sb("scb", [128, 4, 64], BF16)
        S32 = sb("S32", [128, 2, 128])
        Sb = sb("Sb", [128, 2, 128], BF16)
        Dc = sb("Dc", [128, 2])
        gsil = sb("gsil", [128, 512], BF16)
        rsil = sb("rsil", [128, 512], BF16)
        merged = sb("merged", [128, D], BF16)
        mT = sb("mT", [128, 8, 128], BF16)
        on = sb("on", [128, D])
        gn = sb("gn", [128, 40])
        omka = sb("omka", [128, 4])
        wlalb = sb("wlalb", [128, 128], BF16)
        lrwb = sb("lrwb", [128, 512], BF16)
        bonesb = sb("bonesb", [128, 128], BF16)
        rkselb = sb("rkselb", [128, 8], BF16)
        PS = [st.enter_context(nc.psum_tensor("ps%d" % i, [128, 512], F32)) for i in range(8)]

        def pk(i):
            return ["ps%dc0" % i, "ps%dc1" % i] if i in (0, 3) else ["ps%d" % i]

        def PSb(i):
            return PS[i][:].bitcast(BF16)

        def ck(stage):
            if stop_after is not None and stage == stop_after:
                raise _Stop()

        def dma(out, in_, reads=(), writes=(), eng="sync"):
            S.op(eng, lambda e: e.dma_start(out=out, in_=in_), reads, writes, dma=True)

        def vec(fn, reads, writes):
            S.op("vector", fn, reads, writes)

        def act(fn, reads, writes):
            S.op("scalar", fn, reads, writes)

        def pool(fn, reads, writes):
            S.op("gpsimd", fn, reads, writes)

        def pe(fn, reads, writes, cls):
            S.op("tensor", fn, reads, writes, pe_class=cls)

        def mm(out, lhsT, rhs, reads, writes, start=True, stop=True, tp=None, fifo=False):
            K_, M_ = lhsT.shape[0], int(np.prod(lhsT.shape[1:]))
            if tp is None:
                cls = "F" if K_ == 128 else "O"
            elif M_ == 64 and K_ == 64:
                cls = "D" if tp[0] == tp[1] else "X"
            else:
                cls = "O"
            pe(lambda e: e.matmul(out, lhsT=lhsT, rhs=rhs, start=start, stop=stop, tile_position=tp), reads, writes, cls)

        def tr(out, in_, ident, reads, writes, fifo=False):
            pe(lambda e: e.transpose(out=out, in_=in_, identity=ident), reads, writes, "F")

        def dbg(name, ap, keys, shape, dt=F32):
            if name in dbg_names:
                d = dout("dbg_" + name, shape, dt)
                dma(d, ap, reads=keys)

        def rstd_from(dst, src, scale, eps, keys_src, keys_dst):
            act(lambda e: e.activation(out=dst, in_=src, func=AF.Ln, bias=eps, scale=scale), keys_src, keys_dst)
            act(lambda e: e.activation(out=dst, in_=dst, func=AF.Exp, scale=-0.5), keys_dst, keys_dst)

        try:
            layers = [0, 1] if fused else [0]
            for l in layers:
                Lx = (lambda ap, l=l: ap[l]) if fused else (lambda ap: ap)
                if l == 0:
                    dma(cst[:], cst_d, writes=["cst"])
                    dma(fnw_bc[:], fnw_d.partition_broadcast(128), writes=["fnw"], eng="gpsimd")
                    if not fused:
                        dma(use[:], use_d, writes=["use"])
                dma(pc[:], Lx(pcol), writes=["pc"])
                dma(rksel[:], Lx(rksel_d), writes=["rksel"])
                vec(lambda e: e.tensor_copy(out=rkselb[:], in_=rksel[:]), ["rksel"], ["rkselb"])
                dma(lrw[:], Lx(lrw_d), writes=["lrw"])
                vec(lambda e: e.tensor_copy(out=lrwb[:], in_=lrw[:]), ["lrw"], ["lrwb"])
                dma(gup[:], Lx(gup_d), writes=["gup"])
                dma(lnw_bc[:], Lx(lnw_d).partition_broadcast(128), writes=["lnw"], eng="gpsimd")
                dma(lnb_bc[:], Lx(lnb_d).partition_broadcast(128), writes=["lnb"], eng="gpsimd")
                dma(gnw_bc[:], Lx(gnw_d).partition_broadcast(128), writes=["gnw"], eng="gpsimd")
                if l == 0:
                    vec(lambda e: e.tensor_copy(out=identb[:], in_=identf), ["cst"], ["identb"])
                    vec(lambda e: e.tensor_copy(out=bonesb[:], in_=blockones), ["cst"], ["bonesb"])
                vec(lambda e: e.tensor_scalar(out=negb[:], in0=pc[:, 16:18], scalar1=-1.0, scalar2=None, op0=ALU.mult), ["pc"], ["negb"])
                vec(lambda e: e.tensor_scalar(out=omka[:], in0=pc[:, 12:16], scalar1=-1.0, scalar2=1.0, op0=ALU.mult, op1=ALU.add), ["pc"], ["omka"])
                W0 = lambda g: pc[:, g:g + 1]
                A0 = lambda g: pc[:, 4 + g:5 + g]
                KK = lambda g: pc[:, 8 + g:9 + g]
                KA = lambda g: pc[:, 12 + g:13 + g]
                NW = lambda kc: pc[:, 20 + kc:21 + kc]

                ck("setup")
                mu_bc = Fpool[:, 0:1664]
                omu_bc = Fpool[:, 1664:3328]
                stg = [Fpool[:, 3328:4992], Fpool[:, 4992:6656]]
                kmu, komu = fk(0, 1664), fk(1664, 3328)
                kst = [fk(3328, 4992), fk(4992, 6656)]
                dma(mu_bc, Lx(mu_d).partition_broadcast(128), writes=kmu, eng="gpsimd")
                vec(lambda e: e.tensor_scalar(out=omu_bc, in0=mu_bc, scalar1=-1.0, scalar2=1.0, op0=ALU.mult, op1=ALU.add), kmu, komu)
                si = 0
                for kc in range(8):
                    rows = slice(kc * 128, (kc + 1) * 128)
                    s_, ks_ = stg[si % 2], kst[si % 2]; si += 1
                    dma(s_[:, 0:GLA_IN], Lx(w_in)[rows, 0:GLA_IN], writes=ks_)
                    act(lambda e, s_=s_, kc=kc: e.activation(out=w1b[:, kc, 0:GLA_IN], in_=s_[:, 0:GLA_IN], func=AF.Copy, scale=NW(kc)),
                        ks_ + ["pc"], ["w1b"])
                    s_, ks_ = stg[si % 2], kst[si % 2]; si += 1
                    dma(s_[:, 0:SHIFT_W], Lx(w_in)[rows, GLA_IN:GLA_IN + SHIFT_W], writes=ks_)
                    pool(lambda e, s_=s_, kc=kc: e.tensor_scalar(out=s_[:, 0:SHIFT_W], in0=s_[:, 0:SHIFT_W], scalar1=NW(kc), scalar2=None, op0=ALU.mult),
                         ks_ + ["pc"], ks_)
                    vec(lambda e, s_=s_, kc=kc: e.tensor_tensor(out=w2b[:, kc, :], in0=s_[:, 0:SHIFT_W], in1=mu_bc, op=ALU.mult), ks_ + kmu, ["w2b"])
                    pool(lambda e, s_=s_, kc=kc: e.tensor_tensor(out=w1b[:, kc, GLA_IN:GLA_IN + SHIFT_W], in0=s_[:, 0:SHIFT_W], in1=omu_bc, op=ALU.mult),
                         ks_ + komu, ["w1b"])
                    s_, ks_ = stg[si % 2], kst[si % 2]; si += 1
                    dma(s_[:, 0:512], Lx(w_in)[rows, GLA_IN + SHIFT_W:IN_W], writes=ks_)
                    act(lambda e, s_=s_, kc=kc: e.activation(out=w1b[:, kc, GLA_IN + SHIFT_W:IN_W], in_=s_[:, 0:512], func=AF.Copy, scale=NW(kc)),
                        ks_ + ["pc"], ["w1b"])
                    s_, ks_ = stg[si % 2], kst[si % 2]; si += 1
                    dma(s_[:, 0:D], Lx(w_out)[rows, :], writes=ks_)
                    vec(lambda e, s_=s_, kc=kc: e.tensor_copy(out=wob[:, kc, :], in_=s_[:, 0:D]), ks_, ["wob"])

                ck("wprep")
                Mst = F(0)[:, 0:256]
                Sst = F(0)[:, 256:512]
                kF0 = ["F0"]
                vec(lambda e: e.memset(Mst, 0.0), [], kF0)
                vec(lambda e: e.memset(Sst, 0.0), [], kF0)
                for j in (range(8) if not fused else []):
                    bM, bPhi, bS = F(1)[:, 0:256], F(1)[:, 256:512], F(2)[:, 0:256]
                    bD, dM, dS_ = F(2)[:, 256:258], F(3)[:, 0:256], F(3)[:, 256:512]
                    dma(bM, sumM_d[j], writes=["F1"])
                    dma(bPhi, sumPhi_d[j], writes=["F1"])
                    dma(bS, sumS_d[j], writes=["F2"])
                    dma(bD, sumD_d[j], writes=["F2"])
                    for g in range(4):
                        for hp in range(2):
                            p0 = hp * 64
                            mm(PS[0][p0:p0 + 64, g * 64:(g + 1) * 64], bPhi[p0:p0 + 64, g * 64:(g + 1) * 64],
                               Mst[p0:p0 + 64, g * 64:(g + 1) * 64], ["F1"] + kF0, pk(0), tp=(p0, p0))
                    vec(lambda e: e.tensor_tensor(out=dM, in0=PS[0][:, 0:256], in1=bM, op=ALU.add), pk(0) + ["F1"], ["F3"])
                    vec(lambda e: e.tensor_tensor(out=dM, in0=dM, in1=Mst, op=ALU.subtract), ["F3"] + kF0, ["F3"])
                    vec(lambda e, j=j: e.scalar_tensor_tensor(out=Mst, in0=dM, scalar=use[:, j:j + 1], in1=Mst, op0=ALU.mult, op1=ALU.add),
                        ["F3", "use"] + kF0, kF0)
                    for grp in range(2):
                        vec(lambda e, grp=grp: e.scalar_tensor_tensor(out=dS_[:, grp * 128:(grp + 1) * 128], in0=Sst[:, grp * 128:(grp + 1) * 128],
                                                                      scalar=bD[:, grp:grp + 1], in1=bS[:, grp * 128:(grp + 1) * 128],
                                                                      op0=ALU.mult, op1=ALU.add), ["F2"] + kF0, ["F3"])
                    vec(lambda e: e.tensor_tensor(out=dS_, in0=dS_, in1=Sst, op=ALU.subtract), ["F3"] + kF0, ["F3"])
                    vec(lambda e, j=j: e.scalar_tensor_tensor(out=Sst, in0=dS_, scalar=use[:, j:j + 1], in1=Sst, op0=ALU.mult, op1=ALU.add),
                        ["F3", "use"] + kF0, kF0)
                for g in range(4):
                    vec(lambda e, g=g: e.tensor_copy(out=A32[:, g, 0:64], in_=Mst[:, g * 64:(g + 1) * 64]), kF0, ["A32"])
                    vec(lambda e, g=g: e.tensor_copy(out=A32[:, g, 64:128], in_=identblk), ["cst"], ["A32"])
                vec(lambda e: e.tensor_copy(out=Ab[:], in_=A32[:]), ["A32"], ["Ab"])
                vec(lambda e: e.tensor_copy(out=S32[:].rearrange("p a b -> p (a b)"), in_=Sst), kF0, ["S32"])
                vec(lambda e: e.tensor_copy(out=Sb[:], in_=S32[:]), ["S32"], ["Sb"])
                vec(lambda e: e.memset(Dc[:], 1.0), [], ["Dc"])

                ck("fold")
                def load_norm(buf, key, src_rows, nrows, rkeys=()):
                    dma(buf[0:nrows, :], src_rows, reads=list(rkeys), writes=[key])
                    act(lambda e: e.activation(out=hb[:], in_=buf[:], func=AF.Square, scale=1.0 / 32.0, accum_out=st1[:, 0:1]),
                        [key], ["hb", "st1"])
                    rstd_from(st1[:, 1:2], st1[:, 0:1], 1.0, 1e-6, ["st1"], ["st1b"])
                    vec(lambda e: e.tensor_scalar(out=hb[:], in0=buf[:], scalar1=st1[:, 1:2], scalar2=None, op0=ALU.mult), [key, "st1b"], ["hb"])

                def transposes():
                    for k in range(8):
                        tr(PSb(0)[:, k * 128:(k + 1) * 128], hb[:, k * 128:(k + 1) * 128], identb[:], ["hb", "identb"], pk(0))

                def transposes_evac():
                    transposes()
                    pool(lambda e: e.tensor_copy(out=hT[:, :, 0:1], in_=hT[:, :, 128:129]), ["hT"], ["hT"])
                    vec(lambda e: e.tensor_copy(out=hT[:, :, 1:129], in_=PSb(0).rearrange("p (k t) -> p k t", t=128)), pk(0), ["hT"])

                def tile_src(tt):
                    if l == 0:
                        return x_sh[1 + tt * 128:1 + (tt + 1) * 128, :], ()
                    return x1s[tt * 128:(tt + 1) * 128, :], ["x1s%d" % tt]

                if l == 0:
                    vec(lambda e: e.memset(xts[0][:], 0.0), [], ["xt0"])
                    load_norm(xts[0], "xt0", x_sh[0:1, :], 1)
                    transposes()
                    vec(lambda e: e.tensor_copy(out=hT[:, :, 128:129], in_=PSb(0).rearrange("p (k t) -> p k t", t=128)[:, :, 0:1]), pk(0), ["hT"])
                else:
                    vec(lambda e: e.memset(hT[:, :, 128:129], 0.0), [], ["hT"])

                ck("halo")
                GHC = [(g, hp, cc) for same in (True, False) for g in range(4) for hp in range(2) for cc in range(2) if (hp == cc) == same]
                HC = [(h, cc) for same in (True, False) for h in range(4) for cc in range(2) if ((h % 2) == cc) == same]
                src0, rk0 = tile_src(0)
                load_norm(xts[0], "xt0", src0, 128, rk0)
                transposes_evac()
                def back_a(t, xt, kx):
                    ys, sq = F(6), F(7)
                    act(lambda e: e.copy(out=sq, in_=PS[2][:]), pk(2), ["F7"])
                    vec(lambda e: e.tensor_tensor(out=ys, in0=PS[3][:], in1=sq, op=ALU.add), pk(3) + ["F7"], ["F6"])
                    ys3 = ys.rearrange("p (h v) -> p h v", v=64)
                    sq3 = sq.rearrange("p (h v) -> p h v", v=64)
                    pool(lambda e: e.tensor_tensor(out=sq, in0=ys, in1=ys, op=ALU.mult), ["F6"], ["F7"])
                    s1, s2, mean, msq, var, rstdg = gn[:, 0:8], gn[:, 8:16], gn[:, 16:24], gn[:, 24:32], gn[:, 32:40], gn[:, 8:16]
                    vec(lambda e: e.reduce_sum(out=s1, in_=ys3, axis=AX.X), ["F6"], ["gn_s1"])
                    vec(lambda e: e.reduce_sum(out=s2, in_=sq3, axis=AX.X), ["F7"], ["gn_s2"])
                    vec(lambda e: e.tensor_scalar(out=mean, in0=s1, scalar1=1.0 / 64, scalar2=None, op0=ALU.mult), ["gn_s1"], ["gn_mean"])
                    vec(lambda e: e.tensor_tensor(out=msq, in0=mean, in1=mean, op=ALU.mult), ["gn_mean"], ["gn_msq"])
                    vec(lambda e: e.scalar_tensor_tensor(out=var, in0=s2, scalar=1.0 / 64, in1=msq, op0=ALU.mult, op1=ALU.subtract), ["gn_s2", "gn_msq"], ["gn_var"])
                    rstd_from(rstdg, var, 1.0, 64e-5, ["gn_var"], ["gn_s2"])
                    bc8 = lambda ap: ap.unsqueeze(2).to_broadcast([128, 8, 64])
                    vec(lambda e: e.tensor_tensor(out=ys3, in0=ys3, in1=bc8(mean), op=ALU.subtract), ["F6", "gn_mean"], ["F6"])
                    vec(lambda e: e.tensor_tensor(out=ys3, in0=ys3, in1=bc8(rstdg), op=ALU.mult), ["F6", "gn_s2"], ["F6"])
                    pool(lambda e: e.tensor_tensor(out=ys, in0=ys, in1=lnw_bc[:], op=ALU.mult), ["F6", "lnw"], ["F6"])
                    pool(lambda e: e.tensor_tensor(out=ys, in0=ys, in1=lnb_bc[:], op=ALU.add), ["F6", "lnb"], ["F6"])
                    vec(lambda e: e.tensor_tensor(out=sq3, in0=Vtok[:].rearrange("p g (h v) -> p (g h) v", v=64), in1=bc8(st1[:, 8:16]), op=ALU.mult),
                        ["Vtok", "bcoef"], ["F7"])
                    vec(lambda e: e.tensor_tensor(out=ys, in0=ys, in1=sq, op=ALU.add), ["F6", "F7"], ["F6"])
                    vec(lambda e: e.tensor_tensor(out=merged[:, 512:1024], in0=ys, in1=rsil[:], op=ALU.mult), ["F6", "rsil"], ["merged_r"])


                def back_b(t, xt, kx):
                    ck("gla")
                    for k in range(8):
                        tr(PSb(2)[:, k * 128:(k + 1) * 128], merged[:, k * 128:(k + 1) * 128], identb[:],
                           ["merged_g" if k < 4 else "merged_r", "identb"], pk(2))
                    vec(lambda e: e.tensor_copy(out=mT[:].rearrange("p a b -> p (a b)"), in_=PSb(2)), pk(2), ["mT"])
                    for half in range(2):
                        bank = 4 + half
                        for k in range(8):
                            mm(PS[bank][:], mT[:, k, :], wob[:, k, half * 512:(half + 1) * 512], ["mT", "wob"], pk(bank), start=(k == 0), stop=(k == 7), fifo=True)
                        vec(lambda e, half=half, bank=bank, xt=xt: e.tensor_tensor(out=xt[:, half * 512:(half + 1) * 512], in0=PS[bank][:],
                                                                            in1=xt[:, half * 512:(half + 1) * 512], op=ALU.add), pk(bank) + [kx], [kx])
                    if not fused:
                        dma(x_out[t * 128:(t + 1) * 128, :], xt[:], reads=[kx])
                    elif l == 0:
                        dma(x1s[t * 128:(t + 1) * 128, :], xt[:], reads=[kx], writes=["x1s%d" % t])
                    if (not fused) or l == layers[-1]:
                        act(lambda e, xt=xt: e.activation(out=on[:], in_=xt[:], func=AF.Square, scale=1.0 / 32.0, accum_out=st1[:, 6:7]), [kx], ["on", "fst"])
                        rstd_from(st1[:, 7:8], st1[:, 6:7], 1.0, 1e-6, ["fst"], ["fst2"])
                        vec(lambda e, xt=xt: e.scalar_tensor_tensor(out=on[:], in0=xt[:], scalar=st1[:, 7:8], in1=fnw_bc[:], op0=ALU.mult, op1=ALU.mult),
                            [kx, "fst2", "fnw"], ["on"])
                        dma(y_out[t * 128:(t + 1) * 128, :], on[:], reads=["on"])

                pending = None
                for t in range(NT):
                    last = (t == NT - 1)
                    xt = xts[t % 3]
                    kx = "xt%d" % (t % 3)
                    if pending is not None:
                        back_a(*pending)
                    if not last:
                        srcn, rkn = tile_src(t + 1)
                        load_norm(xts[(t + 1) % 3], "xt%d" % ((t + 1) % 3), srcn, 128, rkn)
                    cur = lambda k: hT[:, k, 1:129]
                    prv = lambda k: hT[:, k, 0:128]

                    def proj_fm_shift(dst_ps, col0, ncols, pskey):
                        for k in range(8):
                            mm(dst_ps, w1b[:, k, GLA_IN + col0:GLA_IN + col0 + ncols], cur(k), ["w1b", "hT"], pskey, start=(k == 0), stop=False, fifo=True)
                            mm(dst_ps, w2b[:, k, col0:col0 + ncols], prv(k), ["w2b", "hT"], pskey, start=False, stop=(k == 7), fifo=True)

                    def proj_fm(dst_ps, col0, ncols, pskey):
                        for k in range(8):
                            mm(dst_ps, w1b[:, k, col0:col0 + ncols], cur(k), ["w1b", "hT"], pskey, start=(k == 0), stop=(k == 7), fifo=True)

                    def proj_tm(dst_ps, col0, pskey):
                        for k in range(8):
                            mm(dst_ps, cur(k), w1b[:, k, col0:col0 + 512], ["w1b", "hT"], pskey, start=(k == 0), stop=(k == 7), fifo=True)

                    rT, rkT, rvT, sw, aT, cs_, Ea, Eb, kk_, kT2, bT, tmp = [F(i) for i in range(12)]
                    for g in range(4):
                        proj_fm_shift(PS[1][:, g * 128:(g + 1) * 128], 0 * 512 + g * 128, 128, pk(1))
                    act(lambda e: e.copy(out=rT, in_=PS[1][:]), pk(1), ["F0"])
                    for g in range(4):
                        proj_fm_shift(PS[2][:, g * 128:(g + 1) * 128], 1 * 512 + g * 128, 128, pk(2))
                    act(lambda e: e.copy(out=rkT, in_=PS[2][:]), pk(2), ["F1"])
                    for g in range(4):
                        gs = slice(g * 128, (g + 1) * 128)
                        vec(lambda e, g=g, gs=gs: e.tensor_scalar(out=kk_[:, gs], in0=rkT[:, gs], scalar1=KK(g), scalar2=None, op0=ALU.mult), ["F1", "pc"], ["F8"])
                    pool(lambda e: e.tensor_tensor(out=rvb[:].rearrange("p a b -> p (a b)"), in0=kk_, in1=kk_, op=ALU.mult), ["F8"], ["rvb"])
                    for g in range(4):
                        proj_fm_shift(PS[3][:, g * 128:(g + 1) * 128], 2 * 512 + g * 128, 128, pk(3))
                    act(lambda e: e.copy(out=rvT, in_=PS[3][:]), pk(3), ["F2"])
                    proj_fm_shift(PS[4][:, 0:128], 1536, 128, pk(4))
                    proj_fm(PS[4][0:16, 128:256], 1024, 16, pk(4))
                    proj_fm(PS[4][:, 256:384], 0, 128, pk(4))
                    proj_fm(PS[4][:, 384:512], 128, 128, pk(4))
                    act(lambda e: e.activation(out=wlalb[0:64, :], in_=PS[4][0:64, 0:128], func=AF.Tanh), pk(4), ["wlalb"])
                    act(lambda e: e.copy(out=wlalb[64:128, :], in_=PS[4][64:128, 0:128]), pk(4), ["wlalb"])
                    act(lambda e: e.copy(out=glrT[:], in_=PS[4][0:16, 128:256]), pk(4), ["glrT"])
                    act(lambda e: e.copy(out=qT[:].rearrange("p a b -> p (a b)"), in_=PS[4][:, 256:512]), pk(4), ["qT"])
                    proj_fm(PS[5][:, 0:128], 256, 128, pk(5))
                    proj_fm(PS[5][:, 128:256], 384, 128, pk(5))
                    act(lambda e: e.copy(out=kT[:].rearrange("p a b -> p (a b)"), in_=PS[5][:, 0:256]), pk(5), ["kT"])
                    for g in range(4):
                        mm(PS[1][:, g * 128:(g + 1) * 128], lrwb[0:64, g * 128:(g + 1) * 128], wlalb[0:64, :], ["lrwb", "wlalb"], pk(1), tp=(0, 0))
                    for g in range(4):
                        mm(PS[2][:, g * 128:(g + 1) * 128], lrwb[64:128, g * 128:(g + 1) * 128], wlalb[64:128, :], ["lrwb", "wlalb"], pk(2), tp=(64, 0))
                    for g in range(4):
                        gs = slice(g * 128, (g + 1) * 128)
                        act(lambda e, g=g, gs=gs: e.activation(out=sw[:, gs], in_=PS[1][:, gs], func=AF.Sigmoid, bias=W0(g)), pk(1) + ["pc"], ["F3"])
                    vec(lambda e: e.tensor_tensor_scan(out=cs_, data0=scanmask, data1=sw, initial=0.0, op0=ALU.mult, op1=ALU.add), ["cst", "F3"], ["F5"])
                    pool(lambda e: e.tensor_tensor(out=F(12), in0=cs_, in1=sw, op=ALU.subtract), ["F5", "F3"], ["F12"])
                    vec(lambda e: e.tensor_tensor(out=tmp.rearrange("p (a b) -> p a b", b=64),
                                                  in0=cs_.rearrange("p (a b) -> p a b", b=64)[:, :, 63:64].to_broadcast([128, 8, 64]),
                                                  in1=cs_.rearrange("p (a b) -> p a b", b=64), op=ALU.subtract), ["F5"], ["F11"])
                    for g in range(4):
                        gs = slice(g * 128, (g + 1) * 128)
                        act(lambda e, g=g, gs=gs: e.activation(out=aT[:, gs], in_=PS[2][:, gs], func=AF.Sigmoid, bias=A0(g)), pk(2) + ["pc"], ["F4"])
                    proj_tm(PS[6][:], 512, pk(6))
                    vec(lambda e: e.tensor_copy(out=gvb[:], in_=PS[6][:]), pk(6), ["gvb"])
                    for g in range(4):
                        gs = slice(g * 128, (g + 1) * 128)
                        mm(PS[3][:, gs], bonesb[:], rvb[:, g, :], ["bonesb", "rvb"], pk(3))
                    rstd_from(bT, PS[3][:], 1.0, 1e-20, pk(3), ["F10"])
                    vec(lambda e: e.tensor_tensor(out=kk_, in0=kk_, in1=bT, op=ALU.mult), ["F8", "F10"], ["F8"])
                    for g in range(4):
                        gs = slice(g * 128, (g + 1) * 128)
                        vec(lambda e, g=g, gs=gs: e.tensor_scalar(out=kT2[:, gs], in0=aT[:, gs], scalar1=KA(g), scalar2=omka[:, g:g + 1], op0=ALU.mult, op1=ALU.add),
                            ["F4", "pc", "omka"], ["F9"])
                    vec(lambda e: e.tensor_tensor(out=kT2, in0=kT2, in1=rkT, op=ALU.mult), ["F9", "F1"], ["F9"])
                    pool(lambda e: e.tensor_tensor(out=bT, in0=kk_, in1=aT, op=ALU.mult), ["F8", "F4"], ["F10"])
                    proj_tm(PS[7][:], 1040, pk(7))
                    act(lambda e: e.activation(out=gsil[:], in_=PS[7][:], func=AF.Silu), pk(7), ["gsil"])
                    proj_tm(PS[0][:], GLA_IN + SHIFT_W, pk(0))
                    act(lambda e: e.activation(out=rsil[:], in_=PS[0][:], func=AF.Silu), pk(0), ["rsil"])
                    cs3 = cs_.rearrange("p (a b) -> p a b", b=64)
                    arb_a = arb[:, :, :, 0, :].rearrange("p g c i -> p (g c) i")
                    arb_r = arb[:, :, :, 1, :].rearrange("p g c i -> p (g c) i")
                    v3 = lambda ap: ap.rearrange("p (a b) -> p a b", b=64)
                    b3 = lambda ap: ap[:].rearrange("p g (c i) -> p (g c) i", i=64)
                    Dpv = F(12)
                    act(lambda e: e.activation(out=Dpv, in_=Dpv, func=AF.Exp, scale=-KDEC), ["F12"], ["F12"])
                    act(lambda e: e.activation(out=tmp, in_=tmp, func=AF.Exp, scale=-KDEC), ["F11"], ["F11"])
                    act(lambda e: e.activation(out=Ea, in_=cs_, func=AF.Exp, scale=-KDEC), ["F5"], ["F6"])
                    act(lambda e: e.activation(out=Eb, in_=cs_, func=AF.Exp, scale=KDEC), ["F5"], ["F7"])
                    vec(lambda e: e.scalar_tensor_tensor(out=atb[:].rearrange("p a b -> p (a b)"), in0=kk_, scalar=-1.0, in1=Dpv, op0=ALU.mult, op1=ALU.mult),
                        ["F8", "F12"], ["atb"])
                    pool(lambda e: e.tensor_tensor(out=Khb[:].rearrange("p a b -> p (a b)"), in0=kT2, in1=tmp, op=ALU.mult), ["F9", "F11"], ["Khb"])
                    vec(lambda e: e.tensor_tensor(out=Bhb[:].rearrange("p a b -> p (a b)"), in0=bT, in1=tmp, op=ALU.mult), ["F10", "F11"], ["Bhb"])
                    pool(lambda e: e.tensor_copy(out=arb_a, in_=b3(atb)), ["atb"], ["arb_a"])
                    vec(lambda e: e.tensor_tensor(out=arb_r, in0=v3(rT), in1=v3(Ea), op=ALU.mult), ["F0", "F6"], ["arb_r"])
                    pool(lambda e: e.tensor_tensor(out=ktb[:].rearrange("p a b -> p (a b)"), in0=kT2, in1=Eb, op=ALU.mult), ["F9", "F7"], ["ktb"])
                    vec(lambda e: e.tensor_tensor(out=btb[:].rearrange("p a b -> p (a b)"), in0=bT, in1=Eb, op=ALU.mult), ["F10", "F7"], ["btb"])
                    pool(lambda e: e.tensor_copy(out=gamc[:].rearrange("p g c -> p (g c)"), in_=v3(Ea)[:, :, 63]), ["F6"], ["gamc"])

                    if not last:
                        transposes_evac()
                    if pending is not None:
                        back_b(*pending)
                        pending = None

                    ck("proj")
                    v2 = lambda ap: ap.rearrange("p (a b) -> p a b", b=128)
                    spg, csg = v2(F(1)[:, 0:256]), v2(F(1)[:, 256:512])
                    Eg = [v2(F(4)[:, 0:256]), v2(F(4)[:, 256:512]), v2(F(3)[:, 0:256])]
                    f2 = lambda t_: t_.rearrange("p a b -> p (a b)")
                    for grp in range(2):
                        mm(PS[5][:, grp * 128:(grp + 1) * 128], gup[:, grp * 128:(grp + 1) * 128], glrT[:], ["gup", "glrT"], pk(5))
                    for grp in range(2):
                        act(lambda e, grp=grp: e.activation(out=spg[:, grp, :], in_=PS[5][:, grp * 128:(grp + 1) * 128], func=AF.Exp, scale=-1.0, bias=negb[:, grp:grp + 1]),
                            pk(5) + ["negb"], ["F1"])
                    act(lambda e: e.activation(out=f2(spg), in_=f2(spg), func=AF.Ln, bias=1.0), ["F1"], ["F1"])
                    vec(lambda e: e.tensor_tensor_scan(out=f2(csg), data0=scanmask[:, 0:256], data1=f2(spg), initial=0.0, op0=ALU.mult, op1=ALU.add), ["cst", "F1"], ["F1"])
                    act(lambda e: e.activation(out=f2(Eg[0]), in_=f2(csg), func=AF.Exp, scale=-1.0 / 16), ["F1"], ["F4"])
                    act(lambda e: e.activation(out=f2(Eg[1]), in_=f2(csg), func=AF.Exp, scale=1.0 / 16), ["F1"], ["F4"])
                    csg3 = f2(csg).rearrange("p (a b) -> p a b", b=64)
                    pool(lambda e: e.tensor_tensor(out=f2(Eg[2]).rearrange("p (a b) -> p a b", b=64), in0=csg3[:, :, 63:64].to_broadcast([128, 4, 64]), in1=csg3, op=ALU.subtract),
                         ["F1"], ["F3"])
                    act(lambda e: e.activation(out=f2(Eg[2]), in_=f2(Eg[2]), func=AF.Exp, scale=-1.0 / 16), ["F3"], ["F3"])
                    vec(lambda e: e.scalar_tensor_tensor(out=f2(qib[:]), in0=f2(qT[:]), scalar=0.125, in1=f2(Eg[0]), op0=ALU.mult, op1=ALU.mult), ["qT", "F4"], ["qib"])
                    vec(lambda e: e.tensor_tensor(out=f2(kib[:]), in0=f2(kT[:]), in1=f2(Eg[1]), op=ALU.mult), ["kT", "F4"], ["kib"])
                    pool(lambda e: e.tensor_tensor(out=f2(ksb[:]), in0=f2(kT[:]), in1=f2(Eg[2]), op=ALU.mult), ["kT", "F3"], ["ksb"])
                    ck("elem")
                    for (src, dst, key, dkey, bank) in [(atb, atok, "atb", "atok", 1), (Bhb, Bhtok, "Bhb", "Bhtok", 2), (Khb, Khtok, "Khb", "Khtok", 4)]:
                        for g in range(4):
                            tr(PSb(bank)[:, g * 128:(g + 1) * 128], src[:, g, :], identb[:], [key, "identb"], pk(bank))
                        act(lambda e, dst=dst, bank=bank: e.copy(out=dst[:].rearrange("p a b -> p (a b)"), in_=PSb(bank)[:, 0:512]),
                            pk(bank), [dkey])
                    pool(lambda e: e.tensor_copy(out=rvb[:].rearrange("p a b -> p (a b)"), in_=rvT), ["F2"], ["rvb"])
                    for g in range(4):
                        tr(PSb(5)[:, g * 128:(g + 1) * 128], rvb[:, g, :], identb[:], ["rvb", "identb"], pk(5))
                    act(lambda e: e.copy(out=Vtok[:].rearrange("p a b -> p (a b)"), in_=PSb(5)[:, 0:512]), pk(5), ["Vtok"])

                    pool(lambda e: e.tensor_tensor(out=rvb[:].rearrange("p a b -> p (a b)"), in0=rT, in1=kT2, op=ALU.mult), ["F0", "F9"], ["rvb"])
                    for g in range(4):
                        mm(PS[5][:, g * 2:(g + 1) * 2], rvb[:, g, :], rkselb[:, g * 2:(g + 1) * 2], ["rvb", "rkselb"], pk(5))
                    act(lambda e: e.copy(out=st1[:, 8:16], in_=PS[5][:, 0:8]), pk(5), ["bcoef"])
                    ck("trans")
                    for (g, hp, cc) in GHC:
                        gh = g * 2 + hp
                        p0, c0 = hp * 64, cc * 64
                        tk = slice(cc * 64, (cc + 1) * 64)
                        rhs_ar = arb[p0:p0 + 64, g, cc, :, :].rearrange("p a i -> p (a i)")
                        bN, bK, cg = (6 if gh < 4 else 7), (0 if gh < 4 else 1), (gh % 4) * 128
                        mm(PS[bN][c0:c0 + 64, cg:cg + 128], btb[p0:p0 + 64, g, tk], rhs_ar, ["btb", "arb_a", "arb_r"], pk(bN), tp=(p0, c0))
                        mm(PS[bK][c0:c0 + 64, cg:cg + 128], ktb[p0:p0 + 64, g, tk], rhs_ar, ["ktb", "arb_a", "arb_r"], pk(bK), tp=(p0, c0))
                        mm(PS[2][c0:c0 + 64, gh * 64:gh * 64 + 64], atb[p0:p0 + 64, g, tk], btb[p0:p0 + 64, g, tk], ["atb", "btb"], pk(2), tp=(p0, c0))
                    mNA4 = maskNA.unsqueeze(1).to_broadcast([128, 4, 128])
                    mSL8 = maskSL.unsqueeze(1).to_broadcast([128, 8, 64])
                    p4 = lambda ap: ap.rearrange("p (a b) -> p a b", b=128)
                    vec(lambda e: e.tensor_tensor(out=NAb[:, 0:4, :], in0=p4(PS[6][:]), in1=mNA4, op=ALU.mult), pk(6) + ["cst"], ["NAb"])
                    vec(lambda e: e.tensor_tensor(out=NAb[:, 4:8, :], in0=p4(PS[7][:]), in1=mNA4, op=ALU.mult), pk(7) + ["cst"], ["NAb"])
                    vec(lambda e: e.tensor_tensor(out=KAb[:, 0:4, :], in0=p4(PS[0][:]), in1=mNA4, op=ALU.mult), pk(0) + ["cst"], ["KAb"])
                    vec(lambda e: e.tensor_tensor(out=KAb[:, 4:8, :], in0=p4(PS[1][:]), in1=mNA4, op=ALU.mult), pk(1) + ["cst"], ["KAb"])
                    vec(lambda e: e.tensor_tensor(out=Lb[0][:], in0=PS[2][:].rearrange("p (a b) -> p a b", b=64), in1=mSL8, op=ALU.mult), pk(2) + ["cst"], ["Lb0h0", "Lb0h1"])

                    for grp in range(2):
                        tr(PSb(6)[:, grp * 128:(grp + 1) * 128], ksb[:, grp, :], identb[:], ["ksb", "identb"], pk(6))
                    act(lambda e: e.copy(out=f2(kstok[:]), in_=PSb(6)[:, 0:256]), pk(6), ["kstok"])
                    for (h, cc) in HC:
                        grp, hp = h // 2, h % 2
                        p0 = hp * 64
                        c0 = cc * 64
                        tk = slice(c0, c0 + 64)
                        mm(PS[7][c0:c0 + 64, h * 64:(h + 1) * 64], kib[p0:p0 + 64, grp, tk], qib[p0:p0 + 64, grp, tk], ["kib", "qib"], pk(7), tp=(p0, c0))
                    vec(lambda e: e.tensor_tensor(out=scb[:], in0=PS[7][:, 0:256].rearrange("p (a b) -> p a b", b=64),
                                                  in1=maskUU.unsqueeze(1).to_broadcast([128, 4, 64]), op=ALU.mult), pk(7) + ["cst"], ["scb"])
                    ck("score")
                    for g in range(4):
                        for hp in range(2):
                            gh = g * 2 + hp
                            for cc in range(2):
                                c0 = cc * 64
                                mm(PS[3][c0:c0 + 64, gh * 64:gh * 64 + 64], KAb[c0:c0 + 64, gh, 0:64], Vtok[c0:c0 + 64, g, hp * 64:hp * 64 + 64],
                                   ["KAb", "Vtok"], pk(3), tp=(c0, c0))
                    pool(lambda e: e.tensor_copy(out=Xb[:, :, 0:64], in_=atok[:].rearrange("p g (h k) -> p (g h) k", k=64)), ["atok"], ["Xb0", "Xb1"])
                    act(lambda e: e.copy(out=Xb[:, :, 64:128], in_=PS[3][:].rearrange("p (a b) -> p a b", b=64)), pk(3), ["Xb0", "Xb1"])

                    ck("x0")
                    for lvl in range(6):
                        a_, b_ = lvl % 2, (lvl + 1) % 2
                        for hf in range(2):
                            ghs = range(hf * 4, hf * 4 + 4)
                            kX = "Xb%d" % hf
                            kN = "NAb" if lvl == 0 else "Nb%dh%d" % (a_, hf)
                            kL = "Lb%dh%d" % (a_, hf)
                            bank = 4 + hf
                            for gh in ghs:
                                for cc in range(2):
                                    c0 = cc * 64
                                    lhs = NAb[c0:c0 + 64, gh, 0:64] if lvl == 0 else Nb[a_][c0:c0 + 64, gh, :]
                                    mm(PS[bank][c0:c0 + 64, (gh % 4) * 128:(gh % 4) * 128 + 128], lhs, Xb[c0:c0 + 64, gh, :], [kN, kX], pk(bank), tp=(c0, c0))
                            if lvl < 5:
                                for gh in ghs:
                                    for cc in range(2):
                                        c0 = cc * 64
                                        lhsN = NAb[c0:c0 + 64, gh, 0:64] if lvl == 0 else Nb[a_][c0:c0 + 64, gh, :]
                                        Lp = Lb[a_][c0:c0 + 64, gh, :]
                                        gq = (gh % 4) * 64
                                        mm(PS[6 + hf][c0:c0 + 64, gq:gq + 64], Lp, lhsN, [kL, kN], pk(6 + hf), tp=(c0, c0))
                                        if lvl < 4:
                                            mm(PS[hf][c0:c0 + 64, gq:gq + 64], lhsN, Lp, [kL, kN], pk(hf), tp=(c0, c0))
                            hs = slice(hf * 4, hf * 4 + 4)
                            cs256 = slice(hf * 256, hf * 256 + 256)
                            vec(lambda e, hs=hs, bank=bank: e.tensor_tensor(out=Xb[:, hs, :], in0=p4(PS[bank][:]), in1=Xb[:, hs, :], op=ALU.add), pk(bank) + [kX], [kX])
                            if lvl < 5:
                                act(lambda e, b_=b_, hs=hs, hf=hf: e.copy(out=Nb[b_][:, hs, :].rearrange("p a b -> p (a b)"), in_=PS[6 + hf][:, 0:256]),
                                    pk(6 + hf), ["Nb%dh%d" % (b_, hf)])
                                if lvl < 4:
                                    act(lambda e, b_=b_, hs=hs, hf=hf: e.copy(out=Lb[b_][:, hs, :].rearrange("p a b -> p (a b)"), in_=PS[hf][:, 0:256]),
                                        pk(hf), ["Lb%dh%d" % (b_, hf)])

                    ck("neumann")
                    QTs = sw
                    for (g, hp, cc) in GHC:
                        gh = g * 2 + hp
                        if True:
                            if True:
                                p0, c0 = hp * 64, cc * 64
                                Wt = Xb[c0:c0 + 64, gh, 0:64]
                                Ut = Xb[c0:c0 + 64, gh, 64:128]
                                col = (g * 2 + cc) * 64
                                mm(PS[0][p0:p0 + 64, col:col + 64], Wt, Bhtok[c0:c0 + 64, g, p0:p0 + 64], ["Xb0", "Xb1", "Bhtok"], pk(0), tp=(c0, p0))
                                mm(PS[1][p0:p0 + 64, col:col + 64], Bhtok[c0:c0 + 64, g, p0:p0 + 64], Ut, ["Xb0", "Xb1", "Bhtok"], pk(1), start=True, stop=False, tp=(c0, p0))
                                mm(PS[1][p0:p0 + 64, col:col + 64], Khtok[c0:c0 + 64, g, p0:p0 + 64], Vtok[c0:c0 + 64, g, p0:p0 + 64], ["Khtok", "Vtok"], pk(1),
                                   start=False, stop=True, tp=(c0, p0))
                                mm(PS[2][p0:p0 + 64, col:col + 64], Wt, NAb[c0:c0 + 64, gh, 64:128], ["Xb0", "Xb1", "NAb"], pk(2), tp=(c0, p0))
                    pool(lambda e: e.tensor_tensor(out=dG[:].rearrange("p g c k -> p (g c) k"),
                                                   in0=identblk.unsqueeze(1).to_broadcast([128, 8, 64]),
                                                   in1=gamc[:].rearrange("p g c -> p (g c)").unsqueeze(2).to_broadcast([128, 8, 64]), op=ALU.mult),
                         ["cst", "gamc"], ["dG"])
                    vec(lambda e: e.tensor_tensor(out=Pb[:].rearrange("p g c k -> p (g c k)"), in0=PS[0][:], in1=dG[:].rearrange("p g c k -> p (g c k)"), op=ALU.add),
                        pk(0) + ["dG"], ["Pb"])
                    act(lambda e: e.copy(out=QTs, in_=PS[1][:]), pk(1), ["F3"])
                    vec(lambda e: e.tensor_tensor(out=GTb[:].rearrange("p g (c i) -> p (g c) i", i=64), in0=PS[2][:].rearrange("p (a b) -> p a b", b=64),
                                                  in1=arb_r, op=ALU.add), pk(2) + ["arb_r"], ["GTb"])

                    ck("pqg")
                    for cc in range(2):
                        c0 = cc * 64
                        kY = "ps3c%d" % cc
                        for g in range(4):
                            for hp in range(2):
                                gh = g * 2 + hp
                                p0 = hp * 64
                                yo = PS[3][c0:c0 + 64, gh * 64:gh * 64 + 64]
                                mm(yo, NAb[c0:c0 + 64, gh, 64:128], Xb[c0:c0 + 64, gh, 64:128], ["NAb", "Xb0", "Xb1"], [kY], start=True, stop=False, tp=(c0, c0))
                                mm(yo, KAb[c0:c0 + 64, gh, 64:128], Vtok[c0:c0 + 64, g, p0:p0 + 64], ["KAb", "Vtok"], [kY], start=False, stop=True, tp=(c0, c0))
                    for cc in range(2):
                        c0 = cc * 64
                        for hp in (cc, 1 - cc):
                            for g in range(4):
                                gh = g * 2 + hp
                                p0 = hp * 64
                                mm(PS[2][c0:c0 + 64, gh * 64:gh * 64 + 64], GTb[p0:p0 + 64, g, c0:c0 + 64], Ab[p0:p0 + 64, g, 0:64], ["GTb", "Ab"], pk(2), tp=(p0, c0))
                        for g in range(4):
                            for hp in range(2):
                                p0 = hp * 64
                                mm(PS[4][p0:p0 + 64, g * 128:(g + 1) * 128], Pb[p0:p0 + 64, g, cc, :], Ab[p0:p0 + 64, g, :], ["Pb", "Ab"], pk(4), tp=(p0, p0))
                        q4 = QTs.rearrange("p (g c v) -> p g c v", c=2, v=64)
                        vec(lambda e, cc=cc, q4=q4: e.tensor_tensor(out=A32[:, :, 0:64], in0=p4(PS[4][:])[:, :, 0:64], in1=q4[:, :, cc, :], op=ALU.add),
                            pk(4) + ["F3"], ["A32"])
                        vec(lambda e: e.tensor_copy(out=A32[:, :, 64:128], in_=p4(PS[4][:])[:, :, 64:128]), pk(4), ["A32"])
                        pool(lambda e: e.tensor_copy(out=Ab[:], in_=A32[:]), ["A32"], ["Ab"])

                    ck("scan")
                    ck("repi")
                    for cc in range(2):
                        c0 = cc * 64
                        tk = slice(c0, c0 + 64)
                        kO = "ps0c%d" % cc
                        for h in range(4):
                            oo = PS[0][c0:c0 + 64, h * 128:(h + 1) * 128]
                            mm(oo, scb[c0:c0 + 64, h, :], gvb[c0:c0 + 64, h * 128:(h + 1) * 128], ["scb", "gvb"], [kO], tp=(c0, c0))
                        for h in (cc, cc + 2, 1 - cc, 3 - cc):
                            grp, hp = h // 2, h % 2
                            p0 = hp * 64
                            mm(PS[4][c0:c0 + 64, h * 128:(h + 1) * 128], qib[p0:p0 + 64, grp, tk], Sb[p0:p0 + 64, grp, :], ["qib", "Sb"], pk(4), tp=(p0, c0))
                        for h in (cc, cc + 2, 1 - cc, 3 - cc):
                            grp, hp = h // 2, h % 2
                            p0 = hp * 64
                            mm(PS[1][p0:p0 + 64, grp * 128:(grp + 1) * 128], kstok[c0:c0 + 64, grp, p0:p0 + 64], gvb[c0:c0 + 64, h * 128:(h + 1) * 128],
                               ["kstok", "gvb"], pk(1), tp=(c0, p0))
                        for grp in range(2):
                            dcol = Eg[0][:, grp, c0 + 63:c0 + 64]
                            vec(lambda e, grp=grp, dcol=dcol: e.scalar_tensor_tensor(out=S32[:, grp, :], in0=S32[:, grp, :], scalar=dcol,
                                                                                     in1=PS[1][:, grp * 128:(grp + 1) * 128], op0=ALU.mult, op1=ALU.add),
                                ["S32", "F4", "ps1"], ["S32"])
                        pool(lambda e: e.tensor_copy(out=Sb[:], in_=S32[:]), ["S32"], ["Sb"])
                        vec(lambda e, c0=c0: e.tensor_tensor(out=Dc[:], in0=Dc[:], in1=Eg[0][:, :, c0 + 63], op=ALU.mult), ["Dc", "F4"], ["Dc"])
                    os_, osq = kk_, kT2
                    act(lambda e: e.copy(out=osq, in_=PS[4][:]), pk(4), ["F9"])
                    vec(lambda e: e.tensor_tensor(out=os_, in0=PS[0][:], in1=osq, op=ALU.add), pk(0) + ["F9"], ["F8"])
                    pool(lambda e: e.tensor_tensor(out=osq, in0=os_, in1=os_, op=ALU.mult), ["F8"], ["F9"])
                    vec(lambda e: e.reduce_sum(out=st1[:, 2:6], in_=osq.rearrange("p (h v) -> p h v", v=128), axis=AX.X), ["F9"], ["gst"])
                    rstd_from(st1[:, 2:6], st1[:, 2:6], 1.0 / 128, 1e-6, ["gst"], ["gst"])
                    os3 = os_.rearrange("p (h v) -> p h v", v=128)
                    vec(lambda e: e.tensor_tensor(out=os3, in0=os3, in1=st1[:, 2:6].unsqueeze(2).to_broadcast([128, 4, 128]), op=ALU.mult), ["F8", "gst"], ["F8"])
                    pool(lambda e: e.tensor_tensor(out=os3, in0=os3, in1=gnw_bc[:].unsqueeze(1).to_broadcast([128, 4, 128]), op=ALU.mult), ["F8", "gnw"], ["F8"])
                    vec(lambda e: e.tensor_tensor(out=merged[:, 0:512], in0=os_, in1=gsil[:], op=ALU.mult), ["F8", "gsil"], ["merged_g"])

                    pending = (t, xt, kx)
                if pending is not None:
                    back_a(*pending)
                    back_b(*pending)
                    pending = None

        except _Stop:
            pass
        if not fused:
            dma(oA, A32[:].rearrange("p a b -> p (a b)"), reads=["A32"])
            dma(oS, S32[:].rearrange("p a b -> p (a b)"), reads=["S32"])
            dma(oD, Dc[:], reads=["Dc"])
        S.finish()
        S.emit()
    return nc, S


def _consts():
    c = np.zeros((128, 1024), np.float32)
    c[:, 0:128] = np.eye(128, dtype=np.float32)
    e64 = np.eye(64, dtype=np.float32)
    c[:, 128:192] = np.concatenate([e64, e64], 0)
    su = np.triu(np.ones((64, 64), np.float32), 1)
    uu = np.triu(np.ones((64, 64), np.float32), 0)
    c[:, 192:256] = np.concatenate([su, su], 0)
    c[:, 256:320] = np.concatenate([uu, uu], 0)
    c[:, 320:384] = np.concatenate([su.T, su.T], 0)
    bo = np.zeros((128, 128), np.float32)
    bo[0:64, 0:64] = 1.0
    bo[64:128, 64:128] = 1.0
    c[:, 384:512] = bo
    sm = np.ones((128, 512), np.float32)
    sm[:, ::64] = 0.0
    c[:, 512:1024] = sm
    return c


def _fm_cols(v512):
    return np.ascontiguousarray(v512.reshape(4, 128).T)


def _layer_params(inp, l):
    pcol = np.zeros((128, 32), np.float32)
    pcol[:, 0:4] = _fm_cols(inp["rwkv_w0"][l])
    pcol[:, 4:8] = _fm_cols(inp["rwkv_a0"][l])
    pcol[:, 8:12] = _fm_cols(inp["rwkv_k_k"][l])
    pcol[:, 12:16] = _fm_cols(inp["rwkv_k_a"][l])
    pcol[:, 16:18] = inp["gla_gate_bias"][l].reshape(2, 128).T
    pcol[:, 20:28] = inp["norm_w"][l].reshape(8, 128).T
    rk = inp["rwkv_r_k"][l].reshape(512)
    rksel = np.zeros((128, 4, 2), np.float32)
    for g in range(4):
        for hp in range(2):
            rksel[hp * 64:(hp + 1) * 64, g, hp] = rk[g * 128 + hp * 64:g * 128 + (hp + 1) * 64]
    lrw = np.concatenate([inp["rwkv_w_up"][l], inp["rwkv_a_up"][l]], 0)
    return {
        "w_in": np.ascontiguousarray(inp["w_in"][l]), "w_out": np.ascontiguousarray(inp["w_out"][l]),
        "pcol": pcol, "rksel": rksel.reshape(128, 8), "mu": np.ascontiguousarray(inp["rwkv_mu"][l]),
        "lnw": np.ascontiguousarray(inp["rwkv_ln_w"][l]), "lnb": np.ascontiguousarray(inp["rwkv_ln_b"][l]),
        "gnw": np.ascontiguousarray(inp["gla_norm_w"][l]), "fnw": np.ascontiguousarray(inp["final_norm_w"]),
        "lrw": np.ascontiguousarray(lrw), "gup": np.ascontiguousarray(inp["gla_gate_up"][l]), "cst": _consts(),
    }


def _zero_sums():
    return {"sumM": np.zeros((8, 128, 256), np.float32), "sumPhi": np.zeros((8, 128, 256), np.float32),
            "sumS": np.zeros((8, 128, 256), np.float32), "sumD": np.zeros((8, 128, 2), np.float32)}


def _sums_from_results(results):
    s = _zero_sums()
    for j, r in enumerate(results):
        A = r["oA"].reshape(128, 4, 128)
        s["sumM"][j] = A[:, :, 0:64].reshape(128, 256)
        Pc = A[:, :, 64:128].reshape(2, 64, 4, 64)
        s["sumPhi"][j] = Pc.transpose(0, 3, 2, 1).reshape(128, 256)
        s["sumS"][j] = r["oS"]
        s["sumD"][j] = r["oD"]
    return s


def _use_masks():
    u = np.zeros((NCORES, 128, 8), np.float32)
    for c in range(NCORES):
        b, q = divmod(c, 4)
        for j in range(NCORES):
            bj, qj = divmod(j, 4)
            if bj == b and qj < q:
                u[c, :, j] = 1.0
    return u


_NC_CACHE = {}


def _get_nc(NT, fused=False):
    if (NT, fused) not in _NC_CACHE:
        _NC_CACHE[(NT, fused)] = build_pass(NT, fused=fused)[0]
    return _NC_CACHE[(NT, fused)]


def _run_pass(xfull, lp, sums, use):
    in_maps = []
    for c in range(NCORES):
        b, q = divmod(c, 4)
        xs = np.zeros((TOK + 1, D), np.float32)
        xs[1:] = xfull[b, q * TOK:(q + 1) * TOK]
        if q > 0:
            xs[0] = xfull[b, q * TOK - 1]
        m = dict(lp)
        m.update(sums)
        m["x_sh"] = xs
        m["use"] = use[c]
        in_maps.append(m)
    res = run_bass_kernel_spmd(_get_nc(TOK // 128), in_maps, core_ids=list(range(NCORES)))
    return res.results


def _gather(results, key):
    out = np.zeros((2, 4 * TOK, D), np.float32)
    for c, r in enumerate(results):
        b, q = divmod(c, 4)
        out[b, q * TOK:(q + 1) * TOK] = r[key]
    return out


def kernel_unfused(**inputs):
    inp = {k: np.asarray(v, dtype=np.float32) for k, v in inputs.items()}
    use = _use_masks()
    x = inp["x"]
    res = None
    for l in range(2):
        lp = _layer_params(inp, l)
        r0 = _run_pass(x, lp, _zero_sums(), use)
        res = _run_pass(x, lp, _sums_from_results(r0), use)
        x = _gather(res, "x_out")
    return _gather(res, "y_out")


def _fused_params(inp):
    lps = [_layer_params(inp, l) for l in range(2)]
    m = {}
    for k in lps[0]:
        if k in ("fnw", "cst"):
            m[k] = lps[0][k]
        else:
            m[k] = np.ascontiguousarray(np.stack([lps[0][k], lps[1][k]], 0))
    return m


def kernel(**inputs):
    inp = {k: np.asarray(v, dtype=np.float32) for k, v in inputs.items()}
    T = inp["x"].shape[1]
    fp = _fused_params(inp)
    in_maps = []
    for c in range(NCORES):
        b, q = divmod(c, 4)
        xs = np.zeros((T + 1, D), np.float32)
        xs[1:] = inp["x"][b]
        m = dict(fp)
        m["x_sh"] = xs
        in_maps.append(m)
    res = run_bass_kernel_spmd(_get_nc(T // 128, True), in_maps, core_ids=list(range(NCORES)))
    out = np.zeros((2, T, D), np.float32)
    Q = T // 4
    for c, r in enumerate(res.results):
        b, q = divmod(c, 4)
        out[b, q * Q:(q + 1) * Q] = r["y_out"][q * Q:(q + 1) * Q]
    return out
```

```python
import contextlib
import numpy as np
import concourse.bass as bass
import concourse.mybir as mybir
from concourse.bass_utils import run_bass_kernel_spmd

F32 = mybir.dt.float32
BF16 = mybir.dt.bfloat16
AF = mybir.ActivationFunctionType
ALU = mybir.AluOpType
AX = mybir.AxisListType

D = 1024
NCORES = 8
TOK = 2048
GLA_IN = 1552
SHIFT_W = 1664
IN_W = 3728
KDEC = float(np.exp(-0.5))


class Sched:
    ENG = ("sync", "scalar", "vector", "gpsimd", "tensor")
    NDMA = 24

    def __init__(self, nc):
        self.nc = nc
        self.dma_streams = {"sync": ["dq_sync%d" % i for i in range(self.NDMA)], "gpsimd": ["dq_pool%d" % i for i in range(8)]}
        self.streams = list(self.ENG) + self.dma_streams["sync"] + self.dma_streams["gpsimd"]
        self.rrs = {"sync": 0, "gpsimd": 0}
        self.ops = {e: [] for e in self.ENG}
        self.tick = {s: 0 for s in self.streams}
        self.seen = {e: {s: 0 for s in self.streams} for e in self.ENG}
        self.lastw = {}
        self.readers = {}
        self.n = 0
        self.rr = 0
        self.pe_cur = None
        self._drain = -1

    def _need(self, eng, deps, pe_fifo=False):
        req = {}
        for (s, t) in deps:
            if pe_fifo and s == "tensor" and eng == "tensor" and t != self._drain:
                continue
            if t > self.seen[eng][s]:
                req[s] = max(req.get(s, 0), t)
        for s, t in req.items():
            self.seen[eng][s] = t
            self.ops[eng].append(("wait", s, t))

    def op(self, eng, fn, reads=(), writes=(), dma=False, pe_fifo=False, pe_class=None):
        deps = []
        if eng == "tensor":
            if pe_class is not None and pe_class != "O" and pe_class == self.pe_cur:
                pe_fifo = True
            else:
                pe_fifo = True
                self._drain = self.tick["tensor"]
                if self.tick["tensor"] > 0:
                    deps.append(("tensor", self.tick["tensor"]))
            self.pe_cur = pe_class
        for k in reads:
            if k in self.lastw:
                deps.append(self.lastw[k])
        for k in writes:
            if k in self.lastw:
                deps.append(self.lastw[k])
            deps.extend(self.readers.get(k, ()))
        if dma:
            pool_ = self.dma_streams[eng]
            stream = pool_[self.rrs[eng]]
            self.rrs[eng] = (self.rrs[eng] + 1) % len(pool_)
            if self.tick[stream] > 0:
                deps.append((stream, self.tick[stream]))
            inc = 16
        else:
            stream, inc = eng, 1
        self._need(eng, deps, pe_fifo)
        self.tick[stream] += inc
        t = self.tick[stream]
        self.ops[eng].append(("op", fn, stream, inc))
        self.n += 1
        for k in reads:
            self.readers.setdefault(k, []).append((stream, t))
        for k in writes:
            self.lastw[k] = (stream, t)
            self.readers[k] = []

    def finish(self):
        for eng in self.ENG:
            self._need(eng, [(s, self.tick[s]) for s in self.streams if self.tick[s] > 0])

    def emit(self):
        nc = self.nc
        waited = {e: set() for e in self.ENG}
        for e in self.ENG:
            for it in self.ops[e]:
                if it[0] == "wait" and it[1] in waited:
                    waited[it[1]].add(it[2])
        rank = {e: {t: i + 1 for i, t in enumerate(sorted(waited[e]))} for e in self.ENG}
        with contextlib.ExitStack() as st:
            sems = {s: st.enter_context(nc.semaphore("s_" + s)) for s in self.streams}
            block = st.enter_context(nc.Block())

            def runner(e):
                def f(engine):
                    cnt = 0
                    for it in self.ops[e]:
                        if it[0] == "wait":
                            v = rank[it[1]][it[2]] if it[1] in rank else it[2]
                            engine.wait_ge(sems[it[1]], v)
                        else:
                            ins = it[1](engine)
                            if it[2] in rank:
                                cnt += 1
                                if cnt in rank[it[2]]:
                                    ins.then_inc(sems[it[2]], 1)
                            else:
                                ins.then_inc(sems[it[2]], it[3])
                return f
            block.sync(runner("sync"))
            block.scalar(runner("scalar"))
            block.vector(runner("vector"))
            block.gpsimd(runner("gpsimd"))
            block.tensor(runner("tensor"))


class _Stop(Exception):
    pass


def build_pass(NT=16, dbg_names=(), stop_after=None, fused=False):
    nc = bass.Bass("TRN2", target_bir_lowering=False)
    T = NT * 128

    def din(name, shape, dt=F32):
        return nc.dram_tensor(name, list(shape), dt, kind="ExternalInput").ap()

    def dout(name, shape, dt=F32):
        return nc.dram_tensor(name, list(shape), dt, kind="ExternalOutput").ap()

    LD = [2] if fused else []
    x_sh = din("x_sh", [T + 1, D])
    w_in = din("w_in", LD + [D, IN_W])
    w_out = din("w_out", LD + [D, D])
    pcol = din("pcol", LD + [128, 32])
    rksel_d = din("rksel", LD + [128, 8])
    mu_d = din("mu", LD + [SHIFT_W])
    lnw_d = din("lnw", LD + [512])
    lnb_d = din("lnb", LD + [512])
    gnw_d = din("gnw", LD + [128])
    fnw_d = din("fnw", [D])
    lrw_d = din("lrw", LD + [128, 512])
    gup_d = din("gup", LD + [16, 256])
    cst_d = din("cst", [128, 1024])
    y_out = dout("y_out", [T, D])
    if fused:
        x1s = nc.dram_tensor("x1s", [T, D], F32).ap()
    else:
        sumM_d = din("sumM", [8, 128, 256])
        sumPhi_d = din("sumPhi", [8, 128, 256])
        sumS_d = din("sumS", [8, 128, 256])
        sumD_d = din("sumD", [8, 128, 2])
        use_d = din("use", [128, 8])
        x_out = dout("x_out", [T, D])
        oA = dout("oA", [128, 512])
        oS = dout("oS", [128, 256])
        oD = dout("oD", [128, 2])
    dbg_out = {}

    with contextlib.ExitStack() as st:
        def sb(name, shape, dt=F32):
            return st.enter_context(nc.sbuf_tensor(name, list(shape), dt))

        S = Sched(nc)
        w1b = sb("w1b", [128, 8, IN_W], BF16)
        w2b = sb("w2b", [128, 8, SHIFT_W], BF16)
        wob = sb("wob", [128, 8, D], BF16)
        cst = sb("cst_sb", [128, 1024])
        identf = cst[:, 0:128]
        identblk = cst[:, 128:192]
        maskNA = cst[:, 192:320]
        maskSL = cst[:, 320:384]
        maskUU = cst[:, 256:320]
        blockones = cst[:, 384:512]
        scanmask = cst[:, 512:1024]
        identb = sb("identb", [128, 128], BF16)
        pc = sb("pcol_sb", [128, 32])
        negb = sb("negb", [128, 2])
        rksel = sb("rksel_sb", [128, 8])
        lnw_bc = sb("lnw_bc", [128, 512])
        lnb_bc = sb("lnb_bc", [128, 512])
        gnw_bc = sb("gnw_bc", [128, 128])
        fnw_bc = sb("fnw_bc", [128, D])
        lrw = sb("lrw_sb", [128, 512])
        gup = sb("gup_sb", [16, 256])
        use = sb("use_sb", [128, 8])
        NF = 13
        Fpool = sb("Fpool", [128, NF * 512])

        def F(i):
            return Fpool[:, i * 512:(i + 1) * 512]

        def fk(lo, hi):
            return ["F%d" % i for i in range(lo // 512, (hi - 1) // 512 + 1)]

        xt = sb("xt", [128, D])
        xt2 = sb("xt2", [128, D])
        xt3 = sb("xt3", [128, D])
        xts = [xt, xt2, xt3]
        hb = sb("hb", [128, D], BF16)
        hT = sb("hT", [128, 8, 129], BF16)
        st1 = sb("st1", [128, 16])
        arb = sb("arb", [128, 4, 2, 2, 64], BF16)
        btb = sb("btb", [128, 4, 128], BF16)
        ktb = sb("ktb", [128, 4, 128], BF16)
        Bhb = sb("Bhb", [128, 4, 128], BF16)
        Khb = sb("Khb", [128, 4, 128], BF16)
        atb = sb("atb", [128, 4, 128], BF16)
        atok = sb("atok", [128, 4, 128], BF16)
        Bhtok = sb("Bhtok", [128, 4, 128], BF16)
        Khtok = sb("Khtok", [128, 4, 128], BF16)
        Vtok = sb("Vtok", [128, 4, 128], BF16)
        rvb = sb("rvb", [128, 4, 128], BF16)
        NAb = sb("NAb", [128, 8, 128], BF16)
        KAb = sb("KAb", [128, 8, 128], BF16)
        Lb = [sb("Lb%d" % i, [128, 8, 64], BF16) for i in range(2)]
        Nb = [sb("Nb%d" % i, [128, 8, 64], BF16) for i in range(2)]
        Xb = sb("Xb", [128, 8, 128], BF16)
        Pb = sb("Pb", [128, 4, 2, 64], BF16)
        GTb = sb("GTb", [128, 4, 128], BF16)
        A32 = sb("A32", [128, 4, 128])
        Ab = sb("Ab", [128, 4, 128], BF16)
        gamc = sb("gamc", [128, 4, 2])
        dG = sb("dG", [128, 4, 2, 64])
        qT = sb("qT", [128, 2, 128])
        kT = sb("kT", [128, 2, 128])
        glrT = sb("glrT", [16, 128])
        qib = sb("qib", [128, 2, 128], BF16)
        kib = sb("kib", [128, 2, 128], BF16)
        ksb = sb("ksb", [128, 2, 128], BF16)
        kstok = sb("kstok", [128, 2, 128], BF16)
        gvb = sb("gvb", [128, 512], BF16)
        scb = sb("scb", [128, 4, 64], BF16)
        S32 = sb("S32", [128, 2, 128])
        Sb = sb("Sb", [128, 2, 128], BF16)
        Dc = sb("Dc", [128, 2])
        gsil = sb("gsil", [128, 512], BF16)
        rsil = sb("rsil", [128, 512], BF16)
        merged = sb("merged", [128, D], BF16)
        mT = sb("mT", [128, 8, 128], BF16)
        on = sb("on", [128, D])
        gn = sb("gn", [128, 40])
        omka = sb("omka", [128, 4])
        wlalb = sb("wlalb", [128, 128], BF16)
        lrwb = sb("lrwb", [128, 512], BF16)
        bonesb = sb("bonesb", [128, 128], BF16)
        rkselb = sb("rkselb", [128, 8], BF16)
        PS = [st.enter_context(nc.psum_tensor("ps%d" % i, [128, 512], F32)) for i in range(8)]

        def pk(i):
            return ["ps%dc0" % i, "ps%dc1" % i] if i in (0, 3) else ["ps%d" % i]

        def PSb(i):
            return PS[i][:].bitcast(BF16)

        def ck(stage):
            if stop_after is not None and stage == stop_after:
                raise _Stop()

        def dma(out, in_, reads=(), writes=(), eng="sync"):
            S.op(eng, lambda e: e.dma_start(out=out, in_=in_), reads, writes, dma=True)

        def vec(fn, reads, writes):
            S.op("vector", fn, reads, writes)

        def act(fn, reads, writes):
            S.op("scalar", fn, reads, writes)

        def pool(fn, reads, writes):
            S.op("gpsimd", fn, reads, writes)

        def pe(fn, reads, writes, cls):
            S.op("tensor", fn, reads, writes, pe_class=cls)

        def mm(out, lhsT, rhs, reads, writes, start=True, stop=True, tp=None, fifo=False):
            K_, M_ = lhsT.shape[0], int(np.prod(lhsT.shape[1:]))
            if tp is None:
                cls = "F" if K_ == 128 else "G%d" % K_
            elif M_ == 64 and K_ == 64:
                cls = "D" if tp[0] == tp[1] else "X"
            else:
                cls = "R%d_%d_%d" % (tp[0], tp[1], M_)
            pe(lambda e: e.matmul(out, lhsT=lhsT, rhs=rhs, start=start, stop=stop, tile_position=tp), reads, writes, cls)

        def tr(out, in_, ident, reads, writes, fifo=False):
            pe(lambda e: e.transpose(out=out, in_=in_, identity=ident), reads, writes, "F")

        def dbg(name, ap, keys, shape, dt=F32):
            if name in dbg_names:
                d = dout("dbg_" + name, shape, dt)
                dma(d, ap, reads=keys)

        def rstd_from(dst, src, scale, eps, keys_src, keys_dst):
            act(lambda e: e.activation(out=dst, in_=src, func=AF.Ln, bias=eps, scale=scale), keys_src, keys_dst)
            act(lambda e: e.activation(out=dst, in_=dst, func=AF.Exp, scale=-0.5), keys_dst, keys_dst)

        try:
            layers = [0, 1] if fused else [0]
            for l in layers:
                Lx = (lambda ap, l=l: ap[l]) if fused else (lambda ap: ap)
                if l == 0:
                    dma(cst[:], cst_d, writes=["cst"])
                    dma(fnw_bc[:], fnw_d.partition_broadcast(128), writes=["fnw"], eng="gpsimd")
                    if not fused:
                        dma(use[:], use_d, writes=["use"])
                dma(pc[:], Lx(pcol), writes=["pc"])
                dma(rksel[:], Lx(rksel_d), writes=["rksel"])
                vec(lambda e: e.tensor_copy(out=rkselb[:], in_=rksel[:]), ["rksel"], ["rkselb"])
                dma(lrw[:], Lx(lrw_d), writes=["lrw"])
                vec(lambda e: e.tensor_copy(out=lrwb[:], in_=lrw[:]), ["lrw"], ["lrwb"])
                dma(gup[:], Lx(gup_d), writes=["gup"])
                dma(lnw_bc[:], Lx(lnw_d).partition_broadcast(128), writes=["lnw"], eng="gpsimd")
                dma(lnb_bc[:], Lx(lnb_d).partition_broadcast(128), writes=["lnb"], eng="gpsimd")
                dma(gnw_bc[:], Lx(gnw_d).partition_broadcast(128), writes=["gnw"], eng="gpsimd")
                if l == 0:
                    vec(lambda e: e.tensor_copy(out=identb[:], in_=identf), ["cst"], ["identb"])
                    vec(lambda e: e.tensor_copy(out=bonesb[:], in_=blockones), ["cst"], ["bonesb"])
                vec(lambda e: e.tensor_scalar(out=negb[:], in0=pc[:, 16:18], scalar1=-1.0, scalar2=None, op0=ALU.mult), ["pc"], ["negb"])
                vec(lambda e: e.tensor_scalar(out=omka[:], in0=pc[:, 12:16], scalar1=-1.0, scalar2=1.0, op0=ALU.mult, op1=ALU.add), ["pc"], ["omka"])
                W0 = lambda g: pc[:, g:g + 1]
                A0 = lambda g: pc[:, 4 + g:5 + g]
                KK = lambda g: pc[:, 8 + g:9 + g]
                KA = lambda g: pc[:, 12 + g:13 + g]
                NW = lambda kc: pc[:, 20 + kc:21 + kc]

                ck("setup")
                mu_bc = Fpool[:, 0:1664]
                omu_bc = Fpool[:, 1664:3328]
                stg = [Fpool[:, 3328:4992], Fpool[:, 4992:6656]]
                kmu, komu = fk(0, 1664), fk(1664, 3328)
                kst = [fk(3328, 4992), fk(4992, 6656)]
                dma(mu_bc, Lx(mu_d).partition_broadcast(128), writes=kmu, eng="gpsimd")
                vec(lambda e: e.tensor_scalar(out=omu_bc, in0=mu_bc, scalar1=-1.0, scalar2=1.0, op0=ALU.mult, op1=ALU.add), kmu, komu)
                si = 0
                for kc in range(8):
                    rows = slice(kc * 128, (kc + 1) * 128)
                    s_, ks_ = stg[si % 2], kst[si % 2]; si += 1
                    dma(s_[:, 0:GLA_IN], Lx(w_in)[rows, 0:GLA_IN], writes=ks_)
                    act(lambda e, s_=s_, kc=kc: e.activation(out=w1b[:, kc, 0:GLA_IN], in_=s_[:, 0:GLA_IN], func=AF.Copy, scale=NW(kc)),
                        ks_ + ["pc"], ["w1b"])
                    s_, ks_ = stg[si % 2], kst[si % 2]; si += 1
                    dma(s_[:, 0:SHIFT_W], Lx(w_in)[rows, GLA_IN:GLA_IN + SHIFT_W], writes=ks_)
                    pool(lambda e, s_=s_, kc=kc: e.tensor_scalar(out=s_[:, 0:SHIFT_W], in0=s_[:, 0:SHIFT_W], scalar1=NW(kc), scalar2=None, op0=ALU.mult),
                         ks_ + ["pc"], ks_)
                    vec(lambda e, s_=s_, kc=kc: e.tensor_tensor(out=w2b[:, kc, :], in0=s_[:, 0:SHIFT_W], in1=mu_bc, op=ALU.mult), ks_ + kmu, ["w2b"])
                    pool(lambda e, s_=s_, kc=kc: e.tensor_tensor(out=w1b[:, kc, GLA_IN:GLA_IN + SHIFT_W], in0=s_[:, 0:SHIFT_W], in1=omu_bc, op=ALU.mult),
                         ks_ + komu, ["w1b"])
                    s_, ks_ = stg[si % 2], kst[si % 2]; si += 1
                    dma(s_[:, 0:512], Lx(w_in)[rows, GLA_IN + SHIFT_W:IN_W], writes=ks_)
                    act(lambda e, s_=s_, kc=kc: e.activation(out=w1b[:, kc, GLA_IN + SHIFT_W:IN_W], in_=s_[:, 0:512], func=AF.Copy, scale=NW(kc)),
                        ks_ + ["pc"], ["w1b"])
                    s_, ks_ = stg[si % 2], kst[si % 2]; si += 1
                    dma(s_[:, 0:D], Lx(w_out)[rows, :], writes=ks_)
                    vec(lambda e, s_=s_, kc=kc: e.tensor_copy(out=wob[:, kc, :], in_=s_[:, 0:D]), ks_, ["wob"])

                ck("wprep")
                Mst = F(0)[:, 0:256]
                Sst = F(0)[:, 256:512]
                kF0 = ["F0"]
                vec(lambda e: e.memset(Mst, 0.0), [], kF0)
                vec(lambda e: e.memset(Sst, 0.0), [], kF0)
                for j in (range(8) if not fused else []):
                    bM, bPhi, bS = F(1)[:, 0:256], F(1)[:, 256:512], F(2)[:, 0:256]
                    bD, dM, dS_ = F(2)[:, 256:258], F(3)[:, 0:256], F(3)[:, 256:512]
                    dma(bM, sumM_d[j], writes=["F1"])
                    dma(bPhi, sumPhi_d[j], writes=["F1"])
                    dma(bS, sumS_d[j], writes=["F2"])
                    dma(bD, sumD_d[j], writes=["F2"])
                    for g in range(4):
                        for hp in range(2):
                            p0 = hp * 64
                            mm(PS[0][p0:p0 + 64, g * 64:(g + 1) * 64], bPhi[p0:p0 + 64, g * 64:(g + 1) * 64],
                               Mst[p0:p0 + 64, g * 64:(g + 1) * 64], ["F1"] + kF0, pk(0), tp=(p0, p0))
                    vec(lambda e: e.tensor_tensor(out=dM, in0=PS[0][:, 0:256], in1=bM, op=ALU.add), pk(0) + ["F1"], ["F3"])
                    vec(lambda e: e.tensor_tensor(out=dM, in0=dM, in1=Mst, op=ALU.subtract), ["F3"] + kF0, ["F3"])
                    vec(lambda e, j=j: e.scalar_tensor_tensor(out=Mst, in0=dM, scalar=use[:, j:j + 1], in1=Mst, op0=ALU.mult, op1=ALU.add),
                        ["F3", "use"] + kF0, kF0)
                    for grp in range(2):
                        vec(lambda e, grp=grp: e.scalar_tensor_tensor(out=dS_[:, grp * 128:(grp + 1) * 128], in0=Sst[:, grp * 128:(grp + 1) * 128],
                                                                      scalar=bD[:, grp:grp + 1], in1=bS[:, grp * 128:(grp + 1) * 128],
                                                                      op0=ALU.mult, op1=ALU.add), ["F2"] + kF0, ["F3"])
                    vec(lambda e: e.tensor_tensor(out=dS_, in0=dS_, in1=Sst, op=ALU.subtract), ["F3"] + kF0, ["F3"])
                    vec(lambda e, j=j: e.scalar_tensor_tensor(out=Sst, in0=dS_, scalar=use[:, j:j + 1], in1=Sst, op0=ALU.mult, op1=ALU.add),
                        ["F3", "use"] + kF0, kF0)
                for g in range(4):
                    vec(lambda e, g=g: e.tensor_copy(out=A32[:, g, 0:64], in_=Mst[:, g * 64:(g + 1) * 64]), kF0, ["A32"])
                    vec(lambda e, g=g: e.tensor_copy(out=A32[:, g, 64:128], in_=identblk), ["cst"], ["A32"])
                vec(lambda e: e.tensor_copy(out=Ab[:], in_=A32[:]), ["A32"], ["Ab"])
                vec(lambda e: e.tensor_copy(out=S32[:].rearrange("p a b -> p (a b)"), in_=Sst), kF0, ["S32"])
                vec(lambda e: e.tensor_copy(out=Sb[:], in_=S32[:]), ["S32"], ["Sb"])
                vec(lambda e: e.memset(Dc[:], 1.0), [], ["Dc"])

                ck("fold")
                def load_norm(buf, key, src_rows, nrows, rkeys=()):
                    dma(buf[0:nrows, :], src_rows, reads=list(rkeys), writes=[key])
                    act(lambda e: e.activation(out=hb[:], in_=buf[:], func=AF.Square, scale=1.0 / 32.0, accum_out=st1[:, 0:1]),
                        [key], ["hb", "st1"])
                    rstd_from(st1[:, 1:2], st1[:, 0:1], 1.0, 1e-6, ["st1"], ["st1b"])
                    vec(lambda e: e.tensor_scalar(out=hb[:], in0=buf[:], scalar1=st1[:, 1:2], scalar2=None, op0=ALU.mult), [key, "st1b"], ["hb"])

                def transposes():
                    for k in range(8):
                        tr(PSb(0)[:, k * 128:(k + 1) * 128], hb[:, k * 128:(k + 1) * 128], identb[:], ["hb", "identb"], pk(0))

                def transposes_evac():
                    transposes()
                    pool(lambda e: e.tensor_copy(out=hT[:, :, 0:1], in_=hT[:, :, 128:129]), ["hT"], ["hT"])
                    vec(lambda e: e.tensor_copy(out=hT[:, :, 1:129], in_=PSb(0).rearrange("p (k t) -> p k t", t=128)), pk(0), ["hT"])

                def tile_src(tt):
                    if l == 0:
                        return x_sh[1 + tt * 128:1 + (tt + 1) * 128, :], ()
                    return x1s[tt * 128:(tt + 1) * 128, :], ["x1s%d" % tt]

                if l == 0:
                    vec(lambda e: e.memset(xts[0][:], 0.0), [], ["xt0"])
                    load_norm(xts[0], "xt0", x_sh[0:1, :], 1)
                    transposes()
                    vec(lambda e: e.tensor_copy(out=hT[:, :, 128:129], in_=PSb(0).rearrange("p (k t) -> p k t", t=128)[:, :, 0:1]), pk(0), ["hT"])
                else:
                    vec(lambda e: e.memset(hT[:, :, 128:129], 0.0), [], ["hT"])

                ck("halo")
                GHC = [(g, hp, cc) for same in (True, False) for g in range(4) for hp in range(2) for cc in range(2) if (hp == cc) == same]
                HC = [(h, cc) for same in (True, False) for h in range(4) for cc in range(2) if ((h % 2) == cc) == same]
                src0, rk0 = tile_src(0)
                load_norm(xts[0], "xt0", src0, 128, rk0)
                transposes_evac()
                def back_a(t, xt, kx):
                    ys, sq = F(6), F(7)
                    act(lambda e: e.copy(out=sq, in_=PS[2][:]), pk(2), ["F7"])
                    vec(lambda e: e.tensor_tensor(out=ys, in0=PS[3][:], in1=sq, op=ALU.add), pk(3) + ["F7"], ["F6"])
                    ys3 = ys.rearrange("p (h v) -> p h v", v=64)
                    sq3 = sq.rearrange("p (h v) -> p h v", v=64)
                    pool(lambda e: e.tensor_tensor(out=sq, in0=ys, in1=ys, op=ALU.mult), ["F6"], ["F7"])
                    s1, s2, mean, msq, var, rstdg = gn[:, 0:8], gn[:, 8:16], gn[:, 16:24], gn[:, 24:32], gn[:, 32:40], gn[:, 8:16]
                    vec(lambda e: e.reduce_sum(out=s1, in_=ys3, axis=AX.X), ["F6"], ["gn_s1"])
                    vec(lambda e: e.reduce_sum(out=s2, in_=sq3, axis=AX.X), ["F7"], ["gn_s2"])
                    vec(lambda e: e.tensor_scalar(out=mean, in0=s1, scalar1=1.0 / 64, scalar2=None, op0=ALU.mult), ["gn_s1"], ["gn_mean"])
                    vec(lambda e: e.tensor_tensor(out=msq, in0=mean, in1=mean, op=ALU.mult), ["gn_mean"], ["gn_msq"])
                    vec(lambda e: e.scalar_tensor_tensor(out=var, in0=s2, scalar=1.0 / 64, in1=msq, op0=ALU.mult, op1=ALU.subtract), ["gn_s2", "gn_msq"], ["gn_var"])
                    rstd_from(rstdg, var, 1.0, 64e-5, ["gn_var"], ["gn_s2"])
                    bc8 = lambda ap: ap.unsqueeze(2).to_broadcast([128, 8, 64])
                    vec(lambda e: e.tensor_tensor(out=ys3, in0=ys3, in1=bc8(mean), op=ALU.subtract), ["F6", "gn_mean"], ["F6"])
                    vec(lambda e: e.tensor_tensor(out=ys3, in0=ys3, in1=bc8(rstdg), op=ALU.mult), ["F6", "gn_s2"], ["F6"])
                    pool(lambda e: e.tensor_tensor(out=ys, in0=ys, in1=lnw_bc[:], op=ALU.mult), ["F6", "lnw"], ["F6"])
                    pool(lambda e: e.tensor_tensor(out=ys, in0=ys, in1=lnb_bc[:], op=ALU.add), ["F6", "lnb"], ["F6"])
                    vec(lambda e: e.tensor_tensor(out=sq3, in0=Vtok[:].rearrange("p g (h v) -> p (g h) v", v=64), in1=bc8(st1[:, 8:16]), op=ALU.mult),
                        ["Vtok", "bcoef"], ["F7"])
                    vec(lambda e: e.tensor_tensor(out=ys, in0=ys, in1=sq, op=ALU.add), ["F6", "F7"], ["F6"])
                    vec(lambda e: e.tensor_tensor(out=merged[:, 512:1024], in0=ys, in1=rsil[:], op=ALU.mult), ["F6", "rsil"], ["merged_r"])


                def back_b(t, xt, kx):
                    ck("gla")
                    for k in range(8):
                        tr(PSb(2)[:, k * 128:(k + 1) * 128], merged[:, k * 128:(k + 1) * 128], identb[:],
                           ["merged_g" if k < 4 else "merged_r", "identb"], pk(2))
                    vec(lambda e: e.tensor_copy(out=mT[:].rearrange("p a b -> p (a b)"), in_=PSb(2)), pk(2), ["mT"])
                    for half in range(2):
                        bank = 4 + half
                        for k in range(8):
                            mm(PS[bank][:], mT[:, k, :], wob[:, k, half * 512:(half + 1) * 512], ["mT", "wob"], pk(bank), start=(k == 0), stop=(k == 7), fifo=True)
                        vec(lambda e, half=half, bank=bank, xt=xt: e.tensor_tensor(out=xt[:, half * 512:(half + 1) * 512], in0=PS[bank][:],
                                                                            in1=xt[:, half * 512:(half + 1) * 512], op=ALU.add), pk(bank) + [kx], [kx])
                    if not fused:
                        dma(x_out[t * 128:(t + 1) * 128, :], xt[:], reads=[kx])
                    elif l == 0:
                        dma(x1s[t * 128:(t + 1) * 128, :], xt[:], reads=[kx], writes=["x1s%d" % t])
                    if (not fused) or l == layers[-1]:
                        act(lambda e, xt=xt: e.activation(out=on[:], in_=xt[:], func=AF.Square, scale=1.0 / 32.0, accum_out=st1[:, 6:7]), [kx], ["on", "fst"])
                        rstd_from(st1[:, 7:8], st1[:, 6:7], 1.0, 1e-6, ["fst"], ["fst2"])
                        vec(lambda e, xt=xt: e.scalar_tensor_tensor(out=on[:], in0=xt[:], scalar=st1[:, 7:8], in1=fnw_bc[:], op0=ALU.mult, op1=ALU.mult),
                            [kx, "fst2", "fnw"], ["on"])
                        dma(y_out[t * 128:(t + 1) * 128, :], on[:], reads=["on"])

                pending = None
                for t in range(NT):
                    last = (t == NT - 1)
                    xt = xts[t % 3]
                    kx = "xt%d" % (t % 3)
                    if pending is not None:
                        back_a(*pending)
                    if not last:
                        srcn, rkn = tile_src(t + 1)
                        load_norm(xts[(t + 1) % 3], "xt%d" % ((t + 1) % 3), srcn, 128, rkn)
                    cur = lambda k: hT[:, k, 1:129]
                    prv = lambda k: hT[:, k, 0:128]

                    def proj_fm_shift(dst_ps, col0, ncols, pskey):
                        for k in range(8):
                            mm(dst_ps, w1b[:, k, GLA_IN + col0:GLA_IN + col0 + ncols], cur(k), ["w1b", "hT"], pskey, start=(k == 0), stop=False, fifo=True)
                            mm(dst_ps, w2b[:, k, col0:col0 + ncols], prv(k), ["w2b", "hT"], pskey, start=False, stop=(k == 7), fifo=True)

                    def proj_fm(dst_ps, col0, ncols, pskey):
                        for k in range(8):
                            mm(dst_ps, w1b[:, k, col0:col0 + ncols], cur(k), ["w1b", "hT"], pskey, start=(k == 0), stop=(k == 7), fifo=True)

                    def proj_tm(dst_ps, col0, pskey):
                        for k in range(8):
                            mm(dst_ps, cur(k), w1b[:, k, col0:col0 + 512], ["w1b", "hT"], pskey, start=(k == 0), stop=(k == 7), fifo=True)

                    rT, rkT, rvT, sw, aT, cs_, Ea, Eb, kk_, kT2, bT, tmp = [F(i) for i in range(12)]
                    for g in range(4):
                        proj_fm_shift(PS[1][:, g * 128:(g + 1) * 128], 0 * 512 + g * 128, 128, pk(1))
                    act(lambda e: e.copy(out=rT, in_=PS[1][:]), pk(1), ["F0"])
                    for g in range(4):
                        proj_fm_shift(PS[2][:, g * 128:(g + 1) * 128], 1 * 512 + g * 128, 128, pk(2))
                    act(lambda e: e.copy(out=rkT, in_=PS[2][:]), pk(2), ["F1"])
                    for g in range(4):
                        gs = slice(g * 128, (g + 1) * 128)
                        vec(lambda e, g=g, gs=gs: e.tensor_scalar(out=kk_[:, gs], in0=rkT[:, gs], scalar1=KK(g), scalar2=None, op0=ALU.mult), ["F1", "pc"], ["F8"])
                    pool(lambda e: e.tensor_tensor(out=rvb[:].rearrange("p a b -> p (a b)"), in0=kk_, in1=kk_, op=ALU.mult), ["F8"], ["rvb"])
                    for g in range(4):
                        proj_fm_shift(PS[3][:, g * 128:(g + 1) * 128], 2 * 512 + g * 128, 128, pk(3))
                    act(lambda e: e.copy(out=rvT, in_=PS[3][:]), pk(3), ["F2"])
                    proj_fm_shift(PS[4][:, 0:128], 1536, 128, pk(4))
                    proj_fm(PS[4][0:16, 128:256], 1024, 16, pk(4))
                    proj_fm(PS[4][:, 256:384], 0, 128, pk(4))
                    proj_fm(PS[4][:, 384:512], 128, 128, pk(4))
                    act(lambda e: e.activation(out=wlalb[0:64, :], in_=PS[4][0:64, 0:128], func=AF.Tanh), pk(4), ["wlalb"])
                    act(lambda e: e.copy(out=wlalb[64:128, :], in_=PS[4][64:128, 0:128]), pk(4), ["wlalb"])
                    act(lambda e: e.copy(out=glrT[:], in_=PS[4][0:16, 128:256]), pk(4), ["glrT"])
                    act(lambda e: e.copy(out=qT[:].rearrange("p a b -> p (a b)"), in_=PS[4][:, 256:512]), pk(4), ["qT"])
                    proj_fm(PS[5][:, 0:128], 256, 128, pk(5))
                    proj_fm(PS[5][:, 128:256], 384, 128, pk(5))
                    act(lambda e: e.copy(out=kT[:].rearrange("p a b -> p (a b)"), in_=PS[5][:, 0:256]), pk(5), ["kT"])
                    for g in range(4):
                        mm(PS[1][:, g * 128:(g + 1) * 128], lrwb[0:64, g * 128:(g + 1) * 128], wlalb[0:64, :], ["lrwb", "wlalb"], pk(1), tp=(0, 0))
                    for g in range(4):
                        mm(PS[2][:, g * 128:(g + 1) * 128], lrwb[64:128, g * 128:(g + 1) * 128], wlalb[64:128, :], ["lrwb", "wlalb"], pk(2), tp=(64, 0))
                    for g in range(4):
                        gs = slice(g * 128, (g + 1) * 128)
                        act(lambda e, g=g, gs=gs: e.activation(out=sw[:, gs], in_=PS[1][:, gs], func=AF.Sigmoid, bias=W0(g)), pk(1) + ["pc"], ["F3"])
                    vec(lambda e: e.tensor_tensor_scan(out=cs_, data0=scanmask, data1=sw, initial=0.0, op0=ALU.mult, op1=ALU.add), ["cst", "F3"], ["F5"])
                    pool(lambda e: e.tensor_tensor(out=F(12), in0=cs_, in1=sw, op=ALU.subtract), ["F5", "F3"], ["F12"])
                    vec(lambda e: e.tensor_tensor(out=tmp.rearrange("p (a b) -> p a b", b=64),
                                                  in0=cs_.rearrange("p (a b) -> p a b", b=64)[:, :, 63:64].to_broadcast([128, 8, 64]),
                                                  in1=cs_.rearrange("p (a b) -> p a b", b=64), op=ALU.subtract), ["F5"], ["F11"])
                    for g in range(4):
                        gs = slice(g * 128, (g + 1) * 128)
                        act(lambda e, g=g, gs=gs: e.activation(out=aT[:, gs], in_=PS[2][:, gs], func=AF.Sigmoid, bias=A0(g)), pk(2) + ["pc"], ["F4"])
                    proj_tm(PS[6][:], 512, pk(6))
                    vec(lambda e: e.tensor_copy(out=gvb[:], in_=PS[6][:]), pk(6), ["gvb"])
                    for g in range(4):
                        gs = slice(g * 128, (g + 1) * 128)
                        mm(PS[3][:, gs], bonesb[:], rvb[:, g, :], ["bonesb", "rvb"], pk(3))
                    rstd_from(bT, PS[3][:], 1.0, 1e-20, pk(3), ["F10"])
                    vec(lambda e: e.tensor_tensor(out=kk_, in0=kk_, in1=bT, op=ALU.mult), ["F8", "F10"], ["F8"])
                    for g in range(4):
                        gs = slice(g * 128, (g + 1) * 128)
                        vec(lambda e, g=g, gs=gs: e.tensor_scalar(out=kT2[:, gs], in0=aT[:, gs], scalar1=KA(g), scalar2=omka[:, g:g + 1], op0=ALU.mult, op1=ALU.add),
                            ["F4", "pc", "omka"], ["F9"])
                    vec(lambda e: e.tensor_tensor(out=kT2, in0=kT2, in1=rkT, op=ALU.mult), ["F9", "F1"], ["F9"])
                    pool(lambda e: e.tensor_tensor(out=bT, in0=kk_, in1=aT, op=ALU.mult), ["F8", "F4"], ["F10"])
                    proj_tm(PS[7][:], 1040, pk(7))
                    act(lambda e: e.activation(out=gsil[:], in_=PS[7][:], func=AF.Silu), pk(7), ["gsil"])
                    proj_tm(PS[0][:], GLA_IN + SHIFT_W, pk(0))
                    act(lambda e: e.activation(out=rsil[:], in_=PS[0][:], func=AF.Silu), pk(0), ["rsil"])
                    if not last:
                        transposes_evac()
                    if pending is not None:
                        back_b(*pending)
                        pending = None

                    ck("proj")
                    v2 = lambda ap: ap.rearrange("p (a b) -> p a b", b=128)
                    spg, csg = v2(F(1)[:, 0:256]), v2(F(1)[:, 256:512])
                    Eg = [v2(F(4)[:, 0:256]), v2(F(4)[:, 256:512]), v2(F(3)[:, 0:256])]
                    f2 = lambda t_: t_.rearrange("p a b -> p (a b)")
                    for grp in range(2):
                        mm(PS[5][:, grp * 128:(grp + 1) * 128], gup[:, grp * 128:(grp + 1) * 128], glrT[:], ["gup", "glrT"], pk(5))
                    cs3 = cs_.rearrange("p (a b) -> p a b", b=64)
                    arb_a = arb[:, :, :, 0, :].rearrange("p g c i -> p (g c) i")
                    arb_r = arb[:, :, :, 1, :].rearrange("p g c i -> p (g c) i")
                    v3 = lambda ap: ap.rearrange("p (a b) -> p a b", b=64)
                    b3 = lambda ap: ap[:].rearrange("p g (c i) -> p (g c) i", i=64)
                    Dpv = F(12)
                    act(lambda e: e.activation(out=Dpv, in_=Dpv, func=AF.Exp, scale=-KDEC), ["F12"], ["F12"])
                    act(lambda e: e.activation(out=tmp, in_=tmp, func=AF.Exp, scale=-KDEC), ["F11"], ["F11"])
                    act(lambda e: e.activation(out=Ea, in_=cs_, func=AF.Exp, scale=-KDEC), ["F5"], ["F6"])
                    act(lambda e: e.activation(out=Eb, in_=cs_, func=AF.Exp, scale=KDEC), ["F5"], ["F7"])
                    vec(lambda e: e.scalar_tensor_tensor(out=atb[:].rearrange("p a b -> p (a b)"), in0=kk_, scalar=-1.0, in1=Dpv, op0=ALU.mult, op1=ALU.mult),
                        ["F8", "F12"], ["atb"])
                    pool(lambda e: e.tensor_tensor(out=Khb[:].rearrange("p a b -> p (a b)"), in0=kT2, in1=tmp, op=ALU.mult), ["F9", "F11"], ["Khb"])
                    vec(lambda e: e.tensor_tensor(out=Bhb[:].rearrange("p a b -> p (a b)"), in0=bT, in1=tmp, op=ALU.mult), ["F10", "F11"], ["Bhb"])
                    pool(lambda e: e.tensor_copy(out=arb_a, in_=b3(atb)), ["atb"], ["arb_a"])
                    vec(lambda e: e.tensor_tensor(out=arb_r, in0=v3(rT), in1=v3(Ea), op=ALU.mult), ["F0", "F6"], ["arb_r"])
                    pool(lambda e: e.tensor_tensor(out=ktb[:].rearrange("p a b -> p (a b)"), in0=kT2, in1=Eb, op=ALU.mult), ["F9", "F7"], ["ktb"])
                    vec(lambda e: e.tensor_tensor(out=btb[:].rearrange("p a b -> p (a b)"), in0=bT, in1=Eb, op=ALU.mult), ["F10", "F7"], ["btb"])
                    pool(lambda e: e.tensor_copy(out=gamc[:].rearrange("p g c -> p (g c)"), in_=v3(Ea)[:, :, 63]), ["F6"], ["gamc"])

                    for grp in range(2):
                        act(lambda e, grp=grp: e.activation(out=spg[:, grp, :], in_=PS[5][:, grp * 128:(grp + 1) * 128], func=AF.Exp, scale=-1.0, bias=negb[:, grp:grp + 1]),
                            pk(5) + ["negb"], ["F1"])
                    act(lambda e: e.activation(out=f2(spg), in_=f2(spg), func=AF.Ln, bias=1.0), ["F1"], ["F1"])
                    vec(lambda e: e.tensor_tensor_scan(out=f2(csg), data0=scanmask[:, 0:256], data1=f2(spg), initial=0.0, op0=ALU.mult, op1=ALU.add), ["cst", "F1"], ["F1"])
                    act(lambda e: e.activation(out=f2(Eg[0]), in_=f2(csg), func=AF.Exp, scale=-1.0 / 16), ["F1"], ["F4"])
                    act(lambda e: e.activation(out=f2(Eg[1]), in_=f2(csg), func=AF.Exp, scale=1.0 / 16), ["F1"], ["F4"])
                    csg3 = f2(csg).rearrange("p (a b) -> p a b", b=64)
                    pool(lambda e: e.tensor_tensor(out=f2(Eg[2]).rearrange("p (a b) -> p a b", b=64), in0=csg3[:, :, 63:64].to_broadcast([128, 4, 64]), in1=csg3, op=ALU.subtract),
                         ["F1"], ["F3"])
                    act(lambda e: e.activation(out=f2(Eg[2]), in_=f2(Eg[2]), func=AF.Exp, scale=-1.0 / 16), ["F3"], ["F3"])
                    vec(lambda e: e.scalar_tensor_tensor(out=f2(qib[:]), in0=f2(qT[:]), scalar=0.125, in1=f2(Eg[0]), op0=ALU.mult, op1=ALU.mult), ["qT", "F4"], ["qib"])
                    vec(lambda e: e.tensor_tensor(out=f2(kib[:]), in0=f2(kT[:]), in1=f2(Eg[1]), op=ALU.mult), ["kT", "F4"], ["kib"])
                    pool(lambda e: e.tensor_tensor(out=f2(ksb[:]), in0=f2(kT[:]), in1=f2(Eg[2]), op=ALU.mult), ["kT", "F3"], ["ksb"])
                    ck("elem")
                    for (src, dst, key, dkey, bank) in [(atb, atok, "atb", "atok", 1), (Bhb, Bhtok, "Bhb", "Bhtok", 2), (Khb, Khtok, "Khb", "Khtok", 4)]:
                        for g in range(4):
                            tr(PSb(bank)[:, g * 128:(g + 1) * 128], src[:, g, :], identb[:], [key, "identb"], pk(bank))
                        act(lambda e, dst=dst, bank=bank: e.copy(out=dst[:].rearrange("p a b -> p (a b)"), in_=PSb(bank)[:, 0:512]),
                            pk(bank), [dkey])
                    pool(lambda e: e.tensor_copy(out=rvb[:].rearrange("p a b -> p (a b)"), in_=rvT), ["F2"], ["rvb"])
                    for g in range(4):
                        tr(PSb(5)[:, g * 128:(g + 1) * 128], rvb[:, g, :], identb[:], ["rvb", "identb"], pk(5))
                    act(lambda e: e.copy(out=Vtok[:].rearrange("p a b -> p (a b)"), in_=PSb(5)[:, 0:512]), pk(5), ["Vtok"])

                    pool(lambda e: e.tensor_tensor(out=rvb[:].rearrange("p a b -> p (a b)"), in0=rT, in1=kT2, op=ALU.mult), ["F0", "F9"], ["rvb"])
                    for g in range(4):
                        mm(PS[5][:, g * 2:(g + 1) * 2], rvb[:, g, :], rkselb[:, g * 2:(g + 1) * 2], ["rvb", "rkselb"], pk(5))
                    act(lambda e: e.copy(out=st1[:, 8:16], in_=PS[5][:, 0:8]), pk(5), ["bcoef"])
                    ck("trans")
                    for (g, hp, cc) in GHC:
                        gh = g * 2 + hp
                        p0, c0 = hp * 64, cc * 64
                        tk = slice(cc * 64, (cc + 1) * 64)
                        rhs_ar = arb[p0:p0 + 64, g, cc, :, :].rearrange("p a i -> p (a i)")
                        bN, bK, cg = (6 if gh < 4 else 7), (0 if gh < 4 else 1), (gh % 4) * 128
                        mm(PS[bN][c0:c0 + 64, cg:cg + 128], btb[p0:p0 + 64, g, tk], rhs_ar, ["btb", "arb_a", "arb_r"], pk(bN), tp=(p0, c0))
                        mm(PS[bK][c0:c0 + 64, cg:cg + 128], ktb[p0:p0 + 64, g, tk], rhs_ar, ["ktb", "arb_a", "arb_r"], pk(bK), tp=(p0, c0))
                        mm(PS[2][c0:c0 + 64, gh * 64:gh * 64 + 64], atb[p0:p0 + 64, g, tk], btb[p0:p0 + 64, g, tk], ["atb", "btb"], pk(2), tp=(p0, c0))
                    mNA4 = maskNA.unsqueeze(1).to_broadcast([128, 4, 128])
                    mSL8 = maskSL.unsqueeze(1).to_broadcast([128, 8, 64])
                    p4 = lambda ap: ap.rearrange("p (a b) -> p a b", b=128)
                    vec(lambda e: e.tensor_tensor(out=NAb[:, 0:4, :], in0=p4(PS[6][:]), in1=mNA4, op=ALU.mult), pk(6) + ["cst"], ["NAb"])
                    vec(lambda e: e.tensor_tensor(out=NAb[:, 4:8, :], in0=p4(PS[7][:]), in1=mNA4, op=ALU.mult), pk(7) + ["cst"], ["NAb"])
                    vec(lambda e: e.tensor_tensor(out=KAb[:, 0:4, :], in0=p4(PS[0][:]), in1=mNA4, op=ALU.mult), pk(0) + ["cst"], ["KAb"])
                    vec(lambda e: e.tensor_tensor(out=KAb[:, 4:8, :], in0=p4(PS[1][:]), in1=mNA4, op=ALU.mult), pk(1) + ["cst"], ["KAb"])
                    vec(lambda e: e.tensor_tensor(out=Lb[0][:], in0=PS[2][:].rearrange("p (a b) -> p a b", b=64), in1=mSL8, op=ALU.mult), pk(2) + ["cst"], ["Lb0h0", "Lb0h1"])

                    for grp in range(2):
                        tr(PSb(6)[:, grp * 128:(grp + 1) * 128], ksb[:, grp, :], identb[:], ["ksb", "identb"], pk(6))
                    act(lambda e: e.copy(out=f2(kstok[:]), in_=PSb(6)[:, 0:256]), pk(6), ["kstok"])
                    for (h, cc) in HC:
                        grp, hp = h // 2, h % 2
                        p0 = hp * 64
                        c0 = cc * 64
                        tk = slice(c0, c0 + 64)
                        mm(PS[7][c0:c0 + 64, h * 64:(h + 1) * 64], kib[p0:p0 + 64, grp, tk], qib[p0:p0 + 64, grp, tk], ["kib", "qib"], pk(7), tp=(p0, c0))
                    vec(lambda e: e.tensor_tensor(out=scb[:], in0=PS[7][:, 0:256].rearrange("p (a b) -> p a b", b=64),
                                                  in1=maskUU.unsqueeze(1).to_broadcast([128, 4, 64]), op=ALU.mult), pk(7) + ["cst"], ["scb"])
                    ck("score")
                    for g in range(4):
                        for hp in range(2):
                            gh = g * 2 + hp
                            for cc in range(2):
                                c0 = cc * 64
                                mm(PS[3][c0:c0 + 64, gh * 64:gh * 64 + 64], KAb[c0:c0 + 64, gh, 0:64], Vtok[c0:c0 + 64, g, hp * 64:hp * 64 + 64],
                                   ["KAb", "Vtok"], pk(3), tp=(c0, c0))
                    pool(lambda e: e.tensor_copy(out=Xb[:, :, 0:64], in_=atok[:].rearrange("p g (h k) -> p (g h) k", k=64)), ["atok"], ["Xb0", "Xb1"])
                    act(lambda e: e.copy(out=Xb[:, :, 64:128], in_=PS[3][:].rearrange("p (a b) -> p a b", b=64)), pk(3), ["Xb0", "Xb1"])

                    ck("x0")
                    for lvl in range(6):
                        a_, b_ = lvl % 2, (lvl + 1) % 2
                        for hf in range(2):
                            ghs = range(hf * 4, hf * 4 + 4)
                            kX = "Xb%d" % hf
                            kN = "NAb" if lvl == 0 else "Nb%dh%d" % (a_, hf)
                            kL = "Lb%dh%d" % (a_, hf)
                            bank = 4 + hf
                            for gh in ghs:
                                for cc in range(2):
                                    c0 = cc * 64
                                    lhs = NAb[c0:c0 + 64, gh, 0:64] if lvl == 0 else Nb[a_][c0:c0 + 64, gh, :]
                                    mm(PS[bank][c0:c0 + 64, (gh % 4) * 128:(gh % 4) * 128 + 128], lhs, Xb[c0:c0 + 64, gh, :], [kN, kX], pk(bank), tp=(c0, c0))
                            if lvl < 5:
                                for gh in ghs:
                                    for cc in range(2):
                                        c0 = cc * 64
                                        lhsN = NAb[c0:c0 + 64, gh, 0:64] if lvl == 0 else Nb[a_][c0:c0 + 64, gh, :]
                                        Lp = Lb[a_][c0:c0 + 64, gh, :]
                                        gq = (gh % 4) * 64
                                        mm(PS[6 + hf][c0:c0 + 64, gq:gq + 64], Lp, lhsN, [kL, kN], pk(6 + hf), tp=(c0, c0))
                                        if lvl < 4:
                                            mm(PS[hf][c0:c0 + 64, gq:gq + 64], lhsN, Lp, [kL, kN], pk(hf), tp=(c0, c0))
                            hs = slice(hf * 4, hf * 4 + 4)
                            cs256 = slice(hf * 256, hf * 256 + 256)
                            vec(lambda e, hs=hs, bank=bank: e.tensor_tensor(out=Xb[:, hs, :], in0=p4(PS[bank][:]), in1=Xb[:, hs, :], op=ALU.add), pk(bank) + [kX], [kX])
                            if lvl < 5:
                                act(lambda e, b_=b_, hs=hs, hf=hf: e.copy(out=Nb[b_][:, hs, :].rearrange("p a b -> p (a b)"), in_=PS[6 + hf][:, 0:256]),
                                    pk(6 + hf), ["Nb%dh%d" % (b_, hf)])
                                if lvl < 4:
                                    act(lambda e, b_=b_, hs=hs, hf=hf: e.copy(out=Lb[b_][:, hs, :].rearrange("p a b -> p (a b)"), in_=PS[hf][:, 0:256]),
                                        pk(hf), ["Lb%dh%d" % (b_, hf)])

                    ck("neumann")
                    QTs = sw
                    for (g, hp, cc) in GHC:
                        gh = g * 2 + hp
                        if True:
                            if True:
                                p0, c0 = hp * 64, cc * 64
                                Wt = Xb[c0:c0 + 64, gh, 0:64]
                                Ut = Xb[c0:c0 + 64, gh, 64:128]
                                col = (g * 2 + cc) * 64
                                mm(PS[0][p0:p0 + 64, col:col + 64], Wt, Bhtok[c0:c0 + 64, g, p0:p0 + 64], ["Xb0", "Xb1", "Bhtok"], pk(0), tp=(c0, p0))
                                mm(PS[1][p0:p0 + 64, col:col + 64], Bhtok[c0:c0 + 64, g, p0:p0 + 64], Ut, ["Xb0", "Xb1", "Bhtok"], pk(1), start=True, stop=False, tp=(c0, p0))
                                mm(PS[1][p0:p0 + 64, col:col + 64], Khtok[c0:c0 + 64, g, p0:p0 + 64], Vtok[c0:c0 + 64, g, p0:p0 + 64], ["Khtok", "Vtok"], pk(1),
                                   start=False, stop=True, tp=(c0, p0))
                                mm(PS[2][p0:p0 + 64, col:col + 64], Wt, NAb[c0:c0 + 64, gh, 64:128], ["Xb0", "Xb1", "NAb"], pk(2), tp=(c0, p0))
                    pool(lambda e: e.tensor_tensor(out=dG[:].rearrange("p g c k -> p (g c) k"),
                                                   in0=identblk.unsqueeze(1).to_broadcast([128, 8, 64]),
                                                   in1=gamc[:].rearrange("p g c -> p (g c)").unsqueeze(2).to_broadcast([128, 8, 64]), op=ALU.mult),
                         ["cst", "gamc"], ["dG"])
                    vec(lambda e: e.tensor_tensor(out=Pb[:].rearrange("p g c k -> p (g c k)"), in0=PS[0][:], in1=dG[:].rearrange("p g c k -> p (g c k)"), op=ALU.add),
                        pk(0) + ["dG"], ["Pb"])
                    act(lambda e: e.copy(out=QTs, in_=PS[1][:]), pk(1), ["F3"])
                    vec(lambda e: e.tensor_tensor(out=GTb[:].rearrange("p g (c i) -> p (g c) i", i=64), in0=PS[2][:].rearrange("p (a b) -> p a b", b=64),
                                                  in1=arb_r, op=ALU.add), pk(2) + ["arb_r"], ["GTb"])

                    ck("pqg")
                    for cc in range(2):
                        c0 = cc * 64
                        kY = "ps3c%d" % cc
                        for g in range(4):
                            for hp in range(2):
                                gh = g * 2 + hp
                                p0 = hp * 64
                                yo = PS[3][c0:c0 + 64, gh * 64:gh * 64 + 64]
                                mm(yo, NAb[c0:c0 + 64, gh, 64:128], Xb[c0:c0 + 64, gh, 64:128], ["NAb", "Xb0", "Xb1"], [kY], start=True, stop=False, tp=(c0, c0))
                                mm(yo, KAb[c0:c0 + 64, gh, 64:128], Vtok[c0:c0 + 64, g, p0:p0 + 64], ["KAb", "Vtok"], [kY], start=False, stop=True, tp=(c0, c0))
                    for cc in range(2):
                        c0 = cc * 64
                        for hp in (cc, 1 - cc):
                            for g in range(4):
                                gh = g * 2 + hp
                                p0 = hp * 64
                                mm(PS[2][c0:c0 + 64, gh * 64:gh * 64 + 64], GTb[p0:p0 + 64, g, c0:c0 + 64], Ab[p0:p0 + 64, g, 0:64], ["GTb", "Ab"], pk(2), tp=(p0, c0))
                        for g in range(4):
                            for hp in range(2):
                                p0 = hp * 64
                                mm(PS[4][p0:p0 + 64, g * 128:(g + 1) * 128], Pb[p0:p0 + 64, g, cc, :], Ab[p0:p0 + 64, g, :], ["Pb", "Ab"], pk(4), tp=(p0, p0))
                        q4 = QTs.rearrange("p (g c v) -> p g c v", c=2, v=64)
                        vec(lambda e, cc=cc, q4=q4: e.tensor_tensor(out=A32[:, :, 0:64], in0=p4(PS[4][:])[:, :, 0:64], in1=q4[:, :, cc, :], op=ALU.add),
                            pk(4) + ["F3"], ["A32"])
                        vec(lambda e: e.tensor_copy(out=A32[:, :, 64:128], in_=p4(PS[4][:])[:, :, 64:128]), pk(4), ["A32"])
                        pool(lambda e: e.tensor_copy(out=Ab[:], in_=A32[:]), ["A32"], ["Ab"])

                    ck("scan")
                    ck("repi")
                    for cc in range(2):
                        c0 = cc * 64
                        tk = slice(c0, c0 + 64)
                        kO = "ps0c%d" % cc
                        for h in range(4):
                            oo = PS[0][c0:c0 + 64, h * 128:(h + 1) * 128]
                            mm(oo, scb[c0:c0 + 64, h, :], gvb[c0:c0 + 64, h * 128:(h + 1) * 128], ["scb", "gvb"], [kO], tp=(c0, c0))
                        for h in (cc, cc + 2, 1 - cc, 3 - cc):
                            grp, hp = h // 2, h % 2
                            p0 = hp * 64
                            mm(PS[4][c0:c0 + 64, h * 128:(h + 1) * 128], qib[p0:p0 + 64, grp, tk], Sb[p0:p0 + 64, grp, :], ["qib", "Sb"], pk(4), tp=(p0, c0))
                        for h in (cc, cc + 2, 1 - cc, 3 - cc):
                            grp, hp = h // 2, h % 2
                            p0 = hp * 64
                            mm(PS[1][p0:p0 + 64, grp * 128:(grp + 1) * 128], kstok[c0:c0 + 64, grp, p0:p0 + 64], gvb[c0:c0 + 64, h * 128:(h + 1) * 128],
                               ["kstok", "gvb"], pk(1), tp=(c0, p0))
                        for grp in range(2):
                            dcol = Eg[0][:, grp, c0 + 63:c0 + 64]
                            vec(lambda e, grp=grp, dcol=dcol: e.scalar_tensor_tensor(out=S32[:, grp, :], in0=S32[:, grp, :], scalar=dcol,
                                                                                     in1=PS[1][:, grp * 128:(grp + 1) * 128], op0=ALU.mult, op1=ALU.add),
                                ["S32", "F4", "ps1"], ["S32"])
                        pool(lambda e: e.tensor_copy(out=Sb[:], in_=S32[:]), ["S32"], ["Sb"])
                        vec(lambda e, c0=c0: e.tensor_tensor(out=Dc[:], in0=Dc[:], in1=Eg[0][:, :, c0 + 63], op=ALU.mult), ["Dc", "F4"], ["Dc"])
                    os_, osq = kk_, kT2
                    act(lambda e: e.copy(out=osq, in_=PS[4][:]), pk(4), ["F9"])
                    vec(lambda e: e.tensor_tensor(out=os_, in0=PS[0][:], in1=osq, op=ALU.add), pk(0) + ["F9"], ["F8"])
                    pool(lambda e: e.tensor_tensor(out=osq, in0=os_, in1=os_, op=ALU.mult), ["F8"], ["F9"])
                    vec(lambda e: e.reduce_sum(out=st1[:, 2:6], in_=osq.rearrange("p (h v) -> p h v", v=128), axis=AX.X), ["F9"], ["gst"])
                    rstd_from(st1[:, 2:6], st1[:, 2:6], 1.0 / 128, 1e-6, ["gst"], ["gst"])
                    os3 = os_.rearrange("p (h v) -> p h v", v=128)
                    vec(lambda e: e.tensor_tensor(out=os3, in0=os3, in1=st1[:, 2:6].unsqueeze(2).to_broadcast([128, 4, 128]), op=ALU.mult), ["F8", "gst"], ["F8"])
                    pool(lambda e: e.tensor_tensor(out=os3, in0=os3, in1=gnw_bc[:].unsqueeze(1).to_broadcast([128, 4, 128]), op=ALU.mult), ["F8", "gnw"], ["F8"])
                    vec(lambda e: e.tensor_tensor(out=merged[:, 0:512], in0=os_, in1=gsil[:], op=ALU.mult), ["F8", "gsil"], ["merged_g"])

                    pending = (t, xt, kx)
                if pending is not None:
                    back_a(*pending)
                    back_b(*pending)
                    pending = None

        except _Stop:
            pass
        if not fused:
            dma(oA, A32[:].rearrange("p a b -> p (a b)"), reads=["A32"])
            dma(oS, S32[:].rearrange("p a b -> p (a b)"), reads=["S32"])
            dma(oD, Dc[:], reads=["Dc"])
        S.finish()
        S.emit()
    return nc, S


def _consts():
    c = np.zeros((128, 1024), np.float32)
    c[:, 0:128] = np.eye(128, dtype=np.float32)
    e64 = np.eye(64, dtype=np.float32)
    c[:, 128:192] = np.concatenate([e64, e64], 0)
    su = np.triu(np.ones((64, 64), np.float32), 1)
    uu = np.triu(np.ones((64, 64), np.float32), 0)
    c[:, 192:256] = np.concatenate([su, su], 0)
    c[:, 256:320] = np.concatenate([uu, uu], 0)
    c[:, 320:384] = np.concatenate([su.T, su.T], 0)
    bo = np.zeros((128, 128), np.float32)
    bo[0:64, 0:64] = 1.0
    bo[64:128, 64:128] = 1.0
    c[:, 384:512] = bo
    sm = np.ones((128, 512), np.float32)
    sm[:, ::64] = 0.0
    c[:, 512:1024] = sm
    return c


def _fm_cols(v512):
    return np.ascontiguousarray(v512.reshape(4, 128).T)


def _layer_params(inp, l):
    pcol = np.zeros((128, 32), np.float32)
    pcol[:, 0:4] = _fm_cols(inp["rwkv_w0"][l])
    pcol[:, 4:8] = _fm_cols(inp["rwkv_a0"][l])
    pcol[:, 8:12] = _fm_cols(inp["rwkv_k_k"][l])
    pcol[:, 12:16] = _fm_cols(inp["rwkv_k_a"][l])
    pcol[:, 16:18] = inp["gla_gate_bias"][l].reshape(2, 128).T
    pcol[:, 20:28] = inp["norm_w"][l].reshape(8, 128).T
    rk = inp["rwkv_r_k"][l].reshape(512)
    rksel = np.zeros((128, 4, 2), np.float32)
    for g in range(4):
        for hp in range(2):
            rksel[hp * 64:(hp + 1) * 64, g, hp] = rk[g * 128 + hp * 64:g * 128 + (hp + 1) * 64]
    lrw = np.concatenate([inp["rwkv_w_up"][l], inp["rwkv_a_up"][l]], 0)
    return {
        "w_in": np.ascontiguousarray(inp["w_in"][l]), "w_out": np.ascontiguousarray(inp["w_out"][l]),
        "pcol": pcol, "rksel": rksel.reshape(128, 8), "mu": np.ascontiguousarray(inp["rwkv_mu"][l]),
        "lnw": np.ascontiguousarray(inp["rwkv_ln_w"][l]), "lnb": np.ascontiguousarray(inp["rwkv_ln_b"][l]),
        "gnw": np.ascontiguousarray(inp["gla_norm_w"][l]), "fnw": np.ascontiguousarray(inp["final_norm_w"]),
        "lrw": np.ascontiguousarray(lrw), "gup": np.ascontiguousarray(inp["gla_gate_up"][l]), "cst": _consts(),
    }


def _zero_sums():
    return {"sumM": np.zeros((8, 128, 256), np.float32), "sumPhi": np.zeros((8, 128, 256), np.float32),
            "sumS": np.zeros((8, 128, 256), np.float32), "sumD": np.zeros((8, 128, 2), np.float32)}


def _sums_from_results(results):
    s = _zero_sums()
    for j, r in enumerate(results):
        A = r["oA"].reshape(128, 4, 128)
        s["sumM"][j] = A[:, :, 0:64].reshape(128, 256)
        Pc = A[:, :, 64:128].reshape(2, 64, 4, 64)
        s["sumPhi"][j] = Pc.transpose(0, 3, 2, 1).reshape(128, 256)
        s["sumS"][j] = r["oS"]
        s["sumD"][j] = r["oD"]
    return s


def _use_masks():
    u = np.zeros((NCORES, 128, 8), np.float32)
    for c in range(NCORES):
        b, q = divmod(c, 4)
        for j in range(NCORES):
            bj, qj = divmod(j, 4)
            if bj == b and qj < q:
                u[c, :, j] = 1.0
    return u


_NC_CACHE = {}


def _get_nc(NT, fused=False):
    if (NT, fused) not in _NC_CACHE:
        _NC_CACHE[(NT, fused)] = build_pass(NT, fused=fused)[0]
    return _NC_CACHE[(NT, fused)]


def _run_pass(xfull, lp, sums, use):
    in_maps = []
    for c in range(NCORES):
        b, q = divmod(c, 4)
        xs = np.zeros((TOK + 1, D), np.float32)
        xs[1:] = xfull[b, q * TOK:(q + 1) * TOK]
        if q > 0:
            xs[0] = xfull[b, q * TOK - 1]
        m = dict(lp)
        m.update(sums)
        m["x_sh"] = xs
        m["use"] = use[c]
        in_maps.append(m)
    res = run_bass_kernel_spmd(_get_nc(TOK // 128), in_maps, core_ids=list(range(NCORES)))
    return res.results


def _gather(results, key):
    out = np.zeros((2, 4 * TOK, D), np.float32)
    for c, r in enumerate(results):
        b, q = divmod(c, 4)
        out[b, q * TOK:(q + 1) * TOK] = r[key]
    return out


def kernel_unfused(**inputs):
    inp = {k: np.asarray(v, dtype=np.float32) for k, v in inputs.items()}
    use = _use_masks()
    x = inp["x"]
    res = None
    for l in range(2):
        lp = _layer_params(inp, l)
        r0 = _run_pass(x, lp, _zero_sums(), use)
        res = _run_pass(x, lp, _sums_from_results(r0), use)
        x = _gather(res, "x_out")
    return _gather(res, "y_out")


def _fused_params(inp):
    lps = [_layer_params(inp, l) for l in range(2)]
    m = {}
    for k in lps[0]:
        if k in ("fnw", "cst"):
            m[k] = lps[0][k]
        else:
            m[k] = np.ascontiguousarray(np.stack([lps[0][k], lps[1][k]], 0))
    return m


def kernel(**inputs):
    inp = {k: np.asarray(v, dtype=np.float32) for k, v in inputs.items()}
    T = inp["x"].shape[1]
    fp = _fused_params(inp)
    in_maps = []
    for c in range(NCORES):
        b, q = divmod(c, 4)
        xs = np.zeros((T + 1, D), np.float32)
        xs[1:] = inp["x"][b]
        m = dict(fp)
        m["x_sh"] = xs
        in_maps.append(m)
    res = run_bass_kernel_spmd(_get_nc(T // 128, True), in_maps, core_ids=list(range(NCORES)))
    out = np.zeros((2, T, D), np.float32)
    Q = T // 4
    for c, r in enumerate(res.results):
        b, q = divmod(c, 4)
        out[b, q * Q:(q + 1) * Q] = r["y_out"][q * Q:(q + 1) * Q]
    return out
```

```python
import contextlib
import numpy as np
import concourse.bass as bass
import concourse.mybir as mybir
from concourse.bass_utils import run_bass_kernel_spmd

F32 = mybir.dt.float32
BF16 = mybir.dt.bfloat16
AF = mybir.ActivationFunctionType
ALU = mybir.AluOpType
AX = mybir.AxisListType

D = 1024
NCORES = 8
TOK = 2048
GLA_IN = 1552
SHIFT_W = 1664
IN_W = 3728
KDEC = float(np.exp(-0.5))


class Sched:
    ENG = ("sync", "scalar", "vector", "gpsimd", "tensor")
    NDMA = 24

    def __init__(self, nc):
        self.nc = nc
        self.dma_streams = {"sync": ["dq_sync%d" % i for i in range(self.NDMA)], "gpsimd": ["dq_pool%d" % i for i in range(8)]}
        self.streams = list(self.ENG) + self.dma_streams["sync"] + self.dma_streams["gpsimd"]
        self.rrs = {"sync": 0, "gpsimd": 0}
        self.ops = {e: [] for e in self.ENG}
        self.tick = {s: 0 for s in self.streams}
        self.seen = {e: {s: 0 for s in self.streams} for e in self.ENG}
        self.lastw = {}
        self.readers = {}
        self.n = 0
        self.rr = 0
        self.pe_cur = None
        self._drain = -1

    def _need(self, eng, deps, pe_fifo=False):
        req = {}
        for (s, t) in deps:
            if pe_fifo and s == "tensor" and eng == "tensor" and t != self._drain:
                continue
            if t > self.seen[eng][s]:
                req[s] = max(req.get(s, 0), t)
        for s, t in req.items():
            self.seen[eng][s] = t
            self.ops[eng].append(("wait", s, t))

    def op(self, eng, fn, reads=(), writes=(), dma=False, pe_fifo=False, pe_class=None):
        deps = []
        if eng == "tensor":
            if pe_class is not None and pe_class != "O" and pe_class == self.pe_cur:
                pe_fifo = True
            else:
                pe_fifo = True
                self._drain = self.tick["tensor"]
                if self.tick["tensor"] > 0:
                    deps.append(("tensor", self.tick["tensor"]))
            self.pe_cur = pe_class
        for k in reads:
            if k in self.lastw:
                deps.append(self.lastw[k])
        for k in writes:
            if k in self.lastw:
                deps.append(self.lastw[k])
            deps.extend(self.readers.get(k, ()))
        if dma:
            pool_ = self.dma_streams[eng]
            stream = pool_[self.rrs[eng]]
            self.rrs[eng] = (self.rrs[eng] + 1) % len(pool_)
            if self.tick[stream] > 0:
                deps.append((stream, self.tick[stream]))
            inc = 16
        else:
            stream, inc = eng, 1
        self._need(eng, deps, pe_fifo)
        self.tick[stream] += inc
        t = self.tick[stream]
        self.ops[eng].append(("op", fn, stream, inc))
        self.n += 1
        for k in reads:
            self.readers.setdefault(k, []).append((stream, t))
        for k in writes:
            self.lastw[k] = (stream, t)
            self.readers[k] = []

    def finish(self):
        for eng in self.ENG:
            self._need(eng, [(s, self.tick[s]) for s in self.streams if self.tick[s] > 0])

    def emit(self):
        nc = self.nc
        waited = {e: set() for e in self.ENG}
        for e in self.ENG:
            for it in self.ops[e]:
                if it[0] == "wait" and it[1] in waited:
                    waited[it[1]].add(it[2])
        rank = {e: {t: i + 1 for i, t in enumerate(sorted(waited[e]))} for e in self.ENG}
        with contextlib.ExitStack() as st:
            sems = {s: st.enter_context(nc.semaphore("s_" + s)) for s in self.streams}
            block = st.enter_context(nc.Block())

            def runner(e):
                def f(engine):
                    cnt = 0
                    for it in self.ops[e]:
                        if it[0] == "wait":
                            v = rank[it[1]][it[2]] if it[1] in rank else it[2]
                            engine.wait_ge(sems[it[1]], v)
                        else:
                            ins = it[1](engine)
                            if it[2] in rank:
                                cnt += 1
                                if cnt in rank[it[2]]:
                                    ins.then_inc(sems[it[2]], 1)
                            else:
                                ins.then_inc(sems[it[2]], it[3])
                return f
            block.sync(runner("sync"))
            block.scalar(runner("scalar"))
            block.vector(runner("vector"))
            block.gpsimd(runner("gpsimd"))
            block.tensor(runner("tensor"))


class _Stop(Exception):
    pass


def build_pass(NT=16, dbg_names=(), stop_after=None, fused=False):
    nc = bass.Bass("TRN2", target_bir_lowering=False)
    T = NT * 128

    def din(name, shape, dt=F32):
        return nc.dram_tensor(name, list(shape), dt, kind="ExternalInput").ap()

    def dout(name, shape, dt=F32):
        return nc.dram_tensor(name, list(shape), dt, kind="ExternalOutput").ap()

    LD = [2] if fused else []
    x_sh = din("x_sh", [T + 1, D])
    w_in = din("w_in", LD + [D, IN_W])
    w_out = din("w_out", LD + [D, D])
    pcol = din("pcol", LD + [128, 32])
    rksel_d = din("rksel", LD + [128, 8])
    mu_d = din("mu", LD + [SHIFT_W])
    lnw_d = din("lnw", LD + [512])
    lnb_d = din("lnb", LD + [512])
    gnw_d = din("gnw", LD + [128])
    fnw_d = din("fnw", [D])
    lrw_d = din("lrw", LD + [128, 512])
    gup_d = din("gup", LD + [16, 256])
    cst_d = din("cst", [128, 1024])
    y_out = dout("y_out", [T, D])
    if fused:
        x1s = nc.dram_tensor("x1s", [T, D], F32).ap()
    else:
        sumM_d = din("sumM", [8, 128, 256])
        sumPhi_d = din("sumPhi", [8, 128, 256])
        sumS_d = din("sumS", [8, 128, 256])
        sumD_d = din("sumD", [8, 128, 2])
        use_d = din("use", [128, 8])
        x_out = dout("x_out", [T, D])
        oA = dout("oA", [128, 512])
        oS = dout("oS", [128, 256])
        oD = dout("oD", [128, 2])
    dbg_out = {}

    with contextlib.ExitStack() as st:
        def sb(name, shape, dt=F32):
            return st.enter_context(nc.sbuf_tensor(name, list(shape), dt))

        S = Sched(nc)
        w1b = sb("w1b", [128, 8, IN_W], BF16)
        w2b = sb("w2b", [128, 8, SHIFT_W], BF16)
        wob = sb("wob", [128, 8, D], BF16)
        cst = sb("cst_sb", [128, 1024])
        identf = cst[:, 0:128]
        identblk = cst[:, 128:192]
        maskNA = cst[:, 192:320]
        maskSL = cst[:, 320:384]
        maskUU = cst[:, 256:320]
        blockones = cst[:, 384:512]
        scanmask = cst[:, 512:1024]
        identb = sb("identb", [128, 128], BF16)
        pc = sb("pcol_sb", [128, 32])
        negb = sb("negb", [128, 2])
        rksel = sb("rksel_sb", [128, 8])
        lnw_bc = sb("lnw_bc", [128, 512])
        lnb_bc = sb("lnb_bc", [128, 512])
        gnw_bc = sb("gnw_bc", [128, 128])
        fnw_bc = sb("fnw_bc", [128, D])
        lrw = sb("lrw_sb", [128, 512])
        gup = sb("gup_sb", [16, 256])
        use = sb("use_sb", [128, 8])
        NF = 13
        Fpool = sb("Fpool", [128, NF * 512])

        def F(i):
            return Fpool[:, i * 512:(i + 1) * 512]

        def fk(lo, hi):
            return ["F%d" % i for i in range(lo // 512, (hi - 1) // 512 + 1)]

        xt = sb("xt", [128, D])
        xt2 = sb("xt2", [128, D])
        xt3 = sb("xt3", [128, D])
        xts = [xt, xt2, xt3]
        hb = sb("hb", [128, D], BF16)
        hT = sb("hT", [128, 8, 129], BF16)
        st1 = sb("st1", [128, 16])
        arb = sb("arb", [128, 4, 2, 2, 64], BF16)
        btb = sb("btb", [128, 4, 128], BF16)
        ktb = sb("ktb", [128, 4, 128], BF16)
        Bhb = sb("Bhb", [128, 4, 128], BF16)
        Khb = sb("Khb", [128, 4, 128], BF16)
        atb = sb("atb", [128, 4, 128], BF16)
        atok = sb("atok", [128, 4, 128], BF16)
        Bhtok = sb("Bhtok", [128, 4, 128], BF16)
        Khtok = sb("Khtok", [128, 4, 128], BF16)
        Vtok = sb("Vtok", [128, 4, 128], BF16)
        rvb = sb("rvb", [128, 4, 128], BF16)
        NAb = sb("NAb", [128, 8, 128], BF16)
        KAb = sb("KAb", [128, 8, 128], BF16)
        Lb = [sb("Lb%d" % i, [128, 8, 64], BF16) for i in range(2)]
        Nb = [sb("Nb%d" % i, [128, 8, 64], BF16) for i in range(2)]
        Xb = sb("Xb", [128, 8, 128], BF16)
        Pb = sb("Pb", [128, 4, 2, 64], BF16)
        GTb = sb("GTb", [128, 4, 128], BF16)
        A32 = sb("A32", [128, 4, 128])
        Ab = sb("Ab", [128, 4, 128], BF16)
        gamc = sb("gamc", [128, 4, 2])
        dG = sb("dG", [128, 4, 2, 64])
        qT = sb("qT", [128, 2, 128])
        kT = sb("kT", [128, 2, 128])
        glrT = sb("glrT", [16, 128])
        qib = sb("qib", [128, 2, 128], BF16)
        kib = sb("kib", [128, 2, 128], BF16)
        ksb = sb("ksb", [128, 2, 128], BF16)
        kstok = sb("kstok", [128, 2, 128], BF16)
        gvb = sb("gvb", [128, 512], BF16)
        scb = sb("scb", [128, 4, 64], BF16)
        S32 = sb("S32", [128, 2, 128])
        Sb = sb("Sb", [128, 2, 128], BF16)
        Dc = sb("Dc", [128, 2])
        gsil = sb("gsil", [128, 512], BF16)
        rsil = sb("rsil", [128, 512], BF16)
        merged = sb("merged", [128, D], BF16)
        mT = sb("mT", [128, 8, 128], BF16)
        on = sb("on", [128, D])
        gn = sb("gn", [128, 40])
        omka = sb("omka", [128, 4])
        wlalb = sb("wlalb", [128, 128], BF16)
        lrwb = sb("lrwb", [128, 512], BF16)
        bonesb = sb("bonesb", [128, 128], BF16)
        rkselb = sb("rkselb", [128, 8], BF16)
        PS = [st.enter_context(nc.psum_tensor("ps%d" % i, [128, 512], F32)) for i in range(8)]

        def pk(i):
            return ["ps%dc0" % i, "ps%dc1" % i] if i in (0, 3) else ["ps%d" % i]

        def PSb(i):
            return PS[i][:].bitcast(BF16)

        def ck(stage):
            if stop_after is not None and stage == stop_after:
                raise _Stop()

        def dma(out, in_, reads=(), writes=(), eng="sync"):
            S.op(eng, lambda e: e.dma_start(out=out, in_=in_), reads, writes, dma=True)

        def vec(fn, reads, writes):
            S.op("vector", fn, reads, writes)

        def act(fn, reads, writes):
            S.op("scalar", fn, reads, writes)

        def pool(fn, reads, writes):
            S.op("gpsimd", fn, reads, writes)

        def pe(fn, reads, writes, cls):
            S.op("tensor", fn, reads, writes, pe_class=cls)

        def mm(out, lhsT, rhs, reads, writes, start=True, stop=True, tp=None, fifo=False):
            K_, M_ = lhsT.shape[0], int(np.prod(lhsT.shape[1:]))
            if tp is None:
                cls = "F" if K_ == 128 else "G%d" % K_
            elif M_ == 64 and K_ == 64:
                cls = "D" if tp[0] == tp[1] else "X"
            else:
                cls = "R%d_%d_%d" % (tp[0], tp[1], M_)
            pe(lambda e: e.matmul(out, lhsT=lhsT, rhs=rhs, start=start, stop=stop, tile_position=tp), reads, writes, cls)

        def tr(out, in_, ident, reads, writes, fifo=False):
            pe(lambda e: e.transpose(out=out, in_=in_, identity=ident), reads, writes, "F")

        def dbg(name, ap, keys, shape, dt=F32):
            if name in dbg_names:
                d = dout("dbg_" + name, shape, dt)
                dma(d, ap, reads=keys)

        def rstd_from(dst, src, scale, eps, keys_src, keys_dst):
            act(lambda e: e.activation(out=dst, in_=src, func=AF.Ln, bias=eps, scale=scale), keys_src, keys_dst)
            act(lambda e: e.activation(out=dst, in_=dst, func=AF.Exp, scale=-0.5), keys_dst, keys_dst)

        try:
            layers = [0, 1] if fused else [0]
            for l in layers:
                Lx = (lambda ap, l=l: ap[l]) if fused else (lambda ap: ap)
                if l == 0:
                    dma(cst[:], cst_d, writes=["cst"])
                    dma(fnw_bc[:], fnw_d.partition_broadcast(128), writes=["fnw"], eng="gpsimd")
                    if not fused:
                        dma(use[:], use_d, writes=["use"])
                dma(pc[:], Lx(pcol), writes=["pc"])
                dma(rksel[:], Lx(rksel_d), writes=["rksel"])
                vec(lambda e: e.tensor_copy(out=rkselb[:], in_=rksel[:]), ["rksel"], ["rkselb"])
                dma(lrw[:], Lx(lrw_d), writes=["lrw"])
                vec(lambda e: e.tensor_copy(out=lrwb[:], in_=lrw[:]), ["lrw"], ["lrwb"])
                dma(gup[:], Lx(gup_d), writes=["gup"])
                dma(lnw_bc[:], Lx(lnw_d).partition_broadcast(128), writes=["lnw"], eng="gpsimd")
                dma(lnb_bc[:], Lx(lnb_d).partition_broadcast(128), writes=["lnb"], eng="gpsimd")
                dma(gnw_bc[:], Lx(gnw_d).partition_broadcast(128), writes=["gnw"], eng="gpsimd")
                if l == 0:
                    vec(lambda e: e.tensor_copy(out=identb[:], in_=identf), ["cst"], ["identb"])
                    vec(lambda e: e.tensor_copy(out=bonesb[:], in_=blockones), ["cst"], ["bonesb"])
                vec(lambda e: e.tensor_scalar(out=negb[:], in0=pc[:, 16:18], scalar1=-1.0, scalar2=None, op0=ALU.mult), ["pc"], ["negb"])
                vec(lambda e: e.tensor_scalar(out=omka[:], in0=pc[:, 12:16], scalar1=-1.0, scalar2=1.0, op0=ALU.mult, op1=ALU.add), ["pc"], ["omka"])
                W0 = lambda g: pc[:, g:g + 1]
                A0 = lambda g: pc[:, 4 + g:5 + g]
                KK = lambda g: pc[:, 8 + g:9 + g]
                KA = lambda g: pc[:, 12 + g:13 + g]
                NW = lambda kc: pc[:, 20 + kc:21 + kc]

                ck("setup")
                mu_bc = Fpool[:, 0:1664]
                omu_bc = Fpool[:, 1664:3328]
                stg = [Fpool[:, 3328:4992], Fpool[:, 4992:6656]]
                kmu, komu = fk(0, 1664), fk(1664, 3328)
                kst = [fk(3328, 4992), fk(4992, 6656)]
                dma(mu_bc, Lx(mu_d).partition_broadcast(128), writes=kmu, eng="gpsimd")
                vec(lambda e: e.tensor_scalar(out=omu_bc, in0=mu_bc, scalar1=-1.0, scalar2=1.0, op0=ALU.mult, op1=ALU.add), kmu, komu)
                si = 0
                for kc in range(8):
                    rows = slice(kc * 128, (kc + 1) * 128)
                    s_, ks_ = stg[si % 2], kst[si % 2]; si += 1
                    dma(s_[:, 0:GLA_IN], Lx(w_in)[rows, 0:GLA_IN], writes=ks_)
                    act(lambda e, s_=s_, kc=kc: e.activation(out=w1b[:, kc, 0:GLA_IN], in_=s_[:, 0:GLA_IN], func=AF.Copy, scale=NW(kc)),
                        ks_ + ["pc"], ["w1b"])
                    s_, ks_ = stg[si % 2], kst[si % 2]; si += 1
                    dma(s_[:, 0:SHIFT_W], Lx(w_in)[rows, GLA_IN:GLA_IN + SHIFT_W], writes=ks_)
                    pool(lambda e, s_=s_, kc=kc: e.tensor_scalar(out=s_[:, 0:SHIFT_W], in0=s_[:, 0:SHIFT_W], scalar1=NW(kc), scalar2=None, op0=ALU.mult),
                         ks_ + ["pc"], ks_)
                    vec(lambda e, s_=s_, kc=kc: e.tensor_tensor(out=w2b[:, kc, :], in0=s_[:, 0:SHIFT_W], in1=mu_bc, op=ALU.mult), ks_ + kmu, ["w2b"])
                    pool(lambda e, s_=s_, kc=kc: e.tensor_tensor(out=w1b[:, kc, GLA_IN:GLA_IN + SHIFT_W], in0=s_[:, 0:SHIFT_W], in1=omu_bc, op=ALU.mult),
                         ks_ + komu, ["w1b"])
                    s_, ks_ = stg[si % 2], kst[si % 2]; si += 1
                    dma(s_[:, 0:512], Lx(w_in)[rows, GLA_IN + SHIFT_W:IN_W], writes=ks_)
                    act(lambda e, s_=s_, kc=kc: e.activation(out=w1b[:, kc, GLA_IN + SHIFT_W:IN_W], in_=s_[:, 0:512], func=AF.Copy, scale=NW(kc)),
                        ks_ + ["pc"], ["w1b"])
                    s_, ks_ = stg[si % 2], kst[si % 2]; si += 1
                    dma(s_[:, 0:D], Lx(w_out)[rows, :], writes=ks_)
                    vec(lambda e, s_=s_, kc=kc: e.tensor_copy(out=wob[:, kc, :], in_=s_[:, 0:D]), ks_, ["wob"])

                ck("wprep")
                Mst = F(0)[:, 0:256]
                Sst = F(0)[:, 256:512]
                kF0 = ["F0"]
                vec(lambda e: e.memset(Mst, 0.0), [], kF0)
                vec(lambda e: e.memset(Sst, 0.0), [], kF0)
                for j in (range(8) if not fused else []):
                    bM, bPhi, bS = F(1)[:, 0:256], F(1)[:, 256:512], F(2)[:, 0:256]
                    bD, dM, dS_ = F(2)[:, 256:258], F(3)[:, 0:256], F(3)[:, 256:512]
                    dma(bM, sumM_d[j], writes=["F1"])
                    dma(bPhi, sumPhi_d[j], writes=["F1"])
                    dma(bS, sumS_d[j], writes=["F2"])
                    dma(bD, sumD_d[j], writes=["F2"])
                    for g in range(4):
                        for hp in range(2):
                            p0 = hp * 64
                            mm(PS[0][p0:p0 + 64, g * 64:(g + 1) * 64], bPhi[p0:p0 + 64, g * 64:(g + 1) * 64],
                               Mst[p0:p0 + 64, g * 64:(g + 1) * 64], ["F1"] + kF0, pk(0), tp=(p0, p0))
                    vec(lambda e: e.tensor_tensor(out=dM, in0=PS[0][:, 0:256], in1=bM, op=ALU.add), pk(0) + ["F1"], ["F3"])
                    vec(lambda e: e.tensor_tensor(out=dM, in0=dM, in1=Mst, op=ALU.subtract), ["F3"] + kF0, ["F3"])
                    vec(lambda e, j=j: e.scalar_tensor_tensor(out=Mst, in0=dM, scalar=use[:, j:j + 1], in1=Mst, op0=ALU.mult, op1=ALU.add),
                        ["F3", "use"] + kF0, kF0)
                    for grp in range(2):
                        vec(lambda e, grp=grp: e.scalar_tensor_tensor(out=dS_[:, grp * 128:(grp + 1) * 128], in0=Sst[:, grp * 128:(grp + 1) * 128],
                                                                      scalar=bD[:, grp:grp + 1], in1=bS[:, grp * 128:(grp + 1) * 128],
                                                                      op0=ALU.mult, op1=ALU.add), ["F2"] + kF0, ["F3"])
                    vec(lambda e: e.tensor_tensor(out=dS_, in0=dS_, in1=Sst, op=ALU.subtract), ["F3"] + kF0, ["F3"])
                    vec(lambda e, j=j: e.scalar_tensor_tensor(out=Sst, in0=dS_, scalar=use[:, j:j + 1], in1=Sst, op0=ALU.mult, op1=ALU.add),
                        ["F3", "use"] + kF0, kF0)
                for g in range(4):
                    vec(lambda e, g=g: e.tensor_copy(out=A32[:, g, 0:64], in_=Mst[:, g * 64:(g + 1) * 64]), kF0, ["A32"])
                    vec(lambda e, g=g: e.tensor_copy(out=A32[:, g, 64:128], in_=identblk), ["cst"], ["A32"])
                vec(lambda e: e.tensor_copy(out=Ab[:], in_=A32[:]), ["A32"], ["Ab"])
                vec(lambda e: e.tensor_copy(out=S32[:].rearrange("p a b -> p (a b)"), in_=Sst), kF0, ["S32"])
                vec(lambda e: e.tensor_copy(out=Sb[:], in_=S32[:]), ["S32"], ["Sb"])
                vec(lambda e: e.memset(Dc[:], 1.0), [], ["Dc"])

                ck("fold")
                def load_norm(buf, key, src_rows, nrows, rkeys=()):
                    dma(buf[0:nrows, :], src_rows, reads=list(rkeys), writes=[key])
                    act(lambda e: e.activation(out=hb[:], in_=buf[:], func=AF.Square, scale=1.0 / 32.0, accum_out=st1[:, 0:1]),
                        [key], ["hb", "st1"])
                    rstd_from(st1[:, 1:2], st1[:, 0:1], 1.0, 1e-6, ["st1"], ["st1b"])
                    vec(lambda e: e.tensor_scalar(out=hb[:], in0=buf[:], scalar1=st1[:, 1:2], scalar2=None, op0=ALU.mult), [key, "st1b"], ["hb"])

                def transposes():
                    for k in range(8):
                        tr(PSb(0)[:, k * 128:(k + 1) * 128], hb[:, k * 128:(k + 1) * 128], identb[:], ["hb", "identb"], pk(0))

                def transposes_evac():
                    transposes()
                    pool(lambda e: e.tensor_copy(out=hT[:, :, 0:1], in_=hT[:, :, 128:129]), ["hT"], ["hT"])
                    vec(lambda e: e.tensor_copy(out=hT[:, :, 1:129], in_=PSb(0).rearrange("p (k t) -> p k t", t=128)), pk(0), ["hT"])

                def tile_src(tt):
                    if l == 0:
                        return x_sh[1 + tt * 128:1 + (tt + 1) * 128, :], ()
                    return x1s[tt * 128:(tt + 1) * 128, :], ["x1s%d" % tt]

                if l == 0:
                    vec(lambda e: e.memset(xts[0][:], 0.0), [], ["xt0"])
                    load_norm(xts[0], "xt0", x_sh[0:1, :], 1)
                    transposes()
                    vec(lambda e: e.tensor_copy(out=hT[:, :, 128:129], in_=PSb(0).rearrange("p (k t) -> p k t", t=128)[:, :, 0:1]), pk(0), ["hT"])
                else:
                    vec(lambda e: e.memset(hT[:, :, 128:129], 0.0), [], ["hT"])

                ck("halo")
                GHC = [(g, hp, cc) for same in (True, False) for g in range(4) for hp in range(2) for cc in range(2) if (hp == cc) == same]
                HC = [(h, cc) for same in (True, False) for h in range(4) for cc in range(2) if ((h % 2) == cc) == same]
                src0, rk0 = tile_src(0)
                load_norm(xts[0], "xt0", src0, 128, rk0)
                transposes_evac()
                def back_a(t, xt, kx):
                    ys, sq = F(6), F(7)
                    act(lambda e: e.copy(out=sq, in_=PS[2][:]), pk(2), ["F7"])
                    vec(lambda e: e.tensor_tensor(out=ys, in0=PS[3][:], in1=sq, op=ALU.add), pk(3) + ["F7"], ["F6"])
                    ys3 = ys.rearrange("p (h v) -> p h v", v=64)
                    sq3 = sq.rearrange("p (h v) -> p h v", v=64)
                    pool(lambda e: e.tensor_tensor(out=sq, in0=ys, in1=ys, op=ALU.mult), ["F6"], ["F7"])
                    s1, s2, mean, msq, var, rstdg = gn[:, 0:8], gn[:, 8:16], gn[:, 16:24], gn[:, 24:32], gn[:, 32:40], gn[:, 8:16]
                    vec(lambda e: e.reduce_sum(out=s1, in_=ys3, axis=AX.X), ["F6"], ["gn_s1"])
                    vec(lambda e: e.reduce_sum(out=s2, in_=sq3, axis=AX.X), ["F7"], ["gn_s2"])
                    vec(lambda e: e.tensor_scalar(out=mean, in0=s1, scalar1=1.0 / 64, scalar2=None, op0=ALU.mult), ["gn_s1"], ["gn_mean"])
                    vec(lambda e: e.tensor_tensor(out=msq, in0=mean, in1=mean, op=ALU.mult), ["gn_mean"], ["gn_msq"])
                    vec(lambda e: e.scalar_tensor_tensor(out=var, in0=s2, scalar=1.0 / 64, in1=msq, op0=ALU.mult, op1=ALU.subtract), ["gn_s2", "gn_msq"], ["gn_var"])
                    rstd_from(rstdg, var, 1.0, 64e-5, ["gn_var"], ["gn_s2"])
                    bc8 = lambda ap: ap.unsqueeze(2).to_broadcast([128, 8, 64])
                    vec(lambda e: e.tensor_tensor(out=ys3, in0=ys3, in1=bc8(mean), op=ALU.subtract), ["F6", "gn_mean"], ["F6"])
                    vec(lambda e: e.tensor_tensor(out=ys3, in0=ys3, in1=bc8(rstdg), op=ALU.mult), ["F6", "gn_s2"], ["F6"])
                    pool(lambda e: e.tensor_tensor(out=ys, in0=ys, in1=lnw_bc[:], op=ALU.mult), ["F6", "lnw"], ["F6"])
                    pool(lambda e: e.tensor_tensor(out=ys, in0=ys, in1=lnb_bc[:], op=ALU.add), ["F6", "lnb"], ["F6"])
                    vec(lambda e: e.tensor_tensor(out=sq3, in0=Vtok[:].rearrange("p g (h v) -> p (g h) v", v=64), in1=bc8(st1[:, 8:16]), op=ALU.mult),
                        ["Vtok", "bcoef"], ["F7"])
                    vec(lambda e: e.tensor_tensor(out=ys, in0=ys, in1=sq, op=ALU.add), ["F6", "F7"], ["F6"])
                    vec(lambda e: e.tensor_tensor(out=merged[:, 512:1024], in0=ys, in1=rsil[:], op=ALU.mult), ["F6", "rsil"], ["merged_r"])


                def back_b(t, xt, kx):
                    ck("gla")
                    for k in range(8):
                        tr(PSb(2)[:, k * 128:(k + 1) * 128], merged[:, k * 128:(k + 1) * 128], identb[:],
                           ["merged_g" if k < 4 else "merged_r", "identb"], pk(2))
                    vec(lambda e: e.tensor_copy(out=mT[:].rearrange("p a b -> p (a b)"), in_=PSb(2)), pk(2), ["mT"])
                    for half in range(2):
                        bank = 4 + half
                        for k in range(8):
                            mm(PS[bank][:], mT[:, k, :], wob[:, k, half * 512:(half + 1) * 512], ["mT", "wob"], pk(bank), start=(k == 0), stop=(k == 7), fifo=True)
                        vec(lambda e, half=half, bank=bank, xt=xt: e.tensor_tensor(out=xt[:, half * 512:(half + 1) * 512], in0=PS[bank][:],
                                                                            in1=xt[:, half * 512:(half + 1) * 512], op=ALU.add), pk(bank) + [kx], [kx])
                    if not fused:
                        dma(x_out[t * 128:(t + 1) * 128, :], xt[:], reads=[kx])
                    elif l == 0:
                        dma(x1s[t * 128:(t + 1) * 128, :], xt[:], reads=[kx], writes=["x1s%d" % t])
                    if (not fused) or l == layers[-1]:
                        act(lambda e, xt=xt: e.activation(out=on[:], in_=xt[:], func=AF.Square, scale=1.0 / 32.0, accum_out=st1[:, 6:7]), [kx], ["on", "fst"])
                        rstd_from(st1[:, 7:8], st1[:, 6:7], 1.0, 1e-6, ["fst"], ["fst2"])
                        vec(lambda e, xt=xt: e.scalar_tensor_tensor(out=on[:], in0=xt[:], scalar=st1[:, 7:8], in1=fnw_bc[:], op0=ALU.mult, op1=ALU.mult),
                            [kx, "fst2", "fnw"], ["on"])
                        dma(y_out[t * 128:(t + 1) * 128, :], on[:], reads=["on"])

                pending = None
                for t in range(NT):
                    last = (t == NT - 1)
                    xt = xts[t % 3]
                    kx = "xt%d" % (t % 3)
                    if pending is not None:
                        back_a(*pending)
                    if not last:
                        srcn, rkn = tile_src(t + 1)
                        load_norm(xts[(t + 1) % 3], "xt%d" % ((t + 1) % 3), srcn, 128, rkn)
                    cur = lambda k: hT[:, k, 1:129]
                    prv = lambda k: hT[:, k, 0:128]

                    def proj_fm_shift(dst_ps, col0, ncols, pskey):
                        for k in range(8):
                            mm(dst_ps, w1b[:, k, GLA_IN + col0:GLA_IN + col0 + ncols], cur(k), ["w1b", "hT"], pskey, start=(k == 0), stop=False, fifo=True)
                            mm(dst_ps, w2b[:, k, col0:col0 + ncols], prv(k), ["w2b", "hT"], pskey, start=False, stop=(k == 7), fifo=True)

                    def proj_fm(dst_ps, col0, ncols, pskey):
                        for k in range(8):
                            mm(dst_ps, w1b[:, k, col0:col0 + ncols], cur(k), ["w1b", "hT"], pskey, start=(k == 0), stop=(k == 7), fifo=True)

                    def proj_tm(dst_ps, col0, pskey):
                        for k in range(8):
                            mm(dst_ps, cur(k), w1b[:, k, col0:col0 + 512], ["w1b", "hT"], pskey, start=(k == 0), stop=(k == 7), fifo=True)

                    rT, rkT, rvT, sw, aT, cs_, Ea, Eb, kk_, kT2, bT, tmp = [F(i) for i in range(12)]
                    for g in range(4):
                        proj_fm_shift(PS[1][:, g * 128:(g + 1) * 128], 0 * 512 + g * 128, 128, pk(1))
                    act(lambda e: e.copy(out=rT, in_=PS[1][:]), pk(1), ["F0"])
                    for g in range(4):
                        proj_fm_shift(PS[2][:, g * 128:(g + 1) * 128], 1 * 512 + g * 128, 128, pk(2))
                    act(lambda e: e.copy(out=rkT, in_=PS[2][:]), pk(2), ["F1"])
                    for g in range(4):
                        gs = slice(g * 128, (g + 1) * 128)
                        vec(lambda e, g=g, gs=gs: e.tensor_scalar(out=kk_[:, gs], in0=rkT[:, gs], scalar1=KK(g), scalar2=None, op0=ALU.mult), ["F1", "pc"], ["F8"])
                    pool(lambda e: e.tensor_tensor(out=rvb[:].rearrange("p a b -> p (a b)"), in0=kk_, in1=kk_, op=ALU.mult), ["F8"], ["rvb"])
                    for g in range(4):
                        proj_fm_shift(PS[3][:, g * 128:(g + 1) * 128], 2 * 512 + g * 128, 128, pk(3))
                    act(lambda e: e.copy(out=rvT, in_=PS[3][:]), pk(3), ["F2"])
                    proj_fm_shift(PS[4][:, 0:128], 1536, 128, pk(4))
                    proj_fm(PS[4][0:16, 128:256], 1024, 16, pk(4))
                    proj_fm(PS[4][:, 256:384], 0, 128, pk(4))
                    proj_fm(PS[4][:, 384:512], 128, 128, pk(4))
                    act(lambda e: e.activation(out=wlalb[0:64, :], in_=PS[4][0:64, 0:128], func=AF.Tanh), pk(4), ["wlalb"])
                    act(lambda e: e.copy(out=wlalb[64:128, :], in_=PS[4][64:128, 0:128]), pk(4), ["wlalb"])
                    act(lambda e: e.copy(out=glrT[:], in_=PS[4][0:16, 128:256]), pk(4), ["glrT"])
                    act(lambda e: e.copy(out=qT[:].rearrange("p a b -> p (a b)"), in_=PS[4][:, 256:512]), pk(4), ["qT"])
                    proj_fm(PS[5][:, 0:128], 256, 128, pk(5))
                    proj_fm(PS[5][:, 128:256], 384, 128, pk(5))
                    act(lambda e: e.copy(out=kT[:].rearrange("p a b -> p (a b)"), in_=PS[5][:, 0:256]), pk(5), ["kT"])
                    for g in range(4):
                        mm(PS[1][:, g * 128:(g + 1) * 128], lrwb[0:64, g * 128:(g + 1) * 128], wlalb[0:64, :], ["lrwb", "wlalb"], pk(1), tp=(0, 0))
                    for g in range(4):
                        mm(PS[2][:, g * 128:(g + 1) * 128], lrwb[64:128, g * 128:(g + 1) * 128], wlalb[64:128, :], ["lrwb", "wlalb"], pk(2), tp=(64, 0))
                    for g in range(4):
                        gs = slice(g * 128, (g + 1) * 128)
                        act(lambda e, g=g, gs=gs: e.activation(out=sw[:, gs], in_=PS[1][:, gs], func=AF.Sigmoid, bias=W0(g)), pk(1) + ["pc"], ["F3"])
                    vec(lambda e: e.tensor_tensor_scan(out=cs_, data0=scanmask, data1=sw, initial=0.0, op0=ALU.mult, op1=ALU.add), ["cst", "F3"], ["F5"])
                    pool(lambda e: e.tensor_tensor(out=F(12), in0=cs_, in1=sw, op=ALU.subtract), ["F5", "F3"], ["F12"])
                    vec(lambda e: e.tensor_tensor(out=tmp.rearrange("p (a b) -> p a b", b=64),
                                                  in0=cs_.rearrange("p (a b) -> p a b", b=64)[:, :, 63:64].to_broadcast([128, 8, 64]),
                                                  in1=cs_.rearrange("p (a b) -> p a b", b=64), op=ALU.subtract), ["F5"], ["F11"])
                    for g in range(4):
                        gs = slice(g * 128, (g + 1) * 128)
                        act(lambda e, g=g, gs=gs: e.activation(out=aT[:, gs], in_=PS[2][:, gs], func=AF.Sigmoid, bias=A0(g)), pk(2) + ["pc"], ["F4"])
                    proj_tm(PS[6][:], 512, pk(6))
                    vec(lambda e: e.tensor_copy(out=gvb[:], in_=PS[6][:]), pk(6), ["gvb"])
                    for g in range(4):
                        gs = slice(g * 128, (g + 1) * 128)
                        mm(PS[3][:, gs], bonesb[:], rvb[:, g, :], ["bonesb", "rvb"], pk(3))
                    rstd_from(bT, PS[3][:], 1.0, 1e-20, pk(3), ["F10"])
                    vec(lambda e: e.tensor_tensor(out=kk_, in0=kk_, in1=bT, op=ALU.mult), ["F8", "F10"], ["F8"])
                    for g in range(4):
                        gs = slice(g * 128, (g + 1) * 128)
                        vec(lambda e, g=g, gs=gs: e.tensor_scalar(out=kT2[:, gs], in0=aT[:, gs], scalar1=KA(g), scalar2=omka[:, g:g + 1], op0=ALU.mult, op1=ALU.add),
                            ["F4", "pc", "omka"], ["F9"])
                    vec(lambda e: e.tensor_tensor(out=kT2, in0=kT2, in1=rkT, op=ALU.mult), ["F9", "F1"], ["F9"])
                    pool(lambda e: e.tensor_tensor(out=bT, in0=kk_, in1=aT, op=ALU.mult), ["F8", "F4"], ["F10"])
                    proj_tm(PS[7][:], 1040, pk(7))
                    act(lambda e: e.activation(out=gsil[:], in_=PS[7][:], func=AF.Silu), pk(7), ["gsil"])
                    proj_tm(PS[0][:], GLA_IN + SHIFT_W, pk(0))
                    act(lambda e: e.activation(out=rsil[:], in_=PS[0][:], func=AF.Silu), pk(0), ["rsil"])
                    if not last:
                        transposes_evac()
                    if pending is not None:
                        back_b(*pending)
                        pending = None

                    ck("proj")
                    v2 = lambda ap: ap.rearrange("p (a b) -> p a b", b=128)
                    spg, csg = v2(F(1)[:, 0:256]), v2(F(1)[:, 256:512])
                    Eg = [v2(F(4)[:, 0:256]), v2(F(4)[:, 256:512]), v2(F(3)[:, 0:256])]
                    f2 = lambda t_: t_.rearrange("p a b -> p (a b)")
                    for grp in range(2):
                        mm(PS[5][:, grp * 128:(grp + 1) * 128], gup[:, grp * 128:(grp + 1) * 128], glrT[:], ["gup", "glrT"], pk(5))
                    cs3 = cs_.rearrange("p (a b) -> p a b", b=64)
                    arb_a = arb[:, :, :, 0, :].rearrange("p g c i -> p (g c) i")
                    arb_r = arb[:, :, :, 1, :].rearrange("p g c i -> p (g c) i")
                    v3 = lambda ap: ap.rearrange("p (a b) -> p a b", b=64)
                    b3 = lambda ap: ap[:].rearrange("p g (c i) -> p (g c) i", i=64)
                    Dpv = F(12)
                    act(lambda e: e.activation(out=Dpv, in_=Dpv, func=AF.Exp, scale=-KDEC), ["F12"], ["F12"])
                    act(lambda e: e.activation(out=tmp, in_=tmp, func=AF.Exp, scale=-KDEC), ["F11"], ["F11"])
                    act(lambda e: e.activation(out=Ea, in_=cs_, func=AF.Exp, scale=-KDEC), ["F5"], ["F6"])
                    act(lambda e: e.activation(out=Eb, in_=cs_, func=AF.Exp, scale=KDEC), ["F5"], ["F7"])
                    vec(lambda e: e.scalar_tensor_tensor(out=atb[:].rearrange("p a b -> p (a b)"), in0=kk_, scalar=-1.0, in1=Dpv, op0=ALU.mult, op1=ALU.mult),
                        ["F8", "F12"], ["atb"])
                    pool(lambda e: e.tensor_tensor(out=Khb[:].rearrange("p a b -> p (a b)"), in0=kT2, in1=tmp, op=ALU.mult), ["F9", "F11"], ["Khb"])
                    vec(lambda e: e.tensor_tensor(out=Bhb[:].rearrange("p a b -> p (a b)"), in0=bT, in1=tmp, op=ALU.mult), ["F10", "F11"], ["Bhb"])
                    pool(lambda e: e.tensor_copy(out=arb_a, in_=b3(atb)), ["atb"], ["arb_a"])
                    vec(lambda e: e.tensor_tensor(out=arb_r, in0=v3(rT), in1=v3(Ea), op=ALU.mult), ["F0", "F6"], ["arb_r"])
                    pool(lambda e: e.tensor_tensor(out=ktb[:].rearrange("p a b -> p (a b)"), in0=kT2, in1=Eb, op=ALU.mult), ["F9", "F7"], ["ktb"])
                    vec(lambda e: e.tensor_tensor(out=btb[:].rearrange("p a b -> p (a b)"), in0=bT, in1=Eb, op=ALU.mult), ["F10", "F7"], ["btb"])
                    pool(lambda e: e.tensor_copy(out=gamc[:].rearrange("p g c -> p (g c)"), in_=v3(Ea)[:, :, 63]), ["F6"], ["gamc"])

                    for grp in range(2):
                        act(lambda e, grp=grp: e.activation(out=spg[:, grp, :], in_=PS[5][:, grp * 128:(grp + 1) * 128], func=AF.Exp, scale=-1.0, bias=negb[:, grp:grp + 1]),
                            pk(5) + ["negb"], ["F1"])
                    act(lambda e: e.activation(out=f2(spg), in_=f2(spg), func=AF.Ln, bias=1.0), ["F1"], ["F1"])
                    vec(lambda e: e.tensor_tensor_scan(out=f2(csg), data0=scanmask[:, 0:256], data1=f2(spg), initial=0.0, op0=ALU.mult, op1=ALU.add), ["cst", "F1"], ["F1"])
                    act(lambda e: e.activation(out=f2(Eg[0]), in_=f2(csg), func=AF.Exp, scale=-1.0 / 16), ["F1"], ["F4"])
                    act(lambda e: e.activation(out=f2(Eg[1]), in_=f2(csg), func=AF.Exp, scale=1.0 / 16), ["F1"], ["F4"])
                    csg3 = f2(csg).rearrange("p (a b) -> p a b", b=64)
                    pool(lambda e: e.tensor_tensor(out=f2(Eg[2]).rearrange("p (a b) -> p a b", b=64), in0=csg3[:, :, 63:64].to_broadcast([128, 4, 64]), in1=csg3, op=ALU.subtract),
                         ["F1"], ["F3"])
                    act(lambda e: e.activation(out=f2(Eg[2]), in_=f2(Eg[2]), func=AF.Exp, scale=-1.0 / 16), ["F3"], ["F3"])
                    vec(lambda e: e.scalar_tensor_tensor(out=f2(qib[:]), in0=f2(qT[:]), scalar=0.125, in1=f2(Eg[0]), op0=ALU.mult, op1=ALU.mult), ["qT", "F4"], ["qib"])
                    vec(lambda e: e.tensor_tensor(out=f2(kib[:]), in0=f2(kT[:]), in1=f2(Eg[1]), op=ALU.mult), ["kT", "F4"], ["kib"])
                    pool(lambda e: e.tensor_tensor(out=f2(ksb[:]), in0=f2(kT[:]), in1=f2(Eg[2]), op=ALU.mult), ["kT", "F3"], ["ksb"])
                    ck("elem")
                    for (src, dst, key, dkey, bank) in [(atb, atok, "atb", "atok", 1), (Bhb, Bhtok, "Bhb", "Bhtok", 2), (Khb, Khtok, "Khb", "Khtok", 4)]:
                        for g in range(4):
                            tr(PSb(bank)[:, g * 128:(g + 1) * 128], src[:, g, :], identb[:], [key, "identb"], pk(bank))
                        act(lambda e, dst=dst, bank=bank: e.copy(out=dst[:].rearrange("p a b -> p (a b)"), in_=PSb(bank)[:, 0:512]),
                            pk(bank), [dkey])
                    pool(lambda e: e.tensor_copy(out=rvb[:].rearrange("p a b -> p (a b)"), in_=rvT), ["F2"], ["rvb"])
                    for g in range(4):
                        tr(PSb(5)[:, g * 128:(g + 1) * 128], rvb[:, g, :], identb[:], ["rvb", "identb"], pk(5))
                    act(lambda e: e.copy(out=Vtok[:].rearrange("p a b -> p (a b)"), in_=PSb(5)[:, 0:512]), pk(5), ["Vtok"])

                    pool(lambda e: e.tensor_tensor(out=rvb[:].rearrange("p a b -> p (a b)"), in0=rT, in1=kT2, op=ALU.mult), ["F0", "F9"], ["rvb"])
                    for g in range(4):
                        mm(PS[5][:, g * 2:(g + 1) * 2], rvb[:, g, :], rkselb[:, g * 2:(g + 1) * 2], ["rvb", "rkselb"], pk(5))
                    act(lambda e: e.copy(out=st1[:, 8:16], in_=PS[5][:, 0:8]), pk(5), ["bcoef"])
                    ck("trans")
                    for (g, hp, cc) in GHC:
                        gh = g * 2 + hp
                        p0, c0 = hp * 64, cc * 64
                        tk = slice(cc * 64, (cc + 1) * 64)
                        rhs_ar = arb[p0:p0 + 64, g, cc, :, :].rearrange("p a i -> p (a i)")
                        bN, bK, cg = (6 if gh < 4 else 7), (0 if gh < 4 else 1), (gh % 4) * 128
                        mm(PS[bN][c0:c0 + 64, cg:cg + 128], btb[p0:p0 + 64, g, tk], rhs_ar, ["btb", "arb_a", "arb_r"], pk(bN), tp=(p0, c0))
                        mm(PS[bK][c0:c0 + 64, cg:cg + 128], ktb[p0:p0 + 64, g, tk], rhs_ar, ["ktb", "arb_a", "arb_r"], pk(bK), tp=(p0, c0))
                        mm(PS[2][c0:c0 + 64, gh * 64:gh * 64 + 64], atb[p0:p0 + 64, g, tk], btb[p0:p0 + 64, g, tk], ["atb", "btb"], pk(2), tp=(p0, c0))
                    mNA4 = maskNA.unsqueeze(1).to_broadcast([128, 4, 128])
                    mSL8 = maskSL.unsqueeze(1).to_broadcast([128, 8, 64])
                    p4 = lambda ap: ap.rearrange("p (a b) -> p a b", b=128)
                    vec(lambda e: e.tensor_tensor(out=NAb[:, 0:4, :], in0=p4(PS[6][:]), in1=mNA4, op=ALU.mult), pk(6) + ["cst"], ["NAb"])
                    vec(lambda e: e.tensor_tensor(out=NAb[:, 4:8, :], in0=p4(PS[7][:]), in1=mNA4, op=ALU.mult), pk(7) + ["cst"], ["NAb"])
                    vec(lambda e: e.tensor_tensor(out=KAb[:, 0:4, :], in0=p4(PS[0][:]), in1=mNA4, op=ALU.mult), pk(0) + ["cst"], ["KAb"])
                    vec(lambda e: e.tensor_tensor(out=KAb[:, 4:8, :], in0=p4(PS[1][:]), in1=mNA4, op=ALU.mult), pk(1) + ["cst"], ["KAb"])
                    vec(lambda e: e.tensor_tensor(out=Lb[0][:], in0=PS[2][:].rearrange("p (a b) -> p a b", b=64), in1=mSL8, op=ALU.mult), pk(2) + ["cst"], ["Lb0h0", "Lb0h1"])

                    for grp in range(2):
                        tr(PSb(6)[:, grp * 128:(grp + 1) * 128], ksb[:, grp, :], identb[:], ["ksb", "identb"], pk(6))
                    act(lambda e: e.copy(out=f2(kstok[:]), in_=PSb(6)[:, 0:256]), pk(6), ["kstok"])
                    for (h, cc) in HC:
                        grp, hp = h // 2, h % 2
                        p0 = hp * 64
                        c0 = cc * 64
                        tk = slice(c0, c0 + 64)
                        mm(PS[7][c0:c0 + 64, h * 64:(h + 1) * 64], kib[p0:p0 + 64, grp, tk], qib[p0:p0 + 64, grp, tk], ["kib", "qib"], pk(7), tp=(p0, c0))
                    vec(lambda e: e.tensor_tensor(out=scb[:], in0=PS[7][:, 0:256].rearrange("p (a b) -> p a b", b=64),
                                                  in1=maskUU.unsqueeze(1).to_broadcast([128, 4, 64]), op=ALU.mult), pk(7) + ["cst"], ["scb"])
                    ck("score")
                    for g in range(4):
                        for hp in range(2):
                            gh = g * 2 + hp
                            for cc in range(2):
                                c0 = cc * 64
                                mm(PS[3][c0:c0 + 64, gh * 64:gh * 64 + 64], KAb[c0:c0 + 64, gh, 0:64], Vtok[c0:c0 + 64, g, hp * 64:hp * 64 + 64],
                                   ["KAb", "Vtok"], pk(3), tp=(c0, c0))
                    pool(lambda e: e.tensor_copy(out=Xb[:, :, 0:64], in_=atok[:].rearrange("p g (h k) -> p (g h) k", k=64)), ["atok"], ["Xb0", "Xb1"])
                    act(lambda e: e.copy(out=Xb[:, :, 64:128], in_=PS[3][:].rearrange("p (a b) -> p a b", b=64)), pk(3), ["Xb0", "Xb1"])

                    ck("x0")
                    for lvl in range(6):
                        a_, b_ = lvl % 2, (lvl + 1) % 2
                        for hf in range(2):
                            ghs = range(hf * 4, hf * 4 + 4)
                            kX = "Xb%d" % hf
                            kN = "NAb" if lvl == 0 else "Nb%dh%d" % (a_, hf)
                            kL = "Lb%dh%d" % (a_, hf)
                            bank = 4 + hf
                            for gh in ghs:
                                for cc in range(2):
                                    c0 = cc * 64
                                    lhs = NAb[c0:c0 + 64, gh, 0:64] if lvl == 0 else Nb[a_][c0:c0 + 64, gh, :]
                                    mm(PS[bank][c0:c0 + 64, (gh % 4) * 128:(gh % 4) * 128 + 128], lhs, Xb[c0:c0 + 64, gh, :], [kN, kX], pk(bank), tp=(c0, c0))
                            if lvl < 5:
                                for gh in ghs:
                                    for cc in range(2):
                                        c0 = cc * 64
                                        lhsN = NAb[c0:c0 + 64, gh, 0:64] if lvl == 0 else Nb[a_][c0:c0 + 64, gh, :]
                                        Lp = Lb[a_][c0:c0 + 64, gh, :]
                                        gq = (gh % 4) * 64
                                        mm(PS[6 + hf][c0:c0 + 64, gq:gq + 64], Lp, lhsN, [kL, kN], pk(6 + hf), tp=(c0, c0))
                                        if lvl < 4:
                                            mm(PS[hf][c0:c0 + 64, gq:gq + 64], lhsN, Lp, [kL, kN], pk(hf), tp=(c0, c0))
                            hs = slice(hf * 4, hf * 4 + 4)
                            cs256 = slice(hf * 256, hf * 256 + 256)
                            vec(lambda e, hs=hs, bank=bank: e.tensor_tensor(out=Xb[:, hs, :], in0=p4(PS[bank][:]), in1=Xb[:, hs, :], op=ALU.add), pk(bank) + [kX], [kX])
                            if lvl < 5:
                                act(lambda e, b_=b_, hs=hs, hf=hf: e.copy(out=Nb[b_][:, hs, :].rearrange("p a b -> p (a b)"), in_=PS[6 + hf][:, 0:256]),
                                    pk(6 + hf), ["Nb%dh%d" % (b_, hf)])
                                if lvl < 4:
                                    act(lambda e, b_=b_, hs=hs, hf=hf: e.copy(out=Lb[b_][:, hs, :].rearrange("p a b -> p (a b)"), in_=PS[hf][:, 0:256]),
                                        pk(hf), ["Lb%dh%d" % (b_, hf)])

                    ck("neumann")
                    QTs = sw
                    for (g, hp, cc) in GHC:
                        gh = g * 2 + hp
                        if True:
                            if True:
                                p0, c0 = hp * 64, cc * 64
                                Wt = Xb[c0:c0 + 64, gh, 0:64]
                                Ut = Xb[c0:c0 + 64, gh, 64:128]
                                col = (g * 2 + cc) * 64
                                mm(PS[0][p0:p0 + 64, col:col + 64], Wt, Bhtok[c0:c0 + 64, g, p0:p0 + 64], ["Xb0", "Xb1", "Bhtok"], pk(0), tp=(c0, p0))
                                mm(PS[1][p0:p0 + 64, col:col + 64], Bhtok[c0:c0 + 64, g, p0:p0 + 64], Ut, ["Xb0", "Xb1", "Bhtok"], pk(1), start=True, stop=False, tp=(c0, p0))
                                mm(PS[1][p0:p0 + 64, col:col + 64], Khtok[c0:c0 + 64, g, p0:p0 + 64], Vtok[c0:c0 + 64, g, p0:p0 + 64], ["Khtok", "Vtok"], pk(1),
                                   start=False, stop=True, tp=(c0, p0))
                                mm(PS[2][p0:p0 + 64, col:col + 64], Wt, NAb[c0:c0 + 64, gh, 64:128], ["Xb0", "Xb1", "NAb"], pk(2), tp=(c0, p0))
                    pool(lambda e: e.tensor_tensor(out=dG[:].rearrange("p g c k -> p (g c) k"),
                                                   in0=identblk.unsqueeze(1).to_broadcast([128, 8, 64]),
                                                   in1=gamc[:].rearrange("p g c -> p (g c)").unsqueeze(2).to_broadcast([128, 8, 64]), op=ALU.mult),
                         ["cst", "gamc"], ["dG"])
                    vec(lambda e: e.tensor_tensor(out=Pb[:].rearrange("p g c k -> p (g c k)"), in0=PS[0][:], in1=dG[:].rearrange("p g c k -> p (g c k)"), op=ALU.add),
                        pk(0) + ["dG"], ["Pb"])
                    act(lambda e: e.copy(out=QTs, in_=PS[1][:]), pk(1), ["F3"])
                    vec(lambda e: e.tensor_tensor(out=GTb[:].rearrange("p g (c i) -> p (g c) i", i=64), in0=PS[2][:].rearrange("p (a b) -> p a b", b=64),
                                                  in1=arb_r, op=ALU.add), pk(2) + ["arb_r"], ["GTb"])

                    ck("pqg")
                    for cc in range(2):
                        c0 = cc * 64
                        kY = "ps3c%d" % cc
                        for g in range(4):
                            for hp in range(2):
                                gh = g * 2 + hp
                                p0 = hp * 64
                                yo = PS[3][c0:c0 + 64, gh * 64:gh * 64 + 64]
                                mm(yo, NAb[c0:c0 + 64, gh, 64:128], Xb[c0:c0 + 64, gh, 64:128], ["NAb", "Xb0", "Xb1"], [kY], start=True, stop=False, tp=(c0, c0))
                                mm(yo, KAb[c0:c0 + 64, gh, 64:128], Vtok[c0:c0 + 64, g, p0:p0 + 64], ["KAb", "Vtok"], [kY], start=False, stop=True, tp=(c0, c0))
                    for cc in range(2):
                        c0 = cc * 64
                        def gm(hp):
                            for g in range(4):
                                gh = g * 2 + hp
                                p0 = hp * 64
                                mm(PS[2][c0:c0 + 64, gh * 64:gh * 64 + 64], GTb[p0:p0 + 64, g, c0:c0 + 64], Ab[p0:p0 + 64, g, 0:64], ["GTb", "Ab"], pk(2), tp=(p0, c0))
                        gm(cc)
                        for g in range(4):
                            for hp in range(2):
                                p0 = hp * 64
                                mm(PS[4][p0:p0 + 64, g * 128:(g + 1) * 128], Pb[p0:p0 + 64, g, cc, :], Ab[p0:p0 + 64, g, :], ["Pb", "Ab"], pk(4), tp=(p0, p0))
                        gm(1 - cc)
                        q4 = QTs.rearrange("p (g c v) -> p g c v", c=2, v=64)
                        vec(lambda e, cc=cc, q4=q4: e.tensor_tensor(out=A32[:, :, 0:64], in0=p4(PS[4][:])[:, :, 0:64], in1=q4[:, :, cc, :], op=ALU.add),
                            pk(4) + ["F3"], ["A32"])
                        vec(lambda e: e.tensor_copy(out=A32[:, :, 64:128], in_=p4(PS[4][:])[:, :, 64:128]), pk(4), ["A32"])
                        pool(lambda e: e.tensor_copy(out=Ab[:], in_=A32[:]), ["A32"], ["Ab"])

                    ck("scan")
                    ck("repi")
                    for cc in range(2):
                        c0 = cc * 64
                        tk = slice(c0, c0 + 64)
                        kO = "ps0c%d" % cc
                        for h in range(4):
                            oo = PS[0][c0:c0 + 64, h * 128:(h + 1) * 128]
                            mm(oo, scb[c0:c0 + 64, h, :], gvb[c0:c0 + 64, h * 128:(h + 1) * 128], ["scb", "gvb"], [kO], tp=(c0, c0))
                        for hs in ((cc, cc + 2), (1 - cc, 3 - cc)):
                            for h in hs:
                                grp, hp = h // 2, h % 2
                                p0 = hp * 64
                                mm(PS[4][c0:c0 + 64, h * 128:(h + 1) * 128], qib[p0:p0 + 64, grp, tk], Sb[p0:p0 + 64, grp, :], ["qib", "Sb"], pk(4), tp=(p0, c0))
                            for h in hs:
                                grp, hp = h // 2, h % 2
                                p0 = hp * 64
                                mm(PS[1][p0:p0 + 64, grp * 128:(grp + 1) * 128], kstok[c0:c0 + 64, grp, p0:p0 + 64], gvb[c0:c0 + 64, h * 128:(h + 1) * 128],
                                   ["kstok", "gvb"], pk(1), tp=(c0, p0))
                        for grp in range(2):
                            dcol = Eg[0][:, grp, c0 + 63:c0 + 64]
                            vec(lambda e, grp=grp, dcol=dcol: e.scalar_tensor_tensor(out=S32[:, grp, :], in0=S32[:, grp, :], scalar=dcol,
                                                                                     in1=PS[1][:, grp * 128:(grp + 1) * 128], op0=ALU.mult, op1=ALU.add),
                                ["S32", "F4", "ps1"], ["S32"])
                        pool(lambda e: e.tensor_copy(out=Sb[:], in_=S32[:]), ["S32"], ["Sb"])
                        vec(lambda e, c0=c0: e.tensor_tensor(out=Dc[:], in0=Dc[:], in1=Eg[0][:, :, c0 + 63], op=ALU.mult), ["Dc", "F4"], ["Dc"])
                    os_, osq = kk_, kT2
                    act(lambda e: e.copy(out=osq, in_=PS[4][:]), pk(4), ["F9"])
                    vec(lambda e: e.tensor_tensor(out=os_, in0=PS[0][:], in1=osq, op=ALU.add), pk(0) + ["F9"], ["F8"])
                    pool(lambda e: e.tensor_tensor(out=osq, in0=os_, in1=os_, op=ALU.mult), ["F8"], ["F9"])
                    vec(lambda e: e.reduce_sum(out=st1[:, 2:6], in_=osq.rearrange("p (h v) -> p h v", v=128), axis=AX.X), ["F9"], ["gst"])
                    rstd_from(st1[:, 2:6], st1[:, 2:6], 1.0 / 128, 1e-6, ["gst"], ["gst"])
                    os3 = os_.rearrange("p (h v) -> p h v", v=128)
                    vec(lambda e: e.tensor_tensor(out=os3, in0=os3, in1=st1[:, 2:6].unsqueeze(2).to_broadcast([128, 4, 128]), op=ALU.mult), ["F8", "gst"], ["F8"])
                    pool(lambda e: e.tensor_tensor(out=os3, in0=os3, in1=gnw_bc[:].unsqueeze(1).to_broadcast([128, 4, 128]), op=ALU.mult), ["F8", "gnw"], ["F8"])
                    vec(lambda e: e.tensor_tensor(out=merged[:, 0:512], in0=os_, in1=gsil[:], op=ALU.mult), ["F8", "gsil"], ["merged_g"])

                    pending = (t, xt, kx)
                if pending is not None:
                    back_a(*pending)
                    back_b(*pending)
                    pending = None

        except _Stop:
            pass
        if not fused:
            dma(oA, A32[:].rearrange("p a b -> p (a b)"), reads=["A32"])
            dma(oS, S32[:].rearrange("p a b -> p (a b)"), reads=["S32"])
            dma(oD, Dc[:], reads=["Dc"])
        S.finish()
        S.emit()
    return nc, S


def _consts():
    c = np.zeros((128, 1024), np.float32)
    c[:, 0:128] = np.eye(128, dtype=np.float32)
    e64 = np.eye(64, dtype=np.float32)
    c[:, 128:192] = np.concatenate([e64, e64], 0)
    su = np.triu(np.ones((64, 64), np.float32), 1)
    uu = np.triu(np.ones((64, 64), np.float32), 0)
    c[:, 192:256] = np.concatenate([su, su], 0)
    c[:, 256:320] = np.concatenate([uu, uu], 0)
    c[:, 320:384] = np.concatenate([su.T, su.T], 0)
    bo = np.zeros((128, 128), np.float32)
    bo[0:64, 0:64] = 1.0
    bo[64:128, 64:128] = 1.0
    c[:, 384:512] = bo
    sm = np.ones((128, 512), np.float32)
    sm[:, ::64] = 0.0
    c[:, 512:1024] = sm
    return c


def _fm_cols(v512):
    return np.ascontiguousarray(v512.reshape(4, 128).T)


def _layer_params(inp, l):
    pcol = np.zeros((128, 32), np.float32)
    pcol[:, 0:4] = _fm_cols(inp["rwkv_w0"][l])
    pcol[:, 4:8] = _fm_cols(inp["rwkv_a0"][l])
    pcol[:, 8:12] = _fm_cols(inp["rwkv_k_k"][l])
    pcol[:, 12:16] = _fm_cols(inp["rwkv_k_a"][l])
    pcol[:, 16:18] = inp["gla_gate_bias"][l].reshape(2, 128).T
    pcol[:, 20:28] = inp["norm_w"][l].reshape(8, 128).T
    rk = inp["rwkv_r_k"][l].reshape(512)
    rksel = np.zeros((128, 4, 2), np.float32)
    for g in range(4):
        for hp in range(2):
            rksel[hp * 64:(hp + 1) * 64, g, hp] = rk[g * 128 + hp * 64:g * 128 + (hp + 1) * 64]
    lrw = np.concatenate([inp["rwkv_w_up"][l], inp["rwkv_a_up"][l]], 0)
    return {
        "w_in": np.ascontiguousarray(inp["w_in"][l]), "w_out": np.ascontiguousarray(inp["w_out"][l]),
        "pcol": pcol, "rksel": rksel.reshape(128, 8), "mu": np.ascontiguousarray(inp["rwkv_mu"][l]),
        "lnw": np.ascontiguousarray(inp["rwkv_ln_w"][l]), "lnb": np.ascontiguousarray(inp["rwkv_ln_b"][l]),
        "gnw": np.ascontiguousarray(inp["gla_norm_w"][l]), "fnw": np.ascontiguousarray(inp["final_norm_w"]),
        "lrw": np.ascontiguousarray(lrw), "gup": np.ascontiguousarray(inp["gla_gate_up"][l]), "cst": _consts(),
    }


def _zero_sums():
    return {"sumM": np.zeros((8, 128, 256), np.float32), "sumPhi": np.zeros((8, 128, 256), np.float32),
            "sumS": np.zeros((8, 128, 256), np.float32), "sumD": np.zeros((8, 128, 2), np.float32)}


def _sums_from_results(results):
    s = _zero_sums()
    for j, r in enumerate(results):
        A = r["oA"].reshape(128, 4, 128)
        s["sumM"][j] = A[:, :, 0:64].reshape(128, 256)
        Pc = A[:, :, 64:128].reshape(2, 64, 4, 64)
        s["sumPhi"][j] = Pc.transpose(0, 3, 2, 1).reshape(128, 256)
        s["sumS"][j] = r["oS"]
        s["sumD"][j] = r["oD"]
    return s


def _use_masks():
    u = np.zeros((NCORES, 128, 8), np.float32)
    for c in range(NCORES):
        b, q = divmod(c, 4)
        for j in range(NCORES):
            bj, qj = divmod(j, 4)
            if bj == b and qj < q:
                u[c, :, j] = 1.0
    return u


_NC_CACHE = {}


def _get_nc(NT, fused=False):
    if (NT, fused) not in _NC_CACHE:
        _NC_CACHE[(NT, fused)] = build_pass(NT, fused=fused)[0]
    return _NC_CACHE[(NT, fused)]


def _run_pass(xfull, lp, sums, use):
    in_maps = []
    for c in range(NCORES):
        b, q = divmod(c, 4)
        xs = np.zeros((TOK + 1, D), np.float32)
        xs[1:] = xfull[b, q * TOK:(q + 1) * TOK]
        if q > 0:
            xs[0] = xfull[b, q * TOK - 1]
        m = dict(lp)
        m.update(sums)
        m["x_sh"] = xs
        m["use"] = use[c]
        in_maps.append(m)
    res = run_bass_kernel_spmd(_get_nc(TOK // 128), in_maps, core_ids=list(range(NCORES)))
    return res.results


def _gather(results, key):
    out = np.zeros((2, 4 * TOK, D), np.float32)
    for c, r in enumerate(results):
        b, q = divmod(c, 4)
        out[b, q * TOK:(q + 1) * TOK] = r[key]
    return out


def kernel_unfused(**inputs):
    inp = {k: np.asarray(v, dtype=np.float32) for k, v in inputs.items()}
    use = _use_masks()
    x = inp["x"]
    res = None
    for l in range(2):
        lp = _layer_params(inp, l)
        r0 = _run_pass(x, lp, _zero_sums(), use)
        res = _run_pass(x, lp, _sums_from_results(r0), use)
        x = _gather(res, "x_out")
    return _gather(res, "y_out")


def _fused_params(inp):
    lps = [_layer_params(inp, l) for l in range(2)]
    m = {}
    for k in lps[0]:
        if k in ("fnw", "cst"):
            m[k] = lps[0][k]
        else:
            m[k] = np.ascontiguousarray(np.stack([lps[0][k], lps[1][k]], 0))
    return m


def kernel(**inputs):
    inp = {k: np.asarray(v, dtype=np.float32) for k, v in inputs.items()}
    T = inp["x"].shape[1]
    fp = _fused_params(inp)
    in_maps = []
    for c in range(NCORES):
        b, q = divmod(c, 4)
        xs = np.zeros((T + 1, D), np.float32)
        xs[1:] = inp["x"][b]
        m = dict(fp)
        m["x_sh"] = xs
        in_maps.append(m)
    res = run_bass_kernel_spmd(_get_nc(T // 128, True), in_maps, core_ids=list(range(NCORES)))
    out = np.zeros((2, T, D), np.float32)
    Q = T // 4
    for c, r in enumerate(res.results):
        b, q = divmod(c, 4)
        out[b, q * Q:(q + 1) * Q] = r["y_out"][q * Q:(q + 1) * Q]
    return out
```
